# Optimizing a Trainium2 kernel written in Bass

```python
import jax, jax.numpy as jnp
from jax import lax
import numpy as np

D_MODEL = 1024
BATCH = 2
SEQ = 8192
DEPTH = 1
DEC_BATCH = 16
DEC_SEQ = 64
PAST_LEN = 4096

CHUNK = 64
GMLP_CHUNK = 128
GMLP_WIDTH = 1024
GMLP_GROUPS = 8
GMLP_GROUP_DIM = GMLP_WIDTH // GMLP_GROUPS
SSM_INNER = 2 * D_MODEL
SSM_HEAD_DIM = 64
SSM_HEADS = SSM_INNER // SSM_HEAD_DIM
SSM_GROUPS = 4
SSM_HEADS_PER_GROUP = SSM_HEADS // SSM_GROUPS
SSM_STATE = 128
CONV_WIDTH = 4
CONV_DIM = SSM_INNER + 2 * SSM_GROUPS * SSM_STATE
FFN_HIDDEN = 4 * D_MODEL
IN_DIM = 2 * GMLP_WIDTH + SSM_INNER + CONV_DIM + SSM_HEADS + 2 * D_MODEL
EPS = 1e-6

kernel_name = "streaming_gmlp_ssd_hybrid_step"


def rms_norm(x, w):
    xf = x.astype(jnp.float32)
    y = xf * lax.rsqrt(jnp.mean(xf * xf, axis=-1, keepdims=True) + EPS)
    return (y * w.astype(jnp.float32)).astype(x.dtype)


def layer_norm(x, w, b):
    xf = x.astype(jnp.float32)
    mu = jnp.mean(xf, axis=-1, keepdims=True)
    xc = xf - mu
    var = jnp.mean(xc * xc, axis=-1, keepdims=True)
    return (xc * lax.rsqrt(var + EPS) * w.astype(jnp.float32) + b.astype(jnp.float32)).astype(x.dtype)


def gated_group_rms_norm(y, z, w):
    g = y.astype(jnp.float32) * jax.nn.silu(z.astype(jnp.float32))
    g = g.reshape(g.shape[:-1] + (SSM_GROUPS, SSM_INNER // SSM_GROUPS))
    g = g * lax.rsqrt(jnp.mean(g * g, axis=-1, keepdims=True) + EPS)
    return (g.reshape(y.shape) * w.astype(jnp.float32)).astype(y.dtype)


def causal_conv(xbc, hist, w, b):
    L = xbc.shape[1]
    xp = jnp.concatenate([hist.astype(xbc.dtype), xbc], axis=1)
    out = b
    for k in range(CONV_WIDTH):
        out = out + xp[:, k:k + L] * w[k]
    return jax.nn.silu(out), xp[:, -(CONV_WIDTH - 1):]


def gmlp_spatial(v, ws, bs):
    bsz, L, _ = v.shape
    n = min(L, GMLP_CHUNK)
    pos = jnp.arange(n)
    mask = (pos[None, :] // CHUNK) <= (pos[:, None] // CHUNK)
    w = jnp.where(mask[None], ws[:, :n, :n], 0.0).astype(v.dtype)
    vc = v.reshape(bsz, L // n, n, GMLP_GROUPS, GMLP_GROUP_DIM)
    out = jnp.einsum('gij,bcjgd->bcigd', w, vc) + bs[:, :n].T[None, None, :, :, None].astype(v.dtype)
    return out.reshape(bsz, L, GMLP_WIDTH)


def ssd(x, dt, a, bmat, cmat, h0, q):
    bsz, L = x.shape[:2]
    nc = L // q
    G, R, P, N = SSM_GROUPS, SSM_HEADS_PER_GROUP, SSM_HEAD_DIM, SSM_STATE
    x = x.reshape(bsz, nc, q, G, R, P)
    dt = dt.reshape(bsz, nc, q, G, R)
    bm = bmat.reshape(bsz, nc, q, G, N)
    cm = cmat.reshape(bsz, nc, q, G, N)
    acum = jnp.cumsum(dt * a.reshape(G, R), axis=2)
    idx = jnp.arange(q)
    causal = (idx[:, None] >= idx[None, :])[None, None, :, :, None, None]
    seg = acum[:, :, :, None] - acum[:, :, None, :]
    decay = jnp.exp(jnp.where(causal, seg, -jnp.inf))
    xdt = x * dt[..., None]
    cb = jnp.einsum('bcign,bcjgn->bcijg', cm, bm)
    y_diag = jnp.einsum('bcijgr,bcjgrp->bcigrp', cb[..., None] * decay, xdt)
    last = acum[:, :, -1]
    to_end = jnp.exp(last[:, :, None] - acum)
    chunk_states = jnp.einsum('bcjgn,bcjgrp->bcgrpn', bm, to_end[..., None] * xdt)

    def step(h, inp):
        dec, s = inp
        return h * jnp.exp(dec)[..., None, None] + s, h

    h_last, h_prev = lax.scan(step, h0.reshape(bsz, G, R, P, N),
                              (jnp.moveaxis(last, 1, 0), jnp.moveaxis(chunk_states, 1, 0)))
    h_prev = jnp.moveaxis(h_prev, 0, 1)
    y_off = jnp.einsum('bcign,bcgrpn->bcigrp', cm, h_prev) * jnp.exp(acum)[..., None]
    y = (y_diag + y_off).reshape(bsz, L, SSM_HEADS, P)
    return y, h_last.reshape(bsz, SSM_HEADS, P, N)


def hybrid_layer(x, conv_hist, h0, ssd_q, pre_mix_w, w_in, gmlp_ln_w, gmlp_ln_b, gmlp_ws, gmlp_bs,
                 conv_w, conv_b, dt_bias, a_log, d_skip, ssm_norm_w, w_branch_a, w_branch_b, w_out,
                 post_mix_w, pre_ffn_w, w_up, w_down, post_ffn_w):
    bsz, L, _ = x.shape
    h = rms_norm(x, pre_mix_w)
    proj = h @ w_in
    p1 = GMLP_WIDTH
    p2 = p1 + GMLP_WIDTH
    p3 = p2 + SSM_INNER
    p4 = p3 + CONV_DIM
    p5 = p4 + SSM_HEADS
    p6 = p5 + D_MODEL
    u, v, z, xbc, dt_raw, g_a, g_b = jnp.split(proj, [p1, p2, p3, p4, p5, p6], axis=-1)
    u = jax.nn.gelu(u)
    v = layer_norm(jax.nn.gelu(v), gmlp_ln_w, gmlp_ln_b)
    o_a = u * gmlp_spatial(v, gmlp_ws, gmlp_bs)
    xbc, conv_state = causal_conv(xbc, conv_hist, conv_w, conv_b)
    xs, bm, cm = jnp.split(xbc, [SSM_INNER, SSM_INNER + SSM_GROUPS * SSM_STATE], axis=-1)
    dt = jax.nn.softplus(dt_raw.astype(jnp.float32) + dt_bias.astype(jnp.float32))
    a = -jnp.exp(a_log.astype(jnp.float32))
    xs4 = xs.reshape(bsz, L, SSM_HEADS, SSM_HEAD_DIM).astype(jnp.float32)
    y, h_last = ssd(xs4, dt, a,
                    bm.reshape(bsz, L, SSM_GROUPS, SSM_STATE).astype(jnp.float32),
                    cm.reshape(bsz, L, SSM_GROUPS, SSM_STATE).astype(jnp.float32),
                    h0.astype(jnp.float32), ssd_q)
    y = (y + d_skip.astype(jnp.float32)[:, None] * xs4).reshape(bsz, L, SSM_INNER).astype(x.dtype)
    o_b = gated_group_rms_norm(y, z, ssm_norm_w)
    merged = jax.nn.sigmoid(g_a) * (o_a @ w_branch_a) + jax.nn.sigmoid(g_b) * (o_b @ w_branch_b)
    x = x + rms_norm(merged @ w_out, post_mix_w)
    f = jnp.square(jax.nn.relu(rms_norm(x, pre_ffn_w) @ w_up)) @ w_down
    x = x + rms_norm(f, post_ffn_w)
    return x, conv_state, h_last.astype(h0.dtype), v


def setup_inputs(seed: int = 0) -> dict:
    key = jax.random.key(seed)
    ks = jax.random.split(key, 32)
    nrm = lambda k, shape, s: jax.random.normal(k, shape, jnp.float32) * s
    dt0 = jnp.exp(jax.random.uniform(ks[10], (DEPTH, SSM_HEADS), jnp.float32)
                  * (np.log(0.1) - np.log(0.001)) + np.log(0.001))
    return {
        "x_prompt": nrm(ks[0], (BATCH, SEQ, D_MODEL), 1.0),
        "x_sample": nrm(ks[1], (DEC_BATCH, DEC_SEQ, D_MODEL), 1.0),
        "cache_conv": nrm(ks[2], (DEPTH, DEC_BATCH, CONV_WIDTH - 1, CONV_DIM), 1.0),
        "state_ssm": nrm(ks[3], (DEPTH, DEC_BATCH, SSM_HEADS, SSM_HEAD_DIM, SSM_STATE), 0.1),
        "pre_mix_w": 1.0 + nrm(ks[4], (DEPTH, D_MODEL), 0.02),
        "w_in": nrm(ks[5], (DEPTH, D_MODEL, IN_DIM), D_MODEL ** -0.5),
        "gmlp_ln_w": 1.0 + nrm(ks[6], (DEPTH, GMLP_WIDTH), 0.02),
        "gmlp_ln_b": nrm(ks[7], (DEPTH, GMLP_WIDTH), 0.02),
        "gmlp_ws": nrm(ks[8], (DEPTH, GMLP_GROUPS, GMLP_CHUNK, GMLP_CHUNK), GMLP_CHUNK ** -0.5),
        "gmlp_bs": 1.0 + nrm(ks[9], (DEPTH, GMLP_GROUPS, GMLP_CHUNK), 0.02),
        "conv_w": nrm(ks[11], (DEPTH, CONV_WIDTH, CONV_DIM), CONV_WIDTH ** -0.5),
        "conv_b": nrm(ks[12], (DEPTH, CONV_DIM), 0.01),
        "dt_bias": dt0 + jnp.log(-jnp.expm1(-dt0)),
        "a_log": jnp.log(jax.random.uniform(ks[13], (DEPTH, SSM_HEADS), jnp.float32, 1.0, 16.0)),
        "d_skip": 1.0 + nrm(ks[14], (DEPTH, SSM_HEADS), 0.1),
        "ssm_norm_w": 1.0 + nrm(ks[15], (DEPTH, SSM_INNER), 0.02),
        "w_branch_a": nrm(ks[16], (DEPTH, GMLP_WIDTH, D_MODEL), GMLP_WIDTH ** -0.5),
        "w_branch_b": nrm(ks[17], (DEPTH, SSM_INNER, D_MODEL), SSM_INNER ** -0.5),
        "w_out": nrm(ks[18], (DEPTH, D_MODEL, D_MODEL), D_MODEL ** -0.5),
        "post_mix_w": 1.0 + nrm(ks[19], (DEPTH, D_MODEL), 0.02),
        "pre_ffn_w": 1.0 + nrm(ks[20], (DEPTH, D_MODEL), 0.02),
        "w_up": nrm(ks[21], (DEPTH, D_MODEL, FFN_HIDDEN), D_MODEL ** -0.5),
        "w_down": nrm(ks[22], (DEPTH, FFN_HIDDEN, D_MODEL), FFN_HIDDEN ** -0.5),
        "post_ffn_w": 1.0 + nrm(ks[23], (DEPTH, D_MODEL), 0.02),
    }


def reference(x_prompt, x_sample, cache_conv, state_ssm, pre_mix_w, w_in, gmlp_ln_w, gmlp_ln_b,
              gmlp_ws, gmlp_bs, conv_w, conv_b, dt_bias, a_log, d_skip, ssm_norm_w, w_branch_a,
              w_branch_b, w_out, post_mix_w, pre_ffn_w, w_up, w_down, post_ffn_w):
    yp, ys = x_prompt, x_sample
    conv_hist0 = jnp.zeros((x_prompt.shape[0], CONV_WIDTH - 1, CONV_DIM), x_prompt.dtype)
    h00 = jnp.zeros((x_prompt.shape[0], SSM_HEADS, SSM_HEAD_DIM, SSM_STATE), state_ssm.dtype)
    conv_p, ssm_p, conv_s, ssm_s, v_s = [], [], [], [], []
    for l in range(DEPTH):
        lw = (pre_mix_w[l], w_in[l], gmlp_ln_w[l], gmlp_ln_b[l], gmlp_ws[l], gmlp_bs[l], conv_w[l],
              conv_b[l], dt_bias[l], a_log[l], d_skip[l], ssm_norm_w[l], w_branch_a[l], w_branch_b[l],
              w_out[l], post_mix_w[l], pre_ffn_w[l], w_up[l], w_down[l], post_ffn_w[l])
        yp, cp, sp, _ = hybrid_layer(yp, conv_hist0, h00, CHUNK, *lw)
        ys, cs, ss, vs = hybrid_layer(ys, cache_conv[l], state_ssm[l], x_sample.shape[1], *lw)
        conv_p.append(cp)
        ssm_p.append(sp)
        conv_s.append(cs)
        ssm_s.append(ss)
        v_s.append(vs)
    return (yp, ys, jnp.stack(conv_p), jnp.stack(ssm_p), jnp.stack(conv_s), jnp.stack(ssm_s), jnp.stack(v_s))
```

```python
import numpy as np
from contextlib import ExitStack
import concourse.bass as bass
import concourse.mybir as mybir
from concourse.bass_utils import run_bass_kernel_spmd

F32 = mybir.dt.float32
BF16 = mybir.dt.bfloat16
AF = mybir.ActivationFunctionType
ALU = mybir.AluOpType

NCORE = 8
D = 1024
PT = 2048
NTP = 2
NSUP = PT // (128 * NTP)
EPS = 1e-6
NSLOT = 3
C_U, C_V, C_Z, C_XBC, C_DT, C_GA, C_GB = 0, 1024, 2048, 4096, 7168, 7200, 8224
(K_ID, K_MC_P, K_MC_S, K_LM_P, K_LM_S, K_ONES, K_SS0, K_SS1, K_GM_P, K_GM_S, K_CM0, K_CM1,
 K_RM) = range(13)
NCST = 13


def _mk(name, *a, **kw):
    return lambda e: getattr(e, name)(*a, **kw)


class Sched:
    def __init__(self):
        self.ops = []

    def op(self, eng, fn, r=(), w=(), dma=None):
        o = dict(eng=eng, fn=fn, r=tuple(r), w=tuple(w), dma=dma)
        self.ops.append(o)
        return o

    def placeholder(self):
        o = dict(eng=None, fn=None, r=(), w=(), dma=None)
        self.ops.append(o)
        return o

    def emit(self, nc, es, final_eng="sp"):
        ops = [o for o in self.ops if o["eng"] is not None]
        lastw, readers = {}, {}
        for i, o in enumerate(ops):
            o["id"] = i
            deps = {}
            for k in o["r"]:
                p = lastw.get(k)
                if p is not None:
                    deps[p] = "raw"
            for k in o["w"]:
                p = lastw.get(k)
                if p is not None:
                    deps.setdefault(p, "waw")
                for q in readers.get(k, ()):
                    deps.setdefault(q, "war")
            keep = {}
            for p, t in deps.items():
                po = ops[p]
                if po["dma"] is None and o["dma"] is None and po["eng"] == o["eng"]:
                    if o["eng"] == "pe" or t != "raw":
                        continue
                keep[p] = t
            o["deps"] = keep
            for k in o["r"]:
                readers.setdefault(k, set()).add(i)
            for k in o["w"]:
                lastw[k] = i
                readers[k] = set()
        need = set()
        for o in ops:
            need.update(o["deps"].keys())
        cnt = {}
        for o in ops:
            if o["dma"] is not None:
                s = "d_" + o["dma"]
                cnt[s] = cnt.get(s, 0) + 16
                o["sig"] = (s, cnt[s], 16)
            elif o["id"] in need:
                s = "e_" + o["eng"]
                cnt[s] = cnt.get(s, 0) + 1
                o["sig"] = (s, cnt[s], 1)
            else:
                o["sig"] = None
        sems = {name: es.enter_context(nc.semaphore(name)) for name in sorted(cnt)}
        block = es.enter_context(nc.Block())

        def runner(engname):
            def f(e):
                waited = {}
                for o in ops:
                    if o["eng"] != engname:
                        continue
                    req = {}
                    for p in o["deps"]:
                        s, v, _ = ops[p]["sig"]
                        if v > req.get(s, 0):
                            req[s] = v
                    for s, v in req.items():
                        if waited.get(s, 0) < v:
                            e.wait_ge(sems[s], v)
                            waited[s] = v
                    ins = o["fn"](e)
                    if o["sig"] is not None:
                        ins.then_inc(sems[o["sig"][0]], o["sig"][2])
                if engname == final_eng:
                    for s, v in cnt.items():
                        if s.startswith("d_") and waited.get(s, 0) < v:
                            e.wait_ge(sems[s], v)
            return f

        block.tensor(runner("pe"))
        block.scalar(runner("act"))
        block.vector(runner("dve"))
        block.gpsimd(runner("pool"))
        block.sync(runner("sp"))
        return len(ops)


def build_program(nsup=NSUP, debug=False, npre=3 * NSUP):
    nc = bass.Bass("TRN2", target_bir_lowering=False)
    S = Sched()
    es = ExitStack()
    dr = {}

    def din(name, shape):
        dr[name] = nc.dram_tensor(name, list(shape), F32, kind="ExternalInput").ap()

    def dout(name, shape):
        dr[name] = nc.dram_tensor(name, list(shape), F32, kind="ExternalOutput").ap()

    din("xp", [PT, D]); din("xh", [4, D]); din("xsm", [128, D])
    din("cachel", [128, 2 * 24 * 3]); din("stateT", [2, 128, 2048])
    din("w_in", [D, 9248]); din("wa", [D, D]); din("wb", [2048, D]); din("wo", [D, D])
    din("wup", [D, 4096]); din("wdn", [4096, D])
    for nm in ("pre_mix_w", "ln_w", "ln_b", "post_mix_w", "pre_ffn_w", "post_ffn_w", "bs_p", "bs_s"):
        din(nm, [1, D])
    din("nw_l", [128, 16])
    for nm in ("a_log", "dt_bias", "d_skip"):
        din(nm, [1, 32])
    din("convw_l", [128, 96]); din("convb_l", [128, 24])
    din("xprev", [3 * PT, D]); din("pflag", [1, 3 * NSUP])
    din("cst", [128, NCST * 128]); din("wsT_p", [128, 1024]); din("wsT_s", [128, 1024])
    dout("yp", [PT, D]); dout("ys", [128, D]); dout("convp", [128, 72]); dout("ssmp", [128, 2048])
    dout("convs", [128, 144]); dout("ssms", [2, 128, 2048]); dout("vs", [128, D])

    def sb(name, shape, dt=F32):
        return es.enter_context(nc.sbuf_tensor("s_" + name, list(shape), dt))

    TTM = 128 * NTP
    dbg_names = []

    def dbg(name, ap, keys, shape):
        if not debug:
            return
        d = nc.dram_tensor("dbg_" + name, [int(v) for v in ap.shape], F32, kind="ExternalOutput").ap()
        dbg_names.append("dbg_" + name)
        S.op("pool", _mk("dma_start", out=d, in_=ap), r=keys, dma="dbg")
    cst = sb("cst", [128, NCST, 128]); cstb = sb("cstb", [128, NCST, 128], BF16)
    bc = {nm: sb("bc_" + nm, [128, D]) for nm in ("pre_mix_w", "ln_w", "ln_b", "post_mix_w", "pre_ffn_w", "post_ffn_w")}
    bsb1 = sb("bsb", [128, 8, 128]); bsb = {"p": bsb1, "s": bsb1}
    nwc = sb("nwc", [128, 16])
    alog = sb("alog", [128, 32]); abc = sb("abc", [128, 32]); dtb = sb("dtb", [128, 32]); dsk = sb("dsk", [128, 32])
    convw = sb("convw", [128, 24, 4]); convb = sb("convb", [128, 24])
    wsraw = sb("wsraw", [128, 8, 128])
    wsT = {"p": sb("wsT_p", [128, 8, 128], BF16), "s": sb("wsT_s", [128, 8, 128], BF16)}
    wdt = sb("wdt", [128, 8, 32], BF16)
    cachet = sb("cachet", [128, 2, 24, 3]); cso = sb("cso", [128, 2, 24, 3])
    histb = sb("histb", [128, 24, 3])
    pflagb = sb("pflagb", [128, 3 * NSUP])
    Sf = [sb("Sf0", [128, 2048]), sb("Sf1", [128, 2048])]
    Sb = [sb("Sb0", [128, 2048], BF16), sb("Sb1", [128, 2048], BF16)]
    slots = [sb("wslot%d" % i, [128, 4096], BF16) for i in range(NSLOT)]
    xres = sb("xres", [128, NTP, D])
    hT = sb("hT", [128, 8, TTM + 4], BF16)
    uT = sb("uT", [128, 8, TTM], BF16)
    vnb = sb("vnb", [128, NTP, D], BF16)
    oaT = sb("oaT", [128, 8, TTM], BF16)
    PA = sb("PA", [128, 8, TTM], BF16)
    xs_ = sb("xs", [128, NTP, 2048])
    Btok = sb("Btok", [128, NTP, 512], BF16)
    BT = sb("BT", [128, 4, TTM], BF16); CT = sb("CT", [128, 4, TTM], BF16)
    obT = sb("obT", [128, 16, TTM], BF16)
    mT = uT
    hidT = xs_[:].rearrange("p a b -> p (a b)").bitcast(BF16).rearrange("p (k c) -> p k c", k=32)
    hnb = sb("hnb", [128, D], BF16)
    vg = sb("vg", [128, D]); vn32 = sb("vn32", [128, D])
    junk = vn32
    stat = sb("stat", [128, 2, 6]); mv = sb("mv", [128, 2])
    sc = sb("sc", [128, 16])
    stage = sb("stage", [128, TTM + 8]); cacc = sb("cacc", [128, TTM])
    xcT = sb("xcT", [128, 4, TTM]); bcT = sb("bcT", [128, TTM])
    dts = sb("dts", [128, NTP, 8, 32])
    dhi = sb("dhi", [128, NTP, 32], BF16); dlo = sb("dlo", [128, NTP, 32], BF16)
    decS = sb("decS", [128, NTP, 64])
    xdt = sb("xdt", [128, 512], BF16); xw = sb("xw", [128, 512], BF16)
    Bm = sb("Bm", [128, 2, 512], BF16); CTm = sb("CTm", [128, 2, 4, 128], BF16)
    cbm = sb("cbm", [128, 128])
    Rhi = sb("Rhi", [128, 8, 128], BF16); Rlo = sb("Rlo", [128, 8, 128], BF16)
    dec = sb("dec", [128, 8, 128]); Mb = sb("Mb", [128, 8, 128], BF16)
    t1 = sb("t1", [128, 512]); zs = sb("zs", [128, 512]); gnb = sb("gnb", [128, 512], BF16)
    stmp = sb("stmp", [128, 512])
    sg = sb("sg", [128, TTM]); mtmp = sb("mtmp", [128, TTM])
    rl = sb("rl", [128, TTM]); fT = sb("fT", [128, TTM])
    yout = vg
    psb = [es.enter_context(nc.psum_tensor("ps%d" % i, [128, 512], F32)) for i in range(8)]
    pstate = {"n": 0}

    def bank():
        b = pstate["n"] % 8
        pstate["n"] += 1
        return b

    def PK(b):
        return ("ps", b)

    ws = {"n": 0, "ph": [S.placeholder() for _ in range(NSLOT)]}

    def wget(src3, kcn, cols):
        i = ws["n"]; ws["n"] += 1
        slot = i % NSLOT
        view = slots[slot][:, 0:kcn * cols].rearrange("p (k c) -> p k c", k=kcn)
        ph = ws["ph"].pop(0)
        ph.update(eng="pool", fn=(_mk("dma_start", out=view, in_=src3)),
                  r=(), w=(("wslot", slot),), dma="w%d" % slot)
        return view, ("wslot", slot)

    def wdone():
        ws["ph"].append(S.placeholder())

    def wsrc(name, r0, nr, c0, ncol):
        return dr[name][r0:r0 + nr, c0:c0 + ncol].rearrange("(k p) c -> p k c", p=128)

    S.op("sp", _mk("dma_start", out=cst[:].rearrange("p k c -> p (k c)"), in_=dr["cst"][:, :]), w=["cst"], dma="c11")
    S.op("pool", _mk("dma_start", out=cstb[:].rearrange("p k c -> p (k c)"), in_=dr["cst"][:, :]), w=["cstb"], dma="c12")
    for nm in bc:
        S.op("sp", _mk("dma_start", out=bc[nm][:], in_=dr[nm].partition_broadcast(128)), w=["bc_" + nm], dma="c13_" + nm)
    S.op("sp", _mk("dma_start", out=bsb["s"][:].rearrange("p g c -> p (g c)"), in_=dr["bs_s"].partition_broadcast(128)), w=["bsb"], dma="c14")
    S.op("sp", _mk("dma_start", out=nwc[:], in_=dr["nw_l"][:, :]), w=["nwc"], dma="c15")
    S.op("sp", _mk("dma_start", out=alog[:], in_=dr["a_log"].partition_broadcast(128)), w=["alog"], dma="c16")
    S.op("sp", _mk("dma_start", out=dtb[:], in_=dr["dt_bias"].partition_broadcast(128)), w=["dtb"], dma="c17")
    S.op("sp", _mk("dma_start", out=dsk[:], in_=dr["d_skip"].partition_broadcast(128)), w=["dsk"], dma="c18")
    S.op("sp", _mk("dma_start", out=convw[:].rearrange("p a b -> p (a b)"), in_=dr["convw_l"][:, :]), w=["convw"], dma="c19")
    S.op("sp", _mk("dma_start", out=convb[:], in_=dr["convb_l"][:, :]), w=["convb"], dma="c20")
    S.op("sp", _mk("dma_start", out=cachet[:].rearrange("p a b c -> p (a b c)"), in_=dr["cachel"][:, :]), w=["cachet"], dma="c21")
    S.op("pool", _mk("dma_start", out=wdt[:], in_=wsrc("w_in", 0, D, C_DT, 32)), w=["wdt"], dma="c22")
    S.op("act", _mk("activation", out=abc[:], in_=alog[:], func=AF.Exp), r=["alog"], w=["abc"])
    S.op("dve", _mk("tensor_scalar", out=abc[:], in0=abc[:], scalar1=-1.0, scalar2=None, op0=ALU.mult), r=["abc"], w=["abc"])
    for kind, gm in (("p", K_GM_P), ("s", K_GM_S)):
        S.op("sp", _mk("dma_start", out=wsraw[:].rearrange("p g c -> p (g c)"), in_=dr["wsT_" + kind][:, :]), w=["wsraw"], dma="c23")
        S.op("dve", _mk("tensor_tensor",
            out=wsT[kind][:], in0=wsraw[:], in1=cst[:, gm, :].unsqueeze(1).broadcast_to([128, 8, 128]), op=ALU.mult),
            r=["wsraw", "cst"], w=["wsT_" + kind])
    for s_ in range(2):
        S.op("sp", _mk("dma_start", out=Sf[s_][:], in_=dr["stateT"][s_, :, :]), w=[("Sf", s_)], dma="c24_%d" % s_)
        S.op("act", _mk("activation", out=Sb[s_][:], in_=Sf[s_][:], func=AF.Copy), r=[("Sf", s_)], w=[("Sb", s_)])

    def rstd_from(ss_ap, n, key):
        S.op("dve", _mk("tensor_scalar", out=ss_ap, in0=ss_ap, scalar1=1.0 / n, scalar2=EPS, op0=ALU.mult, op1=ALU.add), r=[key], w=[key])
        S.op("act", _mk("activation", out=ss_ap, in_=ss_ap, func=AF.Sqrt), r=[key], w=[key])
        S.op("dve", _mk("reciprocal", out=ss_ap, in_=ss_ap), r=[key], w=[key])

    def mm_group(out_ap, pairs, r, w):
        def fn(e):
            ins = None
            n = len(pairs)
            for i, (l, rr) in enumerate(pairs):
                ins = e.matmul(out_ap, lhsT=l, rhs=rr, start=(i == 0), stop=(i == n - 1))
            return ins
        S.op("pe", fn, r=r, w=w)

    def tr_group(items, r, w):
        def fn(e):
            ins = None
            for (o, i_, idn) in items:
                ins = e.transpose(o, i_, idn)
            return ins
        S.op("pe", fn, r=r, w=w)

    def norm_to_hT(src_ap, srckey, wname, t, ncols_valid=128):
        S.op("act", _mk("activation", out=junk[:], in_=src_ap, func=AF.Square, accum_out=sc[:, 0:1]), r=[srckey], w=["vn32", "sc0"])
        rstd_from(sc[:, 0:1], D, "sc0")
        S.op("dve", _mk("scalar_tensor_tensor", out=hnb[:], in0=src_ap, scalar=sc[:, 0:1], in1=bc[wname][:], op0=ALU.mult, op1=ALU.mult),
             r=[srckey, "sc0", "bc_" + wname], w=["hnb"])
        b = bank()
        pb = psb[b][:].bitcast(BF16)
        nv = ncols_valid
        tr_group([(pb[:, kc * 128:kc * 128 + nv], hnb[0:nv, kc * 128:(kc + 1) * 128], cstb[0:nv, K_ID, 0:nv]) for kc in range(8)],
                 r=["hnb", "cstb"], w=[PK(b)])
        c0 = t * 128
        S.op("act", _mk("activation", out=hT[:, :, c0:c0 + nv], in_=pb.rearrange("p (k c) -> p k c", k=8)[:, :, 0:nv], func=AF.Copy),
             r=[PK(b)], w=["hT"])

    def supertile(kind, NT, x_src, y_dst, first_prompt, pre=False, flagidx=None):
        TT = 128 * NT
        halo = 4 if first_prompt else 0
        NX = TT + halo
        if kind == "p":
            segs = [dict(sel=K_ONES, cm=None, rm=None, st=0)]
            mc, lm = K_MC_P, K_LM_P
            nsegc, L = 1, TT
        else:
            segs = [dict(sel=K_SS0, cm=K_CM0, rm=0, st=0), dict(sel=K_SS1, cm=K_CM1, rm=1, st=1)]
            mc, lm = K_MC_S, K_LM_S
            nsegc, L = 2, 64
        for t in range(NT):
            S.op("sp", _mk("dma_start", out=xres[:, t, :], in_=x_src[t * 128:(t + 1) * 128, :]), w=[("xres", t)], dma="x%d" % t)
            norm_to_hT(xres[:, t, :], ("xres", t), "pre_mix_w", t)
        if first_prompt:
            S.op("sp", _mk("dma_start", out=vg[0:4, :], in_=dr["xh"][:, :]), w=["vg"], dma="xh")
            norm_to_hT(vg[:, :], "vg", "pre_mix_w", NT, ncols_valid=4)
        for blk in (range(2) if not pre else ()):
            wv, wk = wget(wsrc("w_in", 0, D, C_U + blk * 512, 512), 8, 512)
            for sub in range(4):
                cb = blk * 4 + sub
                b = bank()
                mm_group(psb[b][:, 0:TT], [(wv[:, kc, sub * 128:(sub + 1) * 128], hT[:, kc, 0:TT]) for kc in range(8)],
                         r=[wk, "hT"], w=[PK(b)])
                S.op("act", _mk("activation", out=uT[:, cb, 0:TT], in_=psb[b][:, 0:TT], func=AF.Gelu_apprx_tanh),
                     r=[PK(b)], w=["uT"])
            wdone()
        vblk = [wget(wsrc("w_in", 0, D, C_V + blk * 512, 512), 8, 512) for blk in (range(2) if not pre else ())]
        for t in (range(NT) if not pre else ()):
            for blk in range(2):
                wv, wk = vblk[blk]
                b = bank()
                mm_group(psb[b][:, :], [(hT[:, kc, t * 128:(t + 1) * 128], wv[:, kc, :]) for kc in range(8)], r=[wk, "hT"], w=[PK(b)])
                S.op("act", _mk("activation", out=vg[:, blk * 512:(blk + 1) * 512], in_=psb[b][:, :], func=AF.Gelu_apprx_tanh),
                     r=[PK(b)], w=["vg"])
            for blk in range(2):
                S.op("dve", _mk("bn_stats", out=stat[:, blk, :], in_=vg[:, blk * 512:(blk + 1) * 512]), r=["vg"], w=["stat"])
            S.op("dve", _mk("bn_aggr", out=mv[:], in_=stat[:].rearrange("p a b -> p (a b)")), r=["stat"], w=["mv"])
            rstd_from(mv[:, 1:2], 1.0, "mv")
            S.op("dve", _mk("tensor_scalar", out=vn32[:], in0=vg[:], scalar1=mv[:, 0:1], scalar2=mv[:, 1:2], op0=ALU.subtract, op1=ALU.mult),
                 r=["vg", "mv"], w=["vn32"])
            S.op("dve", _mk("tensor_tensor", out=vn32[:], in0=vn32[:], in1=bc["ln_w"][:], op=ALU.mult), r=["vn32", "bc_ln_w"], w=["vn32"])
            S.op("dve", _mk("tensor_tensor", out=vn32[:], in0=vn32[:], in1=bc["ln_b"][:], op=ALU.add), r=["vn32", "bc_ln_b"], w=["vn32"])
            S.op("act", _mk("activation", out=vnb[:, t, :], in_=vn32[:], func=AF.Copy), r=["vn32"], w=[("vnb", t)])
            if kind == "s":
                S.op("sp", _mk("dma_start", out=dr["vs"][:, :], in_=vn32[:]), r=["vn32"], dma="o")
            for half in range(2):
                b = bank()
                def fn(e, b=b, half=half, t=t):
                    ins = None
                    for gi in range(4):
                        g = half * 4 + gi
                        ins = e.matmul(psb[b][:, gi * 128:(gi + 1) * 128], lhsT=vnb[:, t, g * 128:(g + 1) * 128], rhs=wsT[kind][:, g, :], start=True, stop=True)
                    return ins
                S.op("pe", fn, r=[("vnb", t), "wsT_" + kind], w=[PK(b)])
                S.op("dve", _mk("tensor_tensor",
                    out=stmp[:].rearrange("p (g c) -> p g c", g=4), in0=psb[b][:, :].rearrange("p (g c) -> p g c", g=4),
                    in1=bsb[kind][:, half * 4:half * 4 + 4, :], op=ALU.add), r=[PK(b), "bsb"], w=["stmp"])
                S.op("dve", _mk("tensor_tensor",
                    out=oaT[:, half * 4:half * 4 + 4, t * 128:(t + 1) * 128], in0=stmp[:].rearrange("p (g c) -> p g c", g=4),
                    in1=uT[:, half * 4:half * 4 + 4, t * 128:(t + 1) * 128], op=ALU.mult), r=["stmp", "uT"], w=["oaT"])
        if not pre:
            wdone(); wdone()
        for blk in (range(2) if not pre else ()):
            wa_v, wa_k = wget(wsrc("wa", 0, D, blk * 512, 512), 8, 512)
            ga_v, ga_k = wget(wsrc("w_in", 0, D, C_GA + blk * 512, 512), 8, 512)
            for sub in range(4):
                fb = blk * 4 + sub
                b1 = bank(); b2 = bank()
                mm_group(psb[b1][:, 0:TT], [(ga_v[:, kc, sub * 128:(sub + 1) * 128], hT[:, kc, 0:TT]) for kc in range(8)], r=[ga_k, "hT"], w=[PK(b1)])
                mm_group(psb[b2][:, 0:TT], [(wa_v[:, kc, sub * 128:(sub + 1) * 128], oaT[:, kc, 0:TT]) for kc in range(8)], r=[wa_k, "oaT"], w=[PK(b2)])
                S.op("act", _mk("activation", out=sg[:, 0:TT], in_=psb[b1][:, 0:TT], func=AF.Sigmoid), r=[PK(b1)], w=["sg"])
                S.op("dve", _mk("tensor_tensor", out=PA[:, fb, 0:TT], in0=sg[:, 0:TT], in1=psb[b2][:, 0:TT], op=ALU.mult),
                     r=["sg", PK(b2)], w=["PA"])
            wdone(); wdone()
        for t in range(NT):
            b = bank()
            mm_group(psb[b][:, 0:32], [(hT[:, kc, t * 128:(t + 1) * 128], wdt[:, kc, :]) for kc in range(8)], r=["hT", "wdt"], w=[PK(b)])
            dk = ("dts", t)
            A = lambda i, t=t: dts[:, t, i, :]
            S.op("dve", _mk("tensor_tensor", out=dts[:, t, 6, :], in0=psb[b][:, 0:32], in1=dtb[:], op=ALU.add), r=[PK(b), "dtb"], w=[dk])
            S.op("act", _mk("activation", out=dts[:, t, 7, :], in_=dts[:, t, 6, :], func=AF.Abs), r=[dk], w=[dk])
            S.op("act", _mk("activation", out=dts[:, t, 7, :], in_=dts[:, t, 7, :], func=AF.Exp, scale=-1.0), r=[dk], w=[dk])
            S.op("act", _mk("activation", out=dts[:, t, 7, :], in_=dts[:, t, 7, :], func=AF.Ln, bias=1.0), r=[dk], w=[dk])
            S.op("dve", _mk("scalar_tensor_tensor", out=dts[:, t, 0, :], in0=dts[:, t, 6, :], scalar=0.0, in1=dts[:, t, 7, :], op0=ALU.max, op1=ALU.add), r=[dk], w=[dk])
            S.op("dve", _mk("tensor_tensor", out=dts[:, t, 1, :], in0=dts[:, t, 0, :], in1=abc[:], op=ALU.mult), r=[dk, "abc"], w=[dk])
            S.op("dve", _mk("tensor_copy", out=dhi[:, t, :], in_=dts[:, t, 1, :]), r=[dk], w=[("dhi", t)])
            S.op("dve", _mk("tensor_tensor", out=dlo[:, t, :], in0=dts[:, t, 1, :], in1=dhi[:, t, :], op=ALU.subtract), r=[dk, ("dhi", t)], w=[("dlo", t)])
            b2 = bank()
            def fn(e, b2=b2, t=t):
                ins = e.matmul(psb[b2][:, 0:32], lhsT=cst[:, mc, :], rhs=dts[:, t, 1, :], start=True, stop=True)
                for si, sgm in enumerate(segs):
                    ins = e.matmul(psb[b2][:, 32 + 32 * si:64 + 32 * si], lhsT=cst[:, sgm["sel"], :], rhs=dts[:, t, 1, :], start=True, stop=True)
                return ins
            S.op("pe", fn, r=[dk, "cst"], w=[PK(b2)])
            S.op("act", _mk("activation", out=dts[:, t, 2, :], in_=psb[b2][:, 0:32], func=AF.Copy), r=[PK(b2)], w=[dk])
            S.op("act", _mk("activation", out=dts[:, t, 3, :], in_=psb[b2][:, 0:32], func=AF.Exp), r=[PK(b2)], w=[dk])
            S.op("act", _mk("activation", out=decS[:, t, 0:32 * len(segs)], in_=psb[b2][:, 32:32 + 32 * len(segs)], func=AF.Exp), r=[PK(b2)], w=[("decS", t)])
            if kind == "p":
                S.op("dve", _mk("tensor_tensor", out=dts[:, t, 4, :], in0=psb[b2][:, 32:64], in1=dts[:, t, 2, :], op=ALU.subtract), r=[PK(b2), dk], w=[dk])
            else:
                for si in range(2):
                    S.op("dve", _mk("tensor_tensor",
                        out=dts[64 * si:64 * si + 64, t, 4, :], in0=psb[b2][64 * si:64 * si + 64, 32 + 32 * si:64 + 32 * si],
                        in1=dts[64 * si:64 * si + 64, t, 2, :], op=ALU.subtract), r=[PK(b2), dk], w=[dk])
            S.op("act", _mk("activation", out=dts[:, t, 4, :], in_=dts[:, t, 4, :], func=AF.Exp), r=[dk], w=[dk])
            S.op("dve", _mk("tensor_tensor", out=dts[:, t, 5, :], in0=dts[:, t, 4, :], in1=dts[:, t, 0, :], op=ALU.mult), r=[dk], w=[dk])
            if pre:
                S.op("dve", _mk("tensor_scalar", out=dts[:, t, 5, :], in0=dts[:, t, 5, :], scalar1=pflagb[:, flagidx:flagidx + 1], scalar2=None, op0=ALU.mult),
                     r=[dk, "pflagb"], w=[dk])

        if kind == "s":
            dbg("dts", dts[:, 0, :, :].rearrange("p a b -> p (a b)"), [("dts", 0)], [128, 256])
            dbg("decS", decS[:, 0, :], [("decS", 0)], [128, 64])
            dbg("hT", hT[:, :, 0:128], ["hT"], [128, 1024])
        def conv_tile(ct, wv, wk, sub, out_ap, out_key):
            b = bank()
            mm_group(psb[b][:, 0:NX], [(wv[:, kc, sub * 128:(sub + 1) * 128], hT[:, kc, 0:NX]) for kc in range(8)], r=[wk, "hT"], w=[PK(b)])
            st3 = stage[:, 0:nsegc * (L + 3)].rearrange("p (s c) -> p s c", s=nsegc)
            if first_prompt:
                S.op("act", _mk("activation", out=stage[:, 0:3], in_=psb[b][:, TT:TT + 3], func=AF.Copy), r=[PK(b)], w=["stage"])
            elif kind == "p":
                S.op("pool", _mk("tensor_copy", out=stage[:, 0:3], in_=histb[:, ct, :]), r=[("histb", ct)], w=["stage"])
            else:
                S.op("pool", _mk("tensor_copy", out=st3[:, :, 0:3], in_=cachet[:, :, ct, :]), r=["cachet"], w=["stage"])
            S.op("act", _mk("activation", out=st3[:, :, 3:3 + L], in_=psb[b][:, 0:TT].rearrange("p (s c) -> p s c", s=nsegc), func=AF.Copy),
                 r=[PK(b)], w=["stage"])
            if kind == "p":
                S.op("pool", _mk("tensor_copy", out=histb[:, ct, :], in_=stage[:, L:L + 3]), r=["stage"], w=[("histb", ct)])
            else:
                S.op("pool", _mk("tensor_copy", out=cso[:, :, ct, :], in_=st3[:, :, L:L + 3]), r=["stage"], w=["cso"])
            ca3 = cacc[:, 0:TT].rearrange("p (s c) -> p s c", s=nsegc)
            S.op("dve", _mk("tensor_scalar", out=ca3, in0=st3[:, :, 0:L], scalar1=convw[:, ct, 0:1], scalar2=convb[:, ct:ct + 1], op0=ALU.mult, op1=ALU.add),
                 r=["stage", "convw", "convb"], w=["cacc"])
            for k in range(1, 4):
                S.op("dve", _mk("scalar_tensor_tensor", out=ca3, in0=st3[:, :, k:k + L], scalar=convw[:, ct, k:k + 1], in1=ca3, op0=ALU.mult, op1=ALU.add),
                     r=["stage", "cacc", "convw"], w=["cacc"])
            S.op("act", _mk("activation", out=out_ap, in_=cacc[:, 0:TT], func=AF.Silu), r=["cacc"], w=[out_key])

        wv, wk = wget(wsrc("w_in", 0, D, C_XBC + 2048, 512), 8, 512)
        for g in range(4):
            conv_tile(16 + g, wv, wk, g, bcT[:, 0:TT], "bcT")
            S.op("act", _mk("activation", out=BT[:, g, 0:TT], in_=bcT[:, 0:TT], func=AF.Copy), r=["bcT"], w=["BT"])
        wdone()
        for t in range(NT):
            b = bank()
            pb = psb[b][:].bitcast(BF16)
            tr_group([(pb[:, g * 128:(g + 1) * 128], BT[:, g, t * 128:(t + 1) * 128], cstb[:, K_ID, :]) for g in range(4)], r=["BT", "cstb"], w=[PK(b)])
            S.op("act", _mk("activation", out=Btok[:, t, :], in_=pb[:, 0:512], func=AF.Copy), r=[PK(b)], w=[("Btok", t)])
        if not pre:
            wv, wk = wget(wsrc("w_in", 0, D, C_XBC + 2560, 512), 8, 512)
            for g in range(4):
                conv_tile(20 + g, wv, wk, g, bcT[:, 0:TT], "bcT")
                S.op("act", _mk("activation", out=CT[:, g, 0:TT], in_=bcT[:, 0:TT], func=AF.Copy), r=["bcT"], w=["CT"])
            wdone()
        for g in range(4):
            wv, wk = wget(wsrc("w_in", 0, D, C_XBC + g * 512, 512), 8, 512)
            for sub in range(4):
                conv_tile(g * 4 + sub, wv, wk, sub, xcT[:, sub, 0:TT], "xcT")
            wdone()
            for t in range(NT):
                b = bank()
                tr_group([(psb[b][:, sub * 128:(sub + 1) * 128], xcT[:, sub, t * 128:(t + 1) * 128], cst[:, K_ID, :]) for sub in range(4)],
                         r=["xcT", "cst"], w=[PK(b)])
                S.op("act", _mk("activation", out=xs_[:, t, g * 512:(g + 1) * 512], in_=psb[b][:, :], func=AF.Copy), r=[PK(b)], w=[("xs", t), "hidT"])
            if not pre:
                zv, zk = wget(wsrc("w_in", 0, D, C_Z + g * 512, 512), 8, 512)
            for t in range(NT):
                dk = ("dts", t)
                tc_ = slice(t * 128, (t + 1) * 128)
                gc = slice(g * 512, (g + 1) * 512)
                xs3 = xs_[:, t, g * 512:(g + 1) * 512].rearrange("p (h c) -> p h c", h=8)
                if not pre:
                  S.op("dve", _mk("tensor_tensor", out=xdt[:].rearrange("p (h c) -> p h c", h=8), in0=xs3,
                     in1=dts[:, t, 0, g * 8:(g + 1) * 8].unsqueeze(2).broadcast_to([128, 8, 64]), op=ALU.mult), r=[("xs", t), dk], w=["xdt"])
                S.op("dve", _mk("tensor_tensor", out=xw[:].rearrange("p (h c) -> p h c", h=8), in0=xs3,
                     in1=dts[:, t, 5, g * 8:(g + 1) * 8].unsqueeze(2).broadcast_to([128, 8, 64]), op=ALU.mult), r=[("xs", t), dk], w=["xw"])
                if first_prompt and not pre and g == 0 and t == 0:
                    dbg("pxs0", xs_[:, 0, 0:512], [("xs", 0)], None)
                    dbg("pBT", BT[:, 0, 0:128], ["BT"], None); dbg("pCT", CT[:, 0, 0:128], ["CT"], None)
                    dbg("pSb", Sb[0][:, 0:512], [("Sb", 0)], None)
                    dbg("phT", hT[:, :, 252:260], ["hT"], None)
                if kind == "s" and g == 0:
                    dbg("xs0", xs_[:, 0, 0:512], [("xs", 0)], [128, 512])
                    dbg("xdt", xdt[:], ["xdt"], [128, 512])
                    dbg("xw", xw[:], ["xw"], [128, 512])
                    dbg("BT", BT[:, 0, 0:128], ["BT"], [128, 128])
                    dbg("CT", CT[:, 0, 0:128], ["CT"], [128, 128])
                    dbg("Btok", Btok[:, 0, 0:128], [("Btok", 0)], [128, 128])
                if not pre:
                    bcb = bank()
                    mm_group(psb[bcb][:, 0:128], [(BT[:, g, tc_], CT[:, g, tc_])], r=["BT", "CT"], w=[PK(bcb)])
                    S.op("dve", _mk("tensor_tensor", out=cbm[:], in0=psb[bcb][:, 0:128], in1=cst[:, mc, :], op=ALU.mult), r=[PK(bcb), "cst"], w=["cbm"])
                    S.op("pool", _mk("tensor_tensor", out=Rhi[:], in0=cstb[:, mc, :].unsqueeze(1).broadcast_to([128, 8, 128]),
                         in1=dhi[:, t, g * 8:(g + 1) * 8].unsqueeze(2).broadcast_to([128, 8, 128]), op=ALU.mult), r=["cstb", ("dhi", t)], w=["Rhi"])
                    S.op("pool", _mk("tensor_tensor", out=Rlo[:], in0=cstb[:, mc, :].unsqueeze(1).broadcast_to([128, 8, 128]),
                         in1=dlo[:, t, g * 8:(g + 1) * 8].unsqueeze(2).broadcast_to([128, 8, 128]), op=ALU.mult), r=["cstb", ("dlo", t)], w=["Rlo"])
                    bs0 = bank(); bs1 = bank()
                    for hh, bb in ((0, bs0), (1, bs1)):
                        mm_group(psb[bb][:, :], [(cstb[:, lm, :], Rhi[:, hh * 4:hh * 4 + 4, :].rearrange("p h c -> p (h c)")),
                                                (cstb[:, lm, :], Rlo[:, hh * 4:hh * 4 + 4, :].rearrange("p h c -> p (h c)"))],
                                 r=["cstb", "Rhi", "Rlo"], w=[PK(bb)])
                        S.op("act", _mk("activation", out=dec[:, hh * 4:hh * 4 + 4, :].rearrange("p h c -> p (h c)"), in_=psb[bb][:, :], func=AF.Exp),
                             r=[PK(bb)], w=["dec"])
                    S.op("dve", _mk("tensor_tensor", out=Mb[:], in0=dec[:], in1=cbm[:].unsqueeze(1).broadcast_to([128, 8, 128]), op=ALU.mult),
                         r=["dec", "cbm"], w=["Mb"])
                    if kind == "s" and g == 0:
                        dbg("cbm", cbm[:], ["cbm"], [128, 128])
                        dbg("dec", dec[:].rearrange("p a b -> p (a b)"), ["dec"], [128, 1024])
                        dbg("Mb", Mb[:].rearrange("p a b -> p (a b)"), ["Mb"], [128, 1024])
                    byd = bank()
                    def fn(e, byd=byd):
                        ins = None
                        for h in range(8):
                            ins = e.matmul(psb[byd][:, h * 64:(h + 1) * 64], lhsT=Mb[:, h, :], rhs=xdt[:, h * 64:(h + 1) * 64], start=True, stop=True)
                        return ins
                    S.op("pe", fn, r=["Mb", "xdt"], w=[PK(byd)])
                    byo = bank()
                    gc = slice(g * 512, (g + 1) * 512)
                    if kind == "p":
                        mm_group(psb[byo][:, :], [(CT[:, g, tc_], Sb[0][:, gc])], r=["CT", ("Sb", 0)], w=[PK(byo)])
                    else:
                        for si in range(2):
                            S.op("dve", _mk("tensor_tensor", out=CTm[:, si, g, :], in0=CT[:, g, tc_], in1=cstb[:, K_CM0 + si, :], op=ALU.mult),
                                 r=["CT", "cstb"], w=["CTm"])
                        mm_group(psb[byo][:, :], [(CTm[:, 0, g, :], Sb[0][:, gc]), (CTm[:, 1, g, :], Sb[1][:, gc])], r=["CTm", ("Sb", 0), ("Sb", 1)], w=[PK(byo)])
                for si, sgm in enumerate(segs):
                    bst = bank()
                    sidx = sgm["st"]
                    if kind == "p":
                        lhs = Btok[:, t, g * 128:(g + 1) * 128]
                        rk = [("Btok", t)]
                    else:
                        S.op("dve", _mk("tensor_scalar", out=Bm[:, si, g * 128:(g + 1) * 128], in0=Btok[:, t, g * 128:(g + 1) * 128],
                             scalar1=cst[:, K_RM, si:si + 1], scalar2=None, op0=ALU.mult), r=[("Btok", t), "cst"], w=["Bm"])
                        lhs = Bm[:, si, g * 128:(g + 1) * 128]
                        rk = ["Bm"]
                    mm_group(psb[bst][:, :], [(lhs, xw[:, :])], r=rk + ["xw"], w=[PK(bst)])
                    S.op("dve", _mk("tensor_tensor",
                        out=Sf[sidx][:, gc].rearrange("p (h c) -> p h c", h=8), in0=Sf[sidx][:, gc].rearrange("p (h c) -> p h c", h=8),
                        in1=decS[:, t, 32 * si + g * 8:32 * si + g * 8 + 8].unsqueeze(2).broadcast_to([128, 8, 64]), op=ALU.mult),
                        r=[("Sf", sidx), ("decS", t)], w=[("Sf", sidx)])
                    S.op("dve", _mk("tensor_tensor", out=Sf[sidx][:, gc], in0=Sf[sidx][:, gc], in1=psb[bst][:, :], op=ALU.add),
                         r=[("Sf", sidx), PK(bst)], w=[("Sf", sidx)])
                    if not pre:
                        S.op("act", _mk("activation", out=Sb[sidx][:, gc], in_=Sf[sidx][:, gc], func=AF.Copy), r=[("Sf", sidx)], w=[("Sb", sidx)])
                if not pre:
                    S.op("dve", _mk("tensor_tensor", out=t1[:].rearrange("p (h c) -> p h c", h=8), in0=psb[byo][:, :].rearrange("p (h c) -> p h c", h=8),
                         in1=dts[:, t, 3, g * 8:(g + 1) * 8].unsqueeze(2).broadcast_to([128, 8, 64]), op=ALU.mult), r=[PK(byo), dk], w=["t1"])
                    S.op("dve", _mk("tensor_tensor", out=t1[:], in0=t1[:], in1=psb[byd][:, :], op=ALU.add), r=["t1", PK(byd)], w=["t1"])
                    S.op("dve", _mk("tensor_tensor", out=xs3, in0=xs3, in1=dsk[:, g * 8:(g + 1) * 8].unsqueeze(2).broadcast_to([128, 8, 64]), op=ALU.mult),
                         r=[("xs", t), "dsk"], w=[("xs", t)])
                    S.op("dve", _mk("tensor_tensor", out=t1[:], in0=t1[:], in1=xs_[:, t, gc], op=ALU.add), r=["t1", ("xs", t)], w=["t1"])
                    if first_prompt and g == 0 and t == 0:
                        dbg("py", t1[:], ["t1"], None)
                    if kind == "s" and g == 0:
                        dbg("y", t1[:], ["t1"], [128, 512])
                        dbg("Sf0", Sf[0][:, 0:512], [("Sf", 0)], [128, 512])
                    bz = bank()
                    mm_group(psb[bz][:, :], [(hT[:, kc, tc_], zv[:, kc, :]) for kc in range(8)], r=["hT", zk], w=[PK(bz)])
                    S.op("act", _mk("activation", out=zs[:], in_=psb[bz][:, :], func=AF.Silu), r=[PK(bz)], w=["zs"])
                    S.op("dve", _mk("tensor_tensor", out=t1[:], in0=t1[:], in1=zs[:], op=ALU.mult), r=["t1", "zs"], w=["t1"])
                    S.op("act", _mk("activation", out=zs[:], in_=t1[:], func=AF.Square, accum_out=sc[:, 1:2]), r=["t1"], w=["zs", "sc1"])
                    rstd_from(sc[:, 1:2], 512.0, "sc1")
                    S.op("dve", _mk("tensor_scalar", out=gnb[:], in0=t1[:], scalar1=sc[:, 1:2], scalar2=None, op0=ALU.mult),
                         r=["t1", "sc1"], w=["gnb"])
                    bt = bank()
                    pb = psb[bt][:].bitcast(BF16)
                    tr_group([(pb[:, sub * 128:(sub + 1) * 128], gnb[:, sub * 128:(sub + 1) * 128], cstb[:, K_ID, :]) for sub in range(4)], r=["gnb", "cstb"], w=[PK(bt)])
                    S.op("dve", _mk("tensor_tensor", out=obT[:, g * 4:g * 4 + 4, t * 128:(t + 1) * 128], in0=pb[:, 0:512].rearrange("p (s c) -> p s c", s=4),
                         in1=nwc[:, g * 4:g * 4 + 4].unsqueeze(2).broadcast_to([128, 4, 128]), op=ALU.mult), r=[PK(bt), "nwc"], w=["obT"])
            if not pre:
                wdone()
        if kind == "s":
            dbg("obT", obT[:, :, 0:128], ["obT"], [128, 2048])
            dbg("PA", PA[:, :, 0:128], ["PA"], [128, 1024])
        if not pre:
            gbs = []
            for q in range(4):
                wv, wk = wget(wsrc("wb", 0, 2048, q * 256, 256), 16, 256)
                if q % 2 == 0:
                    gv, gk = wget(wsrc("w_in", 0, D, C_GB + (q // 2) * 512, 512), 8, 512)
                for sub in range(2):
                    fb = q * 2 + sub
                    gsub = (q % 2) * 2 + sub
                    b1 = bank(); b2 = bank()
                    mm_group(psb[b1][:, 0:TT], [(gv[:, kc, gsub * 128:(gsub + 1) * 128], hT[:, kc, 0:TT]) for kc in range(8)], r=[gk, "hT"], w=[PK(b1)])
                    mm_group(psb[b2][:, 0:TT], [(wv[:, kc, sub * 128:(sub + 1) * 128], obT[:, kc, 0:TT]) for kc in range(16)], r=[wk, "obT"], w=[PK(b2)])
                    S.op("act", _mk("activation", out=sg[:, 0:TT], in_=psb[b1][:, 0:TT], func=AF.Sigmoid), r=[PK(b1)], w=["sg"])
                    S.op("dve", _mk("tensor_tensor", out=mtmp[:, 0:TT], in0=sg[:, 0:TT], in1=psb[b2][:, 0:TT], op=ALU.mult), r=["sg", PK(b2)], w=["mtmp"])
                    S.op("dve", _mk("tensor_tensor", out=mT[:, fb, 0:TT], in0=mtmp[:, 0:TT], in1=PA[:, fb, 0:TT], op=ALU.add), r=["mtmp", "PA"], w=["uT"])
                if q % 2 == 0:
                    wdone()
                    pending_gb = True
                else:
                    wdone(); wdone()
            if kind == "s":
                dbg("mT", mT[:, :, 0:128], ["uT"], [128, 1024])
            wo_blk = [wget(wsrc("wo", 0, D, blk * 512, 512), 8, 512) for blk in range(2)]
            for t in range(NT):
                bo = [bank(), bank()]
                for blk in range(2):
                    wv, wk = wo_blk[blk]
                    mm_group(psb[bo[blk]][:, :], [(mT[:, kc, t * 128:(t + 1) * 128], wv[:, kc, :]) for kc in range(8)], r=["uT", wk], w=[PK(bo[blk])])
                    S.op("act", _mk("activation", out=junk[:, blk * 512:(blk + 1) * 512], in_=psb[bo[blk]][:, :], func=AF.Square, accum_out=sc[:, 2 + blk:3 + blk]),
                         r=[PK(bo[blk])], w=["vn32", "sc2"])
                S.op("dve", _mk("tensor_tensor", out=sc[:, 2:3], in0=sc[:, 2:3], in1=sc[:, 3:4], op=ALU.add), r=["sc2"], w=["sc2"])
                rstd_from(sc[:, 2:3], D, "sc2")
                for blk in range(2):
                    cs = slice(blk * 512, (blk + 1) * 512)
                    S.op("dve", _mk("scalar_tensor_tensor", out=junk[:, cs], in0=psb[bo[blk]][:, :], scalar=sc[:, 2:3], in1=bc["post_mix_w"][:, cs], op0=ALU.mult, op1=ALU.mult),
                         r=[PK(bo[blk]), "sc2", "bc_post_mix_w"], w=["vn32"])
                S.op("dve", _mk("tensor_tensor", out=xres[:, t, :], in0=xres[:, t, :], in1=junk[:], op=ALU.add), r=[("xres", t), "vn32"], w=[("xres", t)])
                norm_to_hT(xres[:, t, :], ("xres", t), "pre_ffn_w", t)
            wdone(); wdone()
            if kind == "s":
                dbg("x1", xres[:, 0, :], [("xres", 0)], [128, 1024])
            for blk in range(8):
                wv, wk = wget(wsrc("wup", 0, D, blk * 512, 512), 8, 512)
                for sub in range(4):
                    hb = blk * 4 + sub
                    b = bank()
                    mm_group(psb[b][:, 0:TT], [(wv[:, kc, sub * 128:(sub + 1) * 128], hT[:, kc, 0:TT]) for kc in range(8)], r=[wk, "hT"], w=[PK(b)])
                    S.op("act", _mk("activation", out=rl[:, 0:TT], in_=psb[b][:, 0:TT], func=AF.Relu), r=[PK(b)], w=["rl"])
                    S.op("pool", _mk("tensor_tensor", out=hidT[:, hb, 0:TT], in0=rl[:, 0:TT], in1=rl[:, 0:TT], op=ALU.mult), r=["rl"], w=["hidT", ("xs", 0), ("xs", 1)])
                wdone()
            bdn = [[bank(), bank()] for _ in range(NT)]
            for fb in range(8):
                wv, wk = wget(wsrc("wdn", 0, 4096, fb * 128, 128), 32, 128)
                b = bank()
                while any(b in pr for pr in bdn):
                    b = bank()
                mm_group(psb[b][:, 0:TT], [(wv[:, kc, :], hidT[:, kc, 0:TT]) for kc in range(32)], r=[wk, "hidT"], w=[PK(b)])
                S.op("act", _mk("activation", out=fT[:, 0:TT], in_=psb[b][:, 0:TT], func=AF.Copy), r=[PK(b)], w=["fT"])
                for t in range(NT):
                    bb = bdn[t][fb // 4]
                    c0 = (fb % 4) * 128
                    tr_group([(psb[bb][:, c0:c0 + 128], fT[:, t * 128:(t + 1) * 128], cst[:, K_ID, :])], r=["fT", "cst"], w=[PK(bb)])
                wdone()
            for t in range(NT):
                for blk in range(2):
                    bb = bdn[t][blk]
                    S.op("act", _mk("activation", out=junk[:, blk * 512:(blk + 1) * 512], in_=psb[bb][:, :], func=AF.Square, accum_out=sc[:, 4 + blk:5 + blk]),
                         r=[PK(bb)], w=["vn32", "sc4"])
                S.op("dve", _mk("tensor_tensor", out=sc[:, 4:5], in0=sc[:, 4:5], in1=sc[:, 5:6], op=ALU.add), r=["sc4"], w=["sc4"])
                rstd_from(sc[:, 4:5], D, "sc4")
                for blk in range(2):
                    bb = bdn[t][blk]
                    cs = slice(blk * 512, (blk + 1) * 512)
                    S.op("dve", _mk("scalar_tensor_tensor", out=yout[:, cs], in0=psb[bb][:, :], scalar=sc[:, 4:5], in1=bc["post_ffn_w"][:, cs], op0=ALU.mult, op1=ALU.mult),
                         r=[PK(bb), "sc4", "bc_post_ffn_w"], w=["vg"])
                S.op("dve", _mk("tensor_tensor", out=yout[:], in0=yout[:], in1=xres[:, t, :], op=ALU.add), r=["vg", ("xres", t)], w=["vg"])
                S.op("sp", _mk("dma_start", out=y_dst[t * 128:(t + 1) * 128, :], in_=yout[:]), r=["vg"], dma="o")

    supertile("s", 1, dr["xsm"], dr["ys"], False)
    for s_ in range(2):
        S.op("sp", _mk("dma_start", out=dr["ssms"][s_, :, :], in_=Sf[s_][:]), r=[("Sf", s_)], dma="o")
    S.op("sp", _mk("dma_start", out=dr["convs"][:, :], in_=cso[:].rearrange("p a b c -> p (a b c)")), r=["cso"], dma="o")
    S.op("sp", _mk("dma_start", out=bsb1[:].rearrange("p g c -> p (g c)"), in_=dr["bs_p"].partition_broadcast(128)), w=["bsb"], dma="c25")
    S.op("dve", _mk("memset", Sf[0][:], 0.0), w=[("Sf", 0)])
    S.op("dve", _mk("memset", histb[:], 0.0), w=[("histb", ct) for ct in range(24)])
    S.op("sp", _mk("dma_start", out=pflagb[:], in_=dr["pflag"].partition_broadcast(128)), w=["pflagb"], dma="c26")
    for su in range(npre):
        r0 = su * 128 * NTP
        supertile("p", NTP, dr["xprev"][r0:r0 + 128 * NTP, :], None, False, pre=True, flagidx=su)
    S.op("act", _mk("activation", out=Sb[0][:], in_=Sf[0][:], func=AF.Copy), r=[("Sf", 0)], w=[("Sb", 0)])
    dbg("hinit", Sf[0][:], [("Sf", 0)], None)
    for su in range(nsup):
        r0 = su * 128 * NTP
        supertile("p", NTP, dr["xp"][r0:r0 + 128 * NTP, :], dr["yp"][r0:r0 + 128 * NTP, :], su == 0)
    S.op("sp", _mk("dma_start", out=dr["ssmp"][:, :], in_=Sf[0][:]), r=[("Sf", 0)], dma="o")
    S.op("sp", _mk("dma_start", out=dr["convp"][:, :], in_=histb[:].rearrange("p a b -> p (a b)")), r=[("histb", ct) for ct in range(24)], dma="o")
    nops = S.emit(nc, es)
    es.close()
    return nc, nops, dbg_names


def _consts():
    c = np.zeros((NCST, 128, 128), np.float32)
    idx = np.arange(128)
    k = idx[:, None]; i = idx[None, :]
    same = (k // 64) == (i // 64)
    c[K_ID] = np.eye(128)
    c[K_MC_P] = (k <= i)
    c[K_MC_S] = (k <= i) & same
    c[K_LM_P] = (k > i)
    c[K_LM_S] = (k > i) & same
    c[K_ONES] = 1.0
    c[K_SS0] = (k < 64) * np.ones((1, 128))
    c[K_SS1] = (k >= 64) * np.ones((1, 128))
    c[K_GM_P] = (k // 64) <= (i // 64)
    c[K_GM_S] = same
    c[K_CM0] = np.ones((128, 1)) * (i < 64)
    c[K_CM1] = np.ones((128, 1)) * (i >= 64)
    c[K_RM][:, 0] = (idx < 64)
    c[K_RM][:, 1] = (idx >= 64)
    return np.ascontiguousarray(c.transpose(1, 0, 2).reshape(128, NCST * 128))


_CACHE = {}


def kernel(**inp):
    f = lambda a: np.ascontiguousarray(np.asarray(a, dtype=np.float32))
    xpr = f(inp["x_prompt"]); xsm = f(inp["x_sample"])
    cache = f(inp["cache_conv"])[0]; state = f(inp["state_ssm"])[0]
    if "nc" not in _CACHE:
        _CACHE["nc"], _, _ = build_program()
    nc = _CACHE["nc"]
    shared = {
        "w_in": f(inp["w_in"])[0], "wa": f(inp["w_branch_a"])[0], "wb": f(inp["w_branch_b"])[0],
        "wo": f(inp["w_out"])[0], "wup": f(inp["w_up"])[0], "wdn": f(inp["w_down"])[0],
        "pre_mix_w": f(inp["pre_mix_w"]), "ln_w": f(inp["gmlp_ln_w"]), "ln_b": f(inp["gmlp_ln_b"]),
        "post_mix_w": f(inp["post_mix_w"]), "pre_ffn_w": f(inp["pre_ffn_w"]), "post_ffn_w": f(inp["post_ffn_w"]),
        "nw_l": np.ascontiguousarray(f(inp["ssm_norm_w"]).reshape(16, 128).T), "a_log": f(inp["a_log"]), "dt_bias": f(inp["dt_bias"]), "d_skip": f(inp["d_skip"]),
        "cst": _consts(),
    }
    ws = f(inp["gmlp_ws"])[0]
    bs = f(inp["gmlp_bs"])[0]
    shared["wsT_p"] = np.ascontiguousarray(ws.transpose(2, 0, 1).reshape(128, 1024))
    ws_s = np.tile(ws[:, :64, :64], (1, 2, 2))
    shared["wsT_s"] = np.ascontiguousarray(ws_s.transpose(2, 0, 1).reshape(128, 1024))
    shared["bs_p"] = np.ascontiguousarray(bs.reshape(1, 1024))
    shared["bs_s"] = np.ascontiguousarray(np.tile(bs[:, :64], (1, 2)).reshape(1, 1024))
    cw = f(inp["conv_w"])[0]
    shared["convw_l"] = np.ascontiguousarray(cw.reshape(4, 24, 128).transpose(2, 1, 0).reshape(128, 96))
    shared["convb_l"] = np.ascontiguousarray(f(inp["conv_b"])[0].reshape(24, 128).T)
    in_maps = []
    for c in range(NCORE):
        b, q = c // 4, c % 4
        m = dict(shared)
        m["xp"] = np.ascontiguousarray(xpr[b, q * PT:(q + 1) * PT])
        xh = np.zeros((4, D), np.float32)
        if q > 0:
            xh[0:3] = xpr[b, q * PT - 3:q * PT]
        m["xh"] = xh
        m["xsm"] = np.ascontiguousarray(xsm[2 * c:2 * c + 2].reshape(128, D))
        cc = cache[2 * c:2 * c + 2]
        m["cachel"] = np.ascontiguousarray(cc.reshape(2, 3, 24, 128).transpose(3, 0, 2, 1).reshape(128, 144))
        st = state[2 * c:2 * c + 2]
        m["stateT"] = np.ascontiguousarray(st.reshape(2, 2048, 128).transpose(0, 2, 1))
        xprev = np.zeros((3 * PT, D), np.float32)
        if q > 0:
            xprev[(3 - q) * PT:] = xpr[b, 0:q * PT]
        m["xprev"] = xprev
        pf = np.zeros((1, 3 * NSUP), np.float32)
        pf[0, (3 - q) * NSUP:] = 1.0
        m["pflag"] = pf
        in_maps.append(m)
    res = run_bass_kernel_spmd(nc, in_maps, core_ids=list(range(NCORE)))
    R = res.results
    yp = np.stack([np.concatenate([R[b * 4 + q]["yp"] for q in range(4)], axis=0) for b in range(2)])
    ys = np.concatenate([R[c]["ys"].reshape(2, 64, D) for c in range(NCORE)], axis=0)
    def conv_back(a, n):
        return a.reshape(128, n, 24, 3).transpose(1, 3, 2, 0).reshape(n, 3, 3072)
    def ssm_back(a):
        return a.T.reshape(32, 64, 128)
    convp = np.stack([conv_back(R[b * 4 + 3]["convp"], 1)[0] for b in range(2)])[None]
    ssmp = np.stack([ssm_back(R[b * 4 + 3]["ssmp"]) for b in range(2)])[None]
    convs = np.concatenate([conv_back(R[c]["convs"], 2) for c in range(NCORE)], axis=0)[None]
    ssms = np.stack([ssm_back(R[c]["ssms"][s]) for c in range(NCORE) for s in range(2)])[None]
    vs = np.concatenate([R[c]["vs"].reshape(2, 64, D) for c in range(NCORE)], axis=0)[None]
    out = (yp, ys, convp, ssmp, convs, ssms, vs)
    return tuple(np.ascontiguousarray(o, dtype=np.float32) for o in out)
```

```python
import numpy as np
from contextlib import ExitStack
import concourse.bass as bass
import concourse.mybir as mybir
from concourse.bass_utils import run_bass_kernel_spmd

F32 = mybir.dt.float32
BF16 = mybir.dt.bfloat16
AF = mybir.ActivationFunctionType
ALU = mybir.AluOpType

NCORE = 8
D = 1024
PT = 2048
NTP = 2
NSUP = PT // (128 * NTP)
EPS = 1e-6
NSLOT = 4
C_U, C_V, C_Z, C_XBC, C_DT, C_GA, C_GB = 0, 1024, 2048, 4096, 7168, 7200, 8224
(K_ID, K_MC_P, K_MC_S, K_LM_P, K_LM_S, K_ONES, K_SS0, K_SS1, K_GM_P, K_GM_S, K_CM0, K_CM1,
 K_RM) = range(13)
NCST = 13


def _mk(name, *a, **kw):
    return lambda e: getattr(e, name)(*a, **kw)


class Sched:
    def __init__(self):
        self.ops = []

    def op(self, eng, fn, r=(), w=(), dma=None):
        o = dict(eng=eng, fn=fn, r=tuple(r), w=tuple(w), dma=dma)
        self.ops.append(o)
        return o

    def placeholder(self):
        o = dict(eng=None, fn=None, r=(), w=(), dma=None)
        self.ops.append(o)
        return o

    def emit(self, nc, es, final_eng="sp"):
        ops = [o for o in self.ops if o["eng"] is not None]
        lastw, readers = {}, {}
        for i, o in enumerate(ops):
            o["id"] = i
            deps = {}
            for k in o["r"]:
                p = lastw.get(k)
                if p is not None:
                    deps[p] = "raw"
            for k in o["w"]:
                p = lastw.get(k)
                if p is not None:
                    deps.setdefault(p, "waw")
                for q in readers.get(k, ()):
                    deps.setdefault(q, "war")
            keep = {}
            for p, t in deps.items():
                po = ops[p]
                if po["dma"] is None and o["dma"] is None and po["eng"] == o["eng"]:
                    if o["eng"] == "pe" or t != "raw":
                        continue
                keep[p] = t
            o["deps"] = keep
            for k in o["r"]:
                readers.setdefault(k, set()).add(i)
            for k in o["w"]:
                lastw[k] = i
                readers[k] = set()
        need = set()
        for o in ops:
            need.update(o["deps"].keys())
        cnt = {}
        for o in ops:
            if o["dma"] is not None:
                s = "d_" + o["dma"]
                cnt[s] = cnt.get(s, 0) + 16
                o["sig"] = (s, cnt[s], 16)
            elif o["id"] in need:
                s = "e_" + o["eng"]
                cnt[s] = cnt.get(s, 0) + 1
                o["sig"] = (s, cnt[s], 1)
            else:
                o["sig"] = None
        sems = {name: es.enter_context(nc.semaphore(name)) for name in sorted(cnt)}
        block = es.enter_context(nc.Block())

        def runner(engname):
            def f(e):
                waited = {}
                for o in ops:
                    if o["eng"] != engname:
                        continue
                    req = {}
                    for p in o["deps"]:
                        s, v, _ = ops[p]["sig"]
                        if v > req.get(s, 0):
                            req[s] = v
                    for s, v in req.items():
                        if waited.get(s, 0) < v:
                            e.wait_ge(sems[s], v)
                            waited[s] = v
                    ins = o["fn"](e)
                    if o["sig"] is not None:
                        ins.then_inc(sems[o["sig"][0]], o["sig"][2])
                if engname == final_eng:
                    for s, v in cnt.items():
                        if s.startswith("d_") and waited.get(s, 0) < v:
                            e.wait_ge(sems[s], v)
            return f

        block.tensor(runner("pe"))
        block.scalar(runner("act"))
        block.vector(runner("dve"))
        block.gpsimd(runner("pool"))
        block.sync(runner("sp"))
        return len(ops)


def build_program(nsup=NSUP, debug=False, npre=3 * NSUP):
    nc = bass.Bass("TRN2", target_bir_lowering=False)
    S = Sched()
    es = ExitStack()
    dr = {}

    def din(name, shape):
        dr[name] = nc.dram_tensor(name, list(shape), F32, kind="ExternalInput").ap()

    def dout(name, shape):
        dr[name] = nc.dram_tensor(name, list(shape), F32, kind="ExternalOutput").ap()

    din("xp", [PT, D]); din("xh", [4, D]); din("xsm", [128, D])
    din("cachel", [128, 2 * 24 * 3]); din("stateT", [2, 128, 2048])
    din("w_in", [D, 9248]); din("wa", [D, D]); din("wb", [2048, D]); din("wo", [D, D])
    din("wup", [D, 4096]); din("wdn", [4096, D])
    for nm in ("pre_mix_w", "ln_w", "ln_b", "post_mix_w", "pre_ffn_w", "post_ffn_w", "bs_p", "bs_s"):
        din(nm, [1, D])
    din("nw_l", [128, 16])
    for nm in ("a_log", "dt_bias", "d_skip"):
        din(nm, [1, 32])
    din("convw_l", [128, 96]); din("convb_l", [128, 24])
    din("xprev", [3 * PT, D]); din("pflag", [1, 3 * NSUP])
    din("cst", [128, NCST * 128]); din("wsT_p", [128, 1024]); din("wsT_s", [128, 1024])
    dout("yp", [PT, D]); dout("ys", [128, D]); dout("convp", [128, 72]); dout("ssmp", [128, 2048])
    dout("convs", [128, 144]); dout("ssms", [2, 128, 2048]); dout("vs", [128, D])

    def sb(name, shape, dt=F32):
        return es.enter_context(nc.sbuf_tensor("s_" + name, list(shape), dt))

    TTM = 128 * NTP
    dbg_names = []

    def dbg(name, ap, keys, shape):
        if not debug:
            return
        d = nc.dram_tensor("dbg_" + name, [int(v) for v in ap.shape], F32, kind="ExternalOutput").ap()
        dbg_names.append("dbg_" + name)
        S.op("pool", _mk("dma_start", out=d, in_=ap), r=keys, dma="dbg")
    cst = sb("cst", [128, NCST, 128]); cstb = sb("cstb", [128, NCST, 128], BF16)
    bc = {nm: sb("bc_" + nm, [128, D]) for nm in ("pre_mix_w", "ln_w", "ln_b", "post_mix_w", "pre_ffn_w", "post_ffn_w")}
    bsb1 = sb("bsb", [128, 8, 128]); bsb = {"p": bsb1, "s": bsb1}
    nwc = sb("nwc", [128, 16])
    alog = sb("alog", [128, 32]); abc = sb("abc", [128, 32]); dtb = sb("dtb", [128, 32]); dsk = sb("dsk", [128, 32])
    convw = sb("convw", [128, 24, 4]); convb = sb("convb", [128, 24])
    neghalf = sb("neghalf", [128, 2])
    wsT = {"p": sb("wsT_p", [128, 8, 128], BF16), "s": sb("wsT_s", [128, 8, 128], BF16)}
    wdt = sb("wdt", [128, 8, 32], BF16)
    cachet = sb("cachet", [128, 2, 24, 3]); cso = sb("cso", [128, 2, 24, 3])
    histb = sb("histb", [128, 24, 3])
    pflagb = sb("pflagb", [128, 3 * NSUP])
    Sf0 = sb("Sf0", [128, 2048])
    Sb = [sb("Sb0", [128, 2048], BF16), sb("Sb1", [128, 2048], BF16)]
    slots = [sb("wslot%d" % i, [128, 4096], BF16) for i in range(NSLOT)]
    xres = sb("xres", [128, NTP, D])
    hT = sb("hT", [128, 8, TTM + 4], BF16)
    uT = sb("uT", [128, 8, TTM], BF16)
    vnb = sb("vnb", [128, NTP, D], BF16)
    oaT = sb("oaT", [128, 8, TTM], BF16)
    PA = sb("PA", [128, 8, TTM], BF16)
    xs_ = sb("xs", [128, NTP, 2048])
    Sf = [Sf0[:, :], xs_[:, 1, :]]
    Btok = sb("Btok", [128, NTP, 512], BF16)
    BT = sb("BT", [128, 4, TTM], BF16); CT = sb("CT", [128, 4, TTM], BF16)
    obT = sb("obT", [128, 16, TTM], BF16)
    mT = uT
    hidT_full = xs_[:].rearrange("p a b -> p (a b)").bitcast(BF16).rearrange("p (k c) -> p k c", k=32)
    hidT_small = xs_[:, 0, :].bitcast(BF16).rearrange("p (k c) -> p k c", k=32)
    hnb = sb("hnb", [128, D], BF16)
    vg = sb("vg", [128, D]); vn32 = sb("vn32", [128, D])
    wsraw = vg[:].rearrange("p (g c) -> p g c", g=8)
    junk = vn32
    stat = sb("stat", [128, 2, 6]); mv = sb("mv", [128, 2])
    sc = sb("sc", [128, 16])
    stage4 = sb("stage", [128, 4, TTM + 8]); cacc4 = sb("cacc", [128, 4, TTM])
    xcT = sb("xcT", [128, 4, TTM])
    dts = sb("dts", [128, NTP, 8, 32])
    dhi = sb("dhi", [128, NTP, 32], BF16); dlo = sb("dlo", [128, NTP, 32], BF16)
    decS = sb("decS", [128, NTP, 64])
    xdt = sb("xdt", [128, 512], BF16); xw = sb("xw", [128, 512], BF16)
    Bm = sb("Bm", [128, 2, 512], BF16); CTm = sb("CTm", [128, 2, 4, 128], BF16)
    cbm = sb("cbm", [128, 128])
    Rhi = sb("Rhi", [128, 8, 128], BF16); Rlo = sb("Rlo", [128, 8, 128], BF16)
    dec = sb("dec", [128, 8, 128]); Mb = sb("Mb", [128, 8, 128], BF16)
    t1 = sb("t1", [128, 512]); zs = sb("zs", [128, 512]); gnb = sb("gnb", [128, 512], BF16)
    stmp = sb("stmp", [128, 512])
    sg = sb("sg", [128, TTM]); mtmp = sb("mtmp", [128, TTM])
    rls = [sb("rl%d" % i, [128, TTM]) for i in range(3)]; fT = sb("fT", [128, TTM])
    yout = vg
    psb = [es.enter_context(nc.psum_tensor("ps%d" % i, [128, 512], F32)) for i in range(8)]
    pstate = {"n": 0}

    def bank():
        b = pstate["n"] % 8
        pstate["n"] += 1
        return b

    def PK(b):
        return ("ps", b)

    ws = {"n": 0, "ph": [S.placeholder() for _ in range(NSLOT)]}

    wscr = nc.dram_tensor("wscr", [48, 128, 4096], BF16)
    scr = {}

    def wget(srcd, kcn, cols):
        i = ws["n"]; ws["n"] += 1
        slot = i % NSLOT
        view = slots[slot][:, 0:kcn * cols].rearrange("p (k c) -> p k c", k=kcn)
        ph = ws["ph"].pop(0)
        bid = srcd["bid"]
        if bid not in scr:
            scr[bid] = len(scr)
            sv = wscr.ap()[scr[bid], :, 0:kcn * cols]
            ph.update(eng="pool", fn=_mk("dma_start", out=view, in_=srcd["ap"]), r=(), w=(("wslot", slot),), dma="w%d" % slot)
            S.op("sp", _mk("dma_start", out=sv, in_=slots[slot][:, 0:kcn * cols]), r=[("wslot", slot)], w=[("scr", scr[bid])], dma="sv%d" % slot)
        else:
            sv = wscr.ap()[scr[bid], :, 0:kcn * cols]
            ph.update(eng="sp", fn=_mk("dma_start", out=slots[slot][:, 0:kcn * cols], in_=sv), r=(("scr", scr[bid]),), w=(("wslot", slot),), dma="w%d" % slot)
        return view, ("wslot", slot)

    def wdone():
        ws["ph"].append(S.placeholder())

    def wsrc(name, r0, nr, c0, ncol):
        return dict(bid=(name, c0), ap=dr[name][r0:r0 + nr, c0:c0 + ncol].rearrange("(k p) c -> p k c", p=128))

    S.op("sp", _mk("dma_start", out=cst[:].rearrange("p k c -> p (k c)"), in_=dr["cst"][:, :]), w=["cst"], dma="c11")
    S.op("pool", _mk("dma_start", out=cstb[:].rearrange("p k c -> p (k c)"), in_=dr["cst"][:, :]), w=["cstb"], dma="c12")
    for nm in bc:
        S.op("sp", _mk("dma_start", out=bc[nm][:], in_=dr[nm].partition_broadcast(128)), w=["bc_" + nm], dma="c13_" + nm)
    S.op("sp", _mk("dma_start", out=bsb["s"][:].rearrange("p g c -> p (g c)"), in_=dr["bs_s"].partition_broadcast(128)), w=["bsb"], dma="c14")
    S.op("sp", _mk("dma_start", out=nwc[:], in_=dr["nw_l"][:, :]), w=["nwc"], dma="c15")
    S.op("sp", _mk("dma_start", out=alog[:], in_=dr["a_log"].partition_broadcast(128)), w=["alog"], dma="c16")
    S.op("sp", _mk("dma_start", out=dtb[:], in_=dr["dt_bias"].partition_broadcast(128)), w=["dtb"], dma="c17")
    S.op("sp", _mk("dma_start", out=dsk[:], in_=dr["d_skip"].partition_broadcast(128)), w=["dsk"], dma="c18")
    S.op("sp", _mk("dma_start", out=convw[:].rearrange("p a b -> p (a b)"), in_=dr["convw_l"][:, :]), w=["convw"], dma="c19")
    S.op("sp", _mk("dma_start", out=convb[:], in_=dr["convb_l"][:, :]), w=["convb"], dma="c20")
    S.op("sp", _mk("dma_start", out=cachet[:].rearrange("p a b c -> p (a b c)"), in_=dr["cachel"][:, :]), w=["cachet"], dma="c21")
    S.op("pool", _mk("dma_start", out=wdt[:], in_=wsrc("w_in", 0, D, C_DT, 32)["ap"]), w=["wdt"], dma="c22")
    S.op("dve", _mk("memset", neghalf[:], -0.5), w=["neghalf"])
    S.op("act", _mk("activation", out=abc[:], in_=alog[:], func=AF.Exp), r=["alog"], w=["abc"])
    S.op("dve", _mk("tensor_scalar", out=abc[:], in0=abc[:], scalar1=-1.0, scalar2=None, op0=ALU.mult), r=["abc"], w=["abc"])
    for kind, gm in (("p", K_GM_P), ("s", K_GM_S)):
        S.op("sp", _mk("dma_start", out=vg[:], in_=dr["wsT_" + kind][:, :]), w=["vg"], dma="c23")
        S.op("dve", _mk("tensor_tensor",
            out=wsT[kind][:], in0=wsraw, in1=cst[:, gm, :].unsqueeze(1).broadcast_to([128, 8, 128]), op=ALU.mult),
            r=["vg", "cst"], w=["wsT_" + kind])
    for s_ in range(2):
        S.op("sp", _mk("dma_start", out=Sf[s_][:], in_=dr["stateT"][s_, :, :]), w=[("Sf", s_)], dma="c24_%d" % s_)
        S.op("act", _mk("activation", out=Sb[s_][:], in_=Sf[s_][:], func=AF.Copy), r=[("Sf", s_)], w=[("Sb", s_)])

    def rstd_from(ss_ap, n, key):
        S.op("dve", _mk("tensor_scalar", out=ss_ap, in0=ss_ap, scalar1=1.0 / n, scalar2=EPS, op0=ALU.mult, op1=ALU.add), r=[key], w=[key])
        S.op("pool", _mk("tensor_tensor", out=ss_ap, in0=ss_ap, in1=neghalf[:, 0:1], op=ALU.pow), r=[key, "neghalf"], w=[key])

    def mm_group(out_ap, pairs, r, w):
        def fn(e):
            ins = None
            n = len(pairs)
            for i, (l, rr) in enumerate(pairs):
                ins = e.matmul(out_ap, lhsT=l, rhs=rr, start=(i == 0), stop=(i == n - 1))
            return ins
        S.op("pe", fn, r=r, w=w)

    def tr_group(items, r, w):
        def fn(e):
            ins = None
            for (o, i_, idn) in items:
                ins = e.transpose(o, i_, idn)
            return ins
        S.op("pe", fn, r=r, w=w)

    def norm_to_hT(src_ap, srckey, wname, t, ncols_valid=128):
        S.op("act", _mk("activation", out=junk[:], in_=src_ap, func=AF.Square, accum_out=sc[:, 0:1]), r=[srckey], w=["vn32", "sc0"])
        rstd_from(sc[:, 0:1], D, "sc0")
        S.op("dve", _mk("scalar_tensor_tensor", out=hnb[:], in0=src_ap, scalar=sc[:, 0:1], in1=bc[wname][:], op0=ALU.mult, op1=ALU.mult),
             r=[srckey, "sc0", "bc_" + wname], w=["hnb"])
        b = bank()
        pb = psb[b][:].bitcast(BF16)
        nv = ncols_valid
        tr_group([(pb[:, kc * 128:kc * 128 + nv], hnb[0:nv, kc * 128:(kc + 1) * 128], cstb[0:nv, K_ID, 0:nv]) for kc in range(8)],
                 r=["hnb", "cstb"], w=[PK(b)])
        c0 = t * 128
        S.op("act", _mk("activation", out=hT[:, :, c0:c0 + nv], in_=pb.rearrange("p (k c) -> p k c", k=8)[:, :, 0:nv], func=AF.Copy),
             r=[PK(b)], w=["hT"])

    def supertile(kind, NT, x_src, y_dst, first_prompt, pre=False, flagidx=None):
        TT = 128 * NT
        halo = 4 if first_prompt else 0
        hidT = hidT_full if NT == NTP else hidT_small

        def getw(srcd, kcn, cols):
            if pre:
                return prew[srcd["bid"]]
            return wget(srcd, kcn, cols)

        def donew():
            if not pre:
                wdone()
        NX = TT + halo
        if kind == "p":
            segs = [dict(sel=K_ONES, cm=None, rm=None, st=0)]
            mc, lm = K_MC_P, K_LM_P
            nsegc, L = 1, TT
        else:
            segs = [dict(sel=K_SS0, cm=K_CM0, rm=0, st=0), dict(sel=K_SS1, cm=K_CM1, rm=1, st=1)]
            mc, lm = K_MC_S, K_LM_S
            nsegc, L = 2, 64
        for t in range(NT):
            S.op("sp", _mk("dma_start", out=xres[:, t, :], in_=x_src[t * 128:(t + 1) * 128, :]), w=[("xres", t)], dma="x%d" % t)
            norm_to_hT(xres[:, t, :], ("xres", t), "pre_mix_w", t)
        if first_prompt:
            S.op("sp", _mk("dma_start", out=vg[0:4, :], in_=dr["xh"][:, :]), w=["vg"], dma="xh")
            norm_to_hT(vg[:, :], "vg", "pre_mix_w", NT, ncols_valid=4)
        for blk in (range(2) if not pre else ()):
            wv, wk = wget(wsrc("w_in", 0, D, C_U + blk * 512, 512), 8, 512)
            for sub in range(4):
                cb = blk * 4 + sub
                b = bank()
                mm_group(psb[b][:, 0:TT], [(wv[:, kc, sub * 128:(sub + 1) * 128], hT[:, kc, 0:TT]) for kc in range(8)],
                         r=[wk, "hT"], w=[PK(b)])
                S.op("act", _mk("activation", out=uT[:, cb, 0:TT], in_=psb[b][:, 0:TT], func=AF.Gelu_apprx_tanh),
                     r=[PK(b)], w=["uT"])
            wdone()
        vblk = [wget(wsrc("w_in", 0, D, C_V + blk * 512, 512), 8, 512) for blk in (range(2) if not pre else ())]
        for t in (range(NT) if not pre else ()):
            for blk in range(2):
                wv, wk = vblk[blk]
                b = bank()
                mm_group(psb[b][:, :], [(hT[:, kc, t * 128:(t + 1) * 128], wv[:, kc, :]) for kc in range(8)], r=[wk, "hT"], w=[PK(b)])
                S.op("act", _mk("activation", out=vg[:, blk * 512:(blk + 1) * 512], in_=psb[b][:, :], func=AF.Gelu_apprx_tanh),
                     r=[PK(b)], w=["vg"])
            for blk in range(2):
                S.op("dve", _mk("bn_stats", out=stat[:, blk, :], in_=vg[:, blk * 512:(blk + 1) * 512]), r=["vg"], w=["stat"])
            S.op("dve", _mk("bn_aggr", out=mv[:], in_=stat[:].rearrange("p a b -> p (a b)")), r=["stat"], w=["mv"])
            rstd_from(mv[:, 1:2], 1.0, "mv")
            S.op("dve", _mk("tensor_scalar", out=vn32[:], in0=vg[:], scalar1=mv[:, 0:1], scalar2=mv[:, 1:2], op0=ALU.subtract, op1=ALU.mult),
                 r=["vg", "mv"], w=["vn32"])
            S.op("dve", _mk("tensor_tensor", out=vn32[:], in0=vn32[:], in1=bc["ln_w"][:], op=ALU.mult), r=["vn32", "bc_ln_w"], w=["vn32"])
            S.op("dve", _mk("tensor_tensor", out=vn32[:], in0=vn32[:], in1=bc["ln_b"][:], op=ALU.add), r=["vn32", "bc_ln_b"], w=["vn32"])
            S.op("act", _mk("activation", out=vnb[:, t, :], in_=vn32[:], func=AF.Copy), r=["vn32"], w=[("vnb", t)])
            if kind == "s":
                S.op("sp", _mk("dma_start", out=dr["vs"][:, :], in_=vn32[:]), r=["vn32"], dma="o")
            for half in range(2):
                b = bank()
                def fn(e, b=b, half=half, t=t):
                    ins = None
                    for gi in range(4):
                        g = half * 4 + gi
                        ins = e.matmul(psb[b][:, gi * 128:(gi + 1) * 128], lhsT=vnb[:, t, g * 128:(g + 1) * 128], rhs=wsT[kind][:, g, :], start=True, stop=True)
                    return ins
                S.op("pe", fn, r=[("vnb", t), "wsT_" + kind], w=[PK(b)])
                S.op("dve", _mk("tensor_tensor",
                    out=stmp[:].rearrange("p (g c) -> p g c", g=4), in0=psb[b][:, :].rearrange("p (g c) -> p g c", g=4),
                    in1=bsb[kind][:, half * 4:half * 4 + 4, :], op=ALU.add), r=[PK(b), "bsb"], w=["stmp"])
                S.op("dve", _mk("tensor_tensor",
                    out=oaT[:, half * 4:half * 4 + 4, t * 128:(t + 1) * 128], in0=stmp[:].rearrange("p (g c) -> p g c", g=4),
                    in1=uT[:, half * 4:half * 4 + 4, t * 128:(t + 1) * 128], op=ALU.mult), r=["stmp", "uT"], w=["oaT"])
        if not pre:
            wdone(); wdone()
        for blk in (range(2) if not pre else ()):
            wa_v, wa_k = wget(wsrc("wa", 0, D, blk * 512, 512), 8, 512)
            ga_v, ga_k = wget(wsrc("w_in", 0, D, C_GA + blk * 512, 512), 8, 512)
            for sub in range(4):
                fb = blk * 4 + sub
                b1 = bank(); b2 = bank()
                mm_group(psb[b1][:, 0:TT], [(ga_v[:, kc, sub * 128:(sub + 1) * 128], hT[:, kc, 0:TT]) for kc in range(8)], r=[ga_k, "hT"], w=[PK(b1)])
                mm_group(psb[b2][:, 0:TT], [(wa_v[:, kc, sub * 128:(sub + 1) * 128], oaT[:, kc, 0:TT]) for kc in range(8)], r=[wa_k, "oaT"], w=[PK(b2)])
                S.op("act", _mk("activation", out=sg[:, 0:TT], in_=psb[b1][:, 0:TT], func=AF.Sigmoid), r=[PK(b1)], w=["sg"])
                S.op("dve", _mk("tensor_tensor", out=PA[:, fb, 0:TT], in0=sg[:, 0:TT], in1=psb[b2][:, 0:TT], op=ALU.mult),
                     r=["sg", PK(b2)], w=["PA"])
            wdone(); wdone()
        for t in range(NT):
            b = bank()
            mm_group(psb[b][:, 0:32], [(hT[:, kc, t * 128:(t + 1) * 128], wdt[:, kc, :]) for kc in range(8)], r=["hT", "wdt"], w=[PK(b)])
            dk = ("dts", t)
            A = lambda i, t=t: dts[:, t, i, :]
            S.op("dve", _mk("tensor_tensor", out=dts[:, t, 6, :], in0=psb[b][:, 0:32], in1=dtb[:], op=ALU.add), r=[PK(b), "dtb"], w=[dk])
            S.op("act", _mk("activation", out=dts[:, t, 7, :], in_=dts[:, t, 6, :], func=AF.Abs), r=[dk], w=[dk])
            S.op("act", _mk("activation", out=dts[:, t, 7, :], in_=dts[:, t, 7, :], func=AF.Exp, scale=-1.0), r=[dk], w=[dk])
            S.op("act", _mk("activation", out=dts[:, t, 7, :], in_=dts[:, t, 7, :], func=AF.Ln, bias=1.0), r=[dk], w=[dk])
            S.op("dve", _mk("scalar_tensor_tensor", out=dts[:, t, 0, :], in0=dts[:, t, 6, :], scalar=0.0, in1=dts[:, t, 7, :], op0=ALU.max, op1=ALU.add), r=[dk], w=[dk])
            S.op("dve", _mk("tensor_tensor", out=dts[:, t, 1, :], in0=dts[:, t, 0, :], in1=abc[:], op=ALU.mult), r=[dk, "abc"], w=[dk])
            S.op("dve", _mk("tensor_copy", out=dhi[:, t, :], in_=dts[:, t, 1, :]), r=[dk], w=[("dhi", t)])
            S.op("dve", _mk("tensor_tensor", out=dlo[:, t, :], in0=dts[:, t, 1, :], in1=dhi[:, t, :], op=ALU.subtract), r=[dk, ("dhi", t)], w=[("dlo", t)])
            b2 = bank()
            def fn(e, b2=b2, t=t):
                ins = e.matmul(psb[b2][:, 0:32], lhsT=cst[:, mc, :], rhs=dts[:, t, 1, :], start=True, stop=True)
                for si, sgm in enumerate(segs):
                    ins = e.matmul(psb[b2][:, 32 + 32 * si:64 + 32 * si], lhsT=cst[:, sgm["sel"], :], rhs=dts[:, t, 1, :], start=True, stop=True)
                return ins
            S.op("pe", fn, r=[dk, "cst"], w=[PK(b2)])
            S.op("act", _mk("activation", out=dts[:, t, 2, :], in_=psb[b2][:, 0:32], func=AF.Copy), r=[PK(b2)], w=[dk])
            S.op("act", _mk("activation", out=dts[:, t, 3, :], in_=psb[b2][:, 0:32], func=AF.Exp), r=[PK(b2)], w=[dk])
            S.op("act", _mk("activation", out=decS[:, t, 0:32 * len(segs)], in_=psb[b2][:, 32:32 + 32 * len(segs)], func=AF.Exp), r=[PK(b2)], w=[("decS", t)])
            if kind == "p":
                S.op("dve", _mk("tensor_tensor", out=dts[:, t, 4, :], in0=psb[b2][:, 32:64], in1=dts[:, t, 2, :], op=ALU.subtract), r=[PK(b2), dk], w=[dk])
            else:
                for si in range(2):
                    S.op("dve", _mk("tensor_tensor",
                        out=dts[64 * si:64 * si + 64, t, 4, :], in0=psb[b2][64 * si:64 * si + 64, 32 + 32 * si:64 + 32 * si],
                        in1=dts[64 * si:64 * si + 64, t, 2, :], op=ALU.subtract), r=[PK(b2), dk], w=[dk])
            S.op("act", _mk("activation", out=dts[:, t, 4, :], in_=dts[:, t, 4, :], func=AF.Exp), r=[dk], w=[dk])
            S.op("dve", _mk("tensor_tensor", out=dts[:, t, 5, :], in0=dts[:, t, 4, :], in1=dts[:, t, 0, :], op=ALU.mult), r=[dk], w=[dk])
            if pre:
                S.op("dve", _mk("tensor_scalar", out=dts[:, t, 5, :], in0=dts[:, t, 5, :], scalar1=pflagb[:, flagidx:flagidx + 1], scalar2=None, op0=ALU.mult),
                     r=[dk, "pflagb"], w=[dk])

        if kind == "s":
            dbg("dts", dts[:, 0, :, :].rearrange("p a b -> p (a b)"), [("dts", 0)], [128, 256])
            dbg("decS", decS[:, 0, :], [("decS", 0)], [128, 64])
            dbg("hT", hT[:, :, 0:128], ["hT"], [128, 1024])
        def conv_block(cts, wv, wk, outs):
            bks = [bank() for _ in cts]
            for i, ct in enumerate(cts):
                mm_group(psb[bks[i]][:, 0:NX], [(wv[:, kc, i * 128:(i + 1) * 128], hT[:, kc, 0:NX]) for kc in range(8)], r=[wk, "hT"], w=[PK(bks[i])])
            st3s = []
            for i, ct in enumerate(cts):
                b = bks[i]
                stage = stage4[:, i, :]
                sk = ("stage", i)
                st3 = stage[:, 0:nsegc * (L + 3)].rearrange("p (s c) -> p s c", s=nsegc)
                st3s.append(st3)
                if first_prompt:
                    S.op("act", _mk("activation", out=stage[:, 0:3], in_=psb[b][:, TT:TT + 3], func=AF.Copy), r=[PK(b)], w=[sk])
                elif kind == "p":
                    S.op("pool", _mk("tensor_copy", out=stage[:, 0:3], in_=histb[:, ct, :]), r=[("histb", ct)], w=[sk])
                else:
                    S.op("pool", _mk("tensor_copy", out=st3[:, :, 0:3], in_=cachet[:, :, ct, :]), r=["cachet"], w=[sk])
                S.op("act", _mk("activation", out=st3[:, :, 3:3 + L], in_=psb[b][:, 0:TT].rearrange("p (s c) -> p s c", s=nsegc), func=AF.Copy),
                     r=[PK(b)], w=[sk])
                if kind == "p":
                    S.op("pool", _mk("tensor_copy", out=histb[:, ct, :], in_=stage[:, L:L + 3]), r=[sk], w=[("histb", ct)])
                else:
                    S.op("pool", _mk("tensor_copy", out=cso[:, :, ct, :], in_=st3[:, :, L:L + 3]), r=[sk], w=["cso"])
            ca3s = [cacc4[:, i, 0:TT].rearrange("p (s c) -> p s c", s=nsegc) for i in range(len(cts))]
            for i, ct in enumerate(cts):
                S.op("dve", _mk("tensor_scalar", out=ca3s[i], in0=st3s[i][:, :, 0:L], scalar1=convw[:, ct, 0:1], scalar2=convb[:, ct:ct + 1], op0=ALU.mult, op1=ALU.add),
                     r=[("stage", i), "convw", "convb"], w=[("cacc", i)])
            for k in range(1, 4):
                for i, ct in enumerate(cts):
                    S.op("dve", _mk("scalar_tensor_tensor", out=ca3s[i], in0=st3s[i][:, :, k:k + L], scalar=convw[:, ct, k:k + 1], in1=ca3s[i], op0=ALU.mult, op1=ALU.add),
                         r=[("stage", i), ("cacc", i), "convw"], w=[("cacc", i)])
            for i, ct in enumerate(cts):
                S.op("act", _mk("activation", out=outs[i][0], in_=cacc4[:, i, 0:TT], func=AF.Silu), r=[("cacc", i)], w=[outs[i][1]])

        wv, wk = getw(wsrc("w_in", 0, D, C_XBC + 2048, 512), 8, 512)
        conv_block([16 + g for g in range(4)], wv, wk, [(BT[:, g, 0:TT], "BT") for g in range(4)])
        donew()
        for t in range(NT):
            b = bank()
            pb = psb[b][:].bitcast(BF16)
            tr_group([(pb[:, g * 128:(g + 1) * 128], BT[:, g, t * 128:(t + 1) * 128], cstb[:, K_ID, :]) for g in range(4)], r=["BT", "cstb"], w=[PK(b)])
            S.op("act", _mk("activation", out=Btok[:, t, :], in_=pb[:, 0:512], func=AF.Copy), r=[PK(b)], w=[("Btok", t)])
        if not pre:
            wv, wk = getw(wsrc("w_in", 0, D, C_XBC + 2560, 512), 8, 512)
            conv_block([20 + g for g in range(4)], wv, wk, [(CT[:, g, 0:TT], "CT") for g in range(4)])
            donew()
        for g in range(4):
            wv, wk = getw(wsrc("w_in", 0, D, C_XBC + g * 512, 512), 8, 512)
            conv_block([g * 4 + sub for sub in range(4)], wv, wk, [(xcT[:, sub, 0:TT], "xcT") for sub in range(4)])
            donew()
            for t in range(NT):
                b = bank()
                tr_group([(psb[b][:, sub * 128:(sub + 1) * 128], xcT[:, sub, t * 128:(t + 1) * 128], cst[:, K_ID, :]) for sub in range(4)],
                         r=["xcT", "cst"], w=[PK(b)])
                S.op("act", _mk("activation", out=xs_[:, t, g * 512:(g + 1) * 512], in_=psb[b][:, :], func=AF.Copy), r=[PK(b)], w=[("xs", t), "hidT"])
            if not pre:
                zv, zk = wget(wsrc("w_in", 0, D, C_Z + g * 512, 512), 8, 512)
            for t in range(NT):
                dk = ("dts", t)
                tc_ = slice(t * 128, (t + 1) * 128)
                gc = slice(g * 512, (g + 1) * 512)
                xs3 = xs_[:, t, g * 512:(g + 1) * 512].rearrange("p (h c) -> p h c", h=8)
                if not pre:
                  S.op("dve", _mk("tensor_tensor", out=xdt[:].rearrange("p (h c) -> p h c", h=8), in0=xs3,
                     in1=dts[:, t, 0, g * 8:(g + 1) * 8].unsqueeze(2).broadcast_to([128, 8, 64]), op=ALU.mult), r=[("xs", t), dk], w=["xdt"])
                S.op("dve", _mk("tensor_tensor", out=xw[:].rearrange("p (h c) -> p h c", h=8), in0=xs3,
                     in1=dts[:, t, 5, g * 8:(g + 1) * 8].unsqueeze(2).broadcast_to([128, 8, 64]), op=ALU.mult), r=[("xs", t), dk], w=["xw"])
                if first_prompt and not pre and g == 0 and t == 0:
                    dbg("pxs0", xs_[:, 0, 0:512], [("xs", 0)], None)
                    dbg("pBT", BT[:, 0, 0:128], ["BT"], None); dbg("pCT", CT[:, 0, 0:128], ["CT"], None)
                    dbg("pSb", Sb[0][:, 0:512], [("Sb", 0)], None)
                    dbg("phT", hT[:, :, 252:260], ["hT"], None)
                if kind == "s" and g == 0:
                    dbg("xs0", xs_[:, 0, 0:512], [("xs", 0)], [128, 512])
                    dbg("xdt", xdt[:], ["xdt"], [128, 512])
                    dbg("xw", xw[:], ["xw"], [128, 512])
                    dbg("BT", BT[:, 0, 0:128], ["BT"], [128, 128])
                    dbg("CT", CT[:, 0, 0:128], ["CT"], [128, 128])
                    dbg("Btok", Btok[:, 0, 0:128], [("Btok", 0)], [128, 128])
                if not pre:
                    bcb = bank()
                    mm_group(psb[bcb][:, 0:128], [(BT[:, g, tc_], CT[:, g, tc_])], r=["BT", "CT"], w=[PK(bcb)])
                    S.op("dve", _mk("tensor_tensor", out=cbm[:], in0=psb[bcb][:, 0:128], in1=cst[:, mc, :], op=ALU.mult), r=[PK(bcb), "cst"], w=["cbm"])
                    S.op("pool", _mk("tensor_tensor", out=Rhi[:], in0=cstb[:, mc, :].unsqueeze(1).broadcast_to([128, 8, 128]),
                         in1=dhi[:, t, g * 8:(g + 1) * 8].unsqueeze(2).broadcast_to([128, 8, 128]), op=ALU.mult), r=["cstb", ("dhi", t)], w=["Rhi"])
                    S.op("pool", _mk("tensor_tensor", out=Rlo[:], in0=cstb[:, mc, :].unsqueeze(1).broadcast_to([128, 8, 128]),
                         in1=dlo[:, t, g * 8:(g + 1) * 8].unsqueeze(2).broadcast_to([128, 8, 128]), op=ALU.mult), r=["cstb", ("dlo", t)], w=["Rlo"])
                    bs0 = bank(); bs1 = bank()
                    for hh, bb in ((0, bs0), (1, bs1)):
                        mm_group(psb[bb][:, :], [(cstb[:, lm, :], Rhi[:, hh * 4:hh * 4 + 4, :].rearrange("p h c -> p (h c)")),
                                                (cstb[:, lm, :], Rlo[:, hh * 4:hh * 4 + 4, :].rearrange("p h c -> p (h c)"))],
                                 r=["cstb", "Rhi", "Rlo"], w=[PK(bb)])
                        S.op("act", _mk("activation", out=dec[:, hh * 4:hh * 4 + 4, :].rearrange("p h c -> p (h c)"), in_=psb[bb][:, :], func=AF.Exp),
                             r=[PK(bb)], w=["dec"])
                    S.op("dve", _mk("tensor_tensor", out=Mb[:], in0=dec[:], in1=cbm[:].unsqueeze(1).broadcast_to([128, 8, 128]), op=ALU.mult),
                         r=["dec", "cbm"], w=["Mb"])
                    if kind == "s" and g == 0:
                        dbg("cbm", cbm[:], ["cbm"], [128, 128])
                        dbg("dec", dec[:].rearrange("p a b -> p (a b)"), ["dec"], [128, 1024])
                        dbg("Mb", Mb[:].rearrange("p a b -> p (a b)"), ["Mb"], [128, 1024])
                    byd = bank()
                    def fn(e, byd=byd):
                        ins = None
                        for h in range(8):
                            ins = e.matmul(psb[byd][:, h * 64:(h + 1) * 64], lhsT=Mb[:, h, :], rhs=xdt[:, h * 64:(h + 1) * 64], start=True, stop=True)
                        return ins
                    S.op("pe", fn, r=["Mb", "xdt"], w=[PK(byd)])
                    byo = bank()
                    gc = slice(g * 512, (g + 1) * 512)
                    if kind == "p":
                        mm_group(psb[byo][:, :], [(CT[:, g, tc_], Sb[0][:, gc])], r=["CT", ("Sb", 0)], w=[PK(byo)])
                    else:
                        for si in range(2):
                            S.op("dve", _mk("tensor_tensor", out=CTm[:, si, g, :], in0=CT[:, g, tc_], in1=cstb[:, K_CM0 + si, :], op=ALU.mult),
                                 r=["CT", "cstb"], w=["CTm"])
                        mm_group(psb[byo][:, :], [(CTm[:, 0, g, :], Sb[0][:, gc]), (CTm[:, 1, g, :], Sb[1][:, gc])], r=["CTm", ("Sb", 0), ("Sb", 1)], w=[PK(byo)])
                for si, sgm in enumerate(segs):
                    bst = bank()
                    sidx = sgm["st"]
                    if kind == "p":
                        lhs = Btok[:, t, g * 128:(g + 1) * 128]
                        rk = [("Btok", t)]
                    else:
                        S.op("dve", _mk("tensor_scalar", out=Bm[:, si, g * 128:(g + 1) * 128], in0=Btok[:, t, g * 128:(g + 1) * 128],
                             scalar1=cst[:, K_RM, si:si + 1], scalar2=None, op0=ALU.mult), r=[("Btok", t), "cst"], w=["Bm"])
                        lhs = Bm[:, si, g * 128:(g + 1) * 128]
                        rk = ["Bm"]
                    mm_group(psb[bst][:, :], [(lhs, xw[:, :])], r=rk + ["xw"], w=[PK(bst)])
                    S.op("dve", _mk("tensor_tensor",
                        out=Sf[sidx][:, gc].rearrange("p (h c) -> p h c", h=8), in0=Sf[sidx][:, gc].rearrange("p (h c) -> p h c", h=8),
                        in1=decS[:, t, 32 * si + g * 8:32 * si + g * 8 + 8].unsqueeze(2).broadcast_to([128, 8, 64]), op=ALU.mult),
                        r=[("Sf", sidx), ("decS", t)], w=[("Sf", sidx)])
                    S.op("dve", _mk("tensor_tensor", out=Sf[sidx][:, gc], in0=Sf[sidx][:, gc], in1=psb[bst][:, :], op=ALU.add),
                         r=[("Sf", sidx), PK(bst)], w=[("Sf", sidx)])
                    if not pre:
                        S.op("act", _mk("activation", out=Sb[sidx][:, gc], in_=Sf[sidx][:, gc], func=AF.Copy), r=[("Sf", sidx)], w=[("Sb", sidx)])
                if not pre:
                    S.op("dve", _mk("tensor_tensor", out=t1[:].rearrange("p (h c) -> p h c", h=8), in0=psb[byo][:, :].rearrange("p (h c) -> p h c", h=8),
                         in1=dts[:, t, 3, g * 8:(g + 1) * 8].unsqueeze(2).broadcast_to([128, 8, 64]), op=ALU.mult), r=[PK(byo), dk], w=["t1"])
                    S.op("dve", _mk("tensor_tensor", out=t1[:], in0=t1[:], in1=psb[byd][:, :], op=ALU.add), r=["t1", PK(byd)], w=["t1"])
                    S.op("dve", _mk("tensor_tensor", out=xs3, in0=xs3, in1=dsk[:, g * 8:(g + 1) * 8].unsqueeze(2).broadcast_to([128, 8, 64]), op=ALU.mult),
                         r=[("xs", t), "dsk"], w=[("xs", t)])
                    S.op("dve", _mk("tensor_tensor", out=t1[:], in0=t1[:], in1=xs_[:, t, gc], op=ALU.add), r=["t1", ("xs", t)], w=["t1"])
                    if first_prompt and g == 0 and t == 0:
                        dbg("py", t1[:], ["t1"], None)
                    if kind == "s" and g == 0:
                        dbg("y", t1[:], ["t1"], [128, 512])
                        dbg("Sf0", Sf[0][:, 0:512], [("Sf", 0)], [128, 512])
                    bz = bank()
                    mm_group(psb[bz][:, :], [(hT[:, kc, tc_], zv[:, kc, :]) for kc in range(8)], r=["hT", zk], w=[PK(bz)])
                    S.op("act", _mk("activation", out=zs[:], in_=psb[bz][:, :], func=AF.Silu), r=[PK(bz)], w=["zs"])
                    S.op("dve", _mk("tensor_tensor", out=t1[:], in0=t1[:], in1=zs[:], op=ALU.mult), r=["t1", "zs"], w=["t1"])
                    S.op("act", _mk("activation", out=zs[:], in_=t1[:], func=AF.Square, accum_out=sc[:, 1:2]), r=["t1"], w=["zs", "sc1"])
                    rstd_from(sc[:, 1:2], 512.0, "sc1")
                    S.op("dve", _mk("tensor_scalar", out=gnb[:], in0=t1[:], scalar1=sc[:, 1:2], scalar2=None, op0=ALU.mult),
                         r=["t1", "sc1"], w=["gnb"])
                    bt = bank()
                    pb = psb[bt][:].bitcast(BF16)
                    tr_group([(pb[:, sub * 128:(sub + 1) * 128], gnb[:, sub * 128:(sub + 1) * 128], cstb[:, K_ID, :]) for sub in range(4)], r=["gnb", "cstb"], w=[PK(bt)])
                    S.op("dve", _mk("tensor_tensor", out=obT[:, g * 4:g * 4 + 4, t * 128:(t + 1) * 128], in0=pb[:, 0:512].rearrange("p (s c) -> p s c", s=4),
                         in1=nwc[:, g * 4:g * 4 + 4].unsqueeze(2).broadcast_to([128, 4, 128]), op=ALU.mult), r=[PK(bt), "nwc"], w=["obT"])
            if not pre:
                wdone()
        if kind == "s":
            dbg("obT", obT[:, :, 0:128], ["obT"], [128, 2048])
            dbg("PA", PA[:, :, 0:128], ["PA"], [128, 1024])
        if not pre:
            gbs = []
            for q in range(4):
                wv, wk = wget(wsrc("wb", 0, 2048, q * 256, 256), 16, 256)
                if q % 2 == 0:
                    gv, gk = wget(wsrc("w_in", 0, D, C_GB + (q // 2) * 512, 512), 8, 512)
                for sub in range(2):
                    fb = q * 2 + sub
                    gsub = (q % 2) * 2 + sub
                    b1 = bank(); b2 = bank()
                    mm_group(psb[b1][:, 0:TT], [(gv[:, kc, gsub * 128:(gsub + 1) * 128], hT[:, kc, 0:TT]) for kc in range(8)], r=[gk, "hT"], w=[PK(b1)])
                    mm_group(psb[b2][:, 0:TT], [(wv[:, kc, sub * 128:(sub + 1) * 128], obT[:, kc, 0:TT]) for kc in range(16)], r=[wk, "obT"], w=[PK(b2)])
                    S.op("act", _mk("activation", out=sg[:, 0:TT], in_=psb[b1][:, 0:TT], func=AF.Sigmoid), r=[PK(b1)], w=["sg"])
                    S.op("dve", _mk("tensor_tensor", out=mtmp[:, 0:TT], in0=sg[:, 0:TT], in1=psb[b2][:, 0:TT], op=ALU.mult), r=["sg", PK(b2)], w=["mtmp"])
                    S.op("dve", _mk("tensor_tensor", out=mT[:, fb, 0:TT], in0=mtmp[:, 0:TT], in1=PA[:, fb, 0:TT], op=ALU.add), r=["mtmp", "PA"], w=["uT"])
                if q % 2 == 0:
                    wdone()
                    pending_gb = True
                else:
                    wdone(); wdone()
            if kind == "s":
                dbg("mT", mT[:, :, 0:128], ["uT"], [128, 1024])
            wo_blk = [wget(wsrc("wo", 0, D, blk * 512, 512), 8, 512) for blk in range(2)]
            for t in range(NT):
                bo = [bank(), bank()]
                for blk in range(2):
                    wv, wk = wo_blk[blk]
                    mm_group(psb[bo[blk]][:, :], [(mT[:, kc, t * 128:(t + 1) * 128], wv[:, kc, :]) for kc in range(8)], r=["uT", wk], w=[PK(bo[blk])])
                    S.op("act", _mk("activation", out=junk[:, blk * 512:(blk + 1) * 512], in_=psb[bo[blk]][:, :], func=AF.Square, accum_out=sc[:, 2 + blk:3 + blk]),
                         r=[PK(bo[blk])], w=["vn32", "sc2"])
                S.op("dve", _mk("tensor_tensor", out=sc[:, 2:3], in0=sc[:, 2:3], in1=sc[:, 3:4], op=ALU.add), r=["sc2"], w=["sc2"])
                rstd_from(sc[:, 2:3], D, "sc2")
                for blk in range(2):
                    cs = slice(blk * 512, (blk + 1) * 512)
                    S.op("dve", _mk("scalar_tensor_tensor", out=junk[:, cs], in0=psb[bo[blk]][:, :], scalar=sc[:, 2:3], in1=bc["post_mix_w"][:, cs], op0=ALU.mult, op1=ALU.mult),
                         r=[PK(bo[blk]), "sc2", "bc_post_mix_w"], w=["vn32"])
                S.op("dve", _mk("tensor_tensor", out=xres[:, t, :], in0=xres[:, t, :], in1=junk[:], op=ALU.add), r=[("xres", t), "vn32"], w=[("xres", t)])
                norm_to_hT(xres[:, t, :], ("xres", t), "pre_ffn_w", t)
            wdone(); wdone()
            if kind == "s":
                dbg("x1", xres[:, 0, :], [("xres", 0)], [128, 1024])
            for blk in range(8):
                wv, wk = wget(wsrc("wup", 0, D, blk * 512, 512), 8, 512)
                for sub in range(4):
                    hb = blk * 4 + sub
                    b = bank()
                    mm_group(psb[b][:, 0:TT], [(wv[:, kc, sub * 128:(sub + 1) * 128], hT[:, kc, 0:TT]) for kc in range(8)], r=[wk, "hT"], w=[PK(b)])
                    rl = rls[hb % 3]
                    S.op("act", _mk("activation", out=rl[:, 0:TT], in_=psb[b][:, 0:TT], func=AF.Relu), r=[PK(b)], w=[("rl", hb % 3)])
                    S.op("pool", _mk("tensor_tensor", out=hidT[:, hb, 0:TT], in0=rl[:, 0:TT], in1=rl[:, 0:TT], op=ALU.mult), r=[("rl", hb % 3)], w=["hidT"] + [("xs", tt_) for tt_ in range(NT)])
                wdone()
            bdn = [[bank(), bank()] for _ in range(NT)]
            for fb in range(8):
                wv, wk = wget(wsrc("wdn", 0, 4096, fb * 128, 128), 32, 128)
                b = bank()
                while any(b in pr for pr in bdn):
                    b = bank()
                mm_group(psb[b][:, 0:TT], [(wv[:, kc, :], hidT[:, kc, 0:TT]) for kc in range(32)], r=[wk, "hidT"], w=[PK(b)])
                S.op("act", _mk("activation", out=fT[:, 0:TT], in_=psb[b][:, 0:TT], func=AF.Copy), r=[PK(b)], w=["fT"])
                for t in range(NT):
                    bb = bdn[t][fb // 4]
                    c0 = (fb % 4) * 128
                    tr_group([(psb[bb][:, c0:c0 + 128], fT[:, t * 128:(t + 1) * 128], cst[:, K_ID, :])], r=["fT", "cst"], w=[PK(bb)])
                wdone()
            for t in range(NT):
                for blk in range(2):
                    bb = bdn[t][blk]
                    S.op("act", _mk("activation", out=junk[:, blk * 512:(blk + 1) * 512], in_=psb[bb][:, :], func=AF.Square, accum_out=sc[:, 4 + blk:5 + blk]),
                         r=[PK(bb)], w=["vn32", "sc4"])
                S.op("dve", _mk("tensor_tensor", out=sc[:, 4:5], in0=sc[:, 4:5], in1=sc[:, 5:6], op=ALU.add), r=["sc4"], w=["sc4"])
                rstd_from(sc[:, 4:5], D, "sc4")
                for blk in range(2):
                    bb = bdn[t][blk]
                    cs = slice(blk * 512, (blk + 1) * 512)
                    S.op("dve", _mk("scalar_tensor_tensor", out=yout[:, cs], in0=psb[bb][:, :], scalar=sc[:, 4:5], in1=bc["post_ffn_w"][:, cs], op0=ALU.mult, op1=ALU.mult),
                         r=[PK(bb), "sc4", "bc_post_ffn_w"], w=["vg"])
                S.op("dve", _mk("tensor_tensor", out=yout[:], in0=yout[:], in1=xres[:, t, :], op=ALU.add), r=["vg", ("xres", t)], w=["vg"])
                S.op("sp", _mk("dma_start", out=y_dst[t * 128:(t + 1) * 128, :], in_=yout[:]), r=["vg"], dma="o")

    supertile("s", 1, dr["xsm"], dr["ys"], False)
    for s_ in range(2):
        S.op("sp", _mk("dma_start", out=dr["ssms"][s_, :, :], in_=Sf[s_][:]), r=[("Sf", s_), ("xs", s_)], dma="o")
    S.op("sp", _mk("dma_start", out=dr["convs"][:, :], in_=cso[:].rearrange("p a b c -> p (a b c)")), r=["cso"], dma="o")
    S.op("sp", _mk("dma_start", out=bsb1[:].rearrange("p g c -> p (g c)"), in_=dr["bs_p"].partition_broadcast(128)), w=["bsb"], dma="c25")
    S.op("dve", _mk("memset", Sf[0][:], 0.0), w=[("Sf", 0)])
    S.op("dve", _mk("memset", histb[:], 0.0), w=[("histb", ct) for ct in range(24)])
    S.op("sp", _mk("dma_start", out=pflagb[:], in_=dr["pflag"].partition_broadcast(128)), w=["pflagb"], dma="c26")
    ws["ph"] = []
    prew = {}
    res_bufs = [(slots[k][:, :], ("wslot", k)) for k in range(4)] + [(obT[:].rearrange("p a b -> p (a b)"), "obT")]
    res_blocks = [wsrc("w_in", 0, D, C_XBC + g * 512, 512) for g in range(4)] + [wsrc("w_in", 0, D, C_XBC + 2048, 512)]
    for k, srcd in enumerate(res_blocks):
        buf, key = res_bufs[k]
        view = buf.rearrange("p (k c) -> p k c", k=8)
        S.op("pool", _mk("dma_start", out=view, in_=srcd["ap"]), w=[key], dma="pw%d" % k)
        prew[srcd["bid"]] = (view, key)
        if srcd["bid"] not in scr:
            scr[srcd["bid"]] = len(scr)
            S.op("sp", _mk("dma_start", out=wscr.ap()[scr[srcd["bid"]], :, :], in_=buf), r=[key], w=[("scr", scr[srcd["bid"]])], dma="psv%d" % k)
    for su in range(npre):
        r0 = su * 128 * NTP
        supertile("p", NTP, dr["xprev"][r0:r0 + 128 * NTP, :], None, False, pre=True, flagidx=su)
    ws["ph"] = [S.placeholder() for _ in range(NSLOT)]
    S.op("act", _mk("activation", out=Sb[0][:], in_=Sf[0][:], func=AF.Copy), r=[("Sf", 0)], w=[("Sb", 0)])
    dbg("hinit", Sf[0][:], [("Sf", 0)], None)
    for su in range(nsup):
        r0 = su * 128 * NTP
        supertile("p", NTP, dr["xp"][r0:r0 + 128 * NTP, :], dr["yp"][r0:r0 + 128 * NTP, :], su == 0)
    S.op("sp", _mk("dma_start", out=dr["ssmp"][:, :], in_=Sf[0][:]), r=[("Sf", 0)], dma="o")
    S.op("sp", _mk("dma_start", out=dr["convp"][:, :], in_=histb[:].rearrange("p a b -> p (a b)")), r=[("histb", ct) for ct in range(24)], dma="o")
    nops = S.emit(nc, es)
    es.close()
    return nc, nops, dbg_names


def _consts():
    c = np.zeros((NCST, 128, 128), np.float32)
    idx = np.arange(128)
    k = idx[:, None]; i = idx[None, :]
    same = (k // 64) == (i // 64)
    c[K_ID] = np.eye(128)
    c[K_MC_P] = (k <= i)
    c[K_MC_S] = (k <= i) & same
    c[K_LM_P] = (k > i)
    c[K_LM_S] = (k > i) & same
    c[K_ONES] = 1.0
    c[K_SS0] = (k < 64) * np.ones((1, 128))
    c[K_SS1] = (k >= 64) * np.ones((1, 128))
    c[K_GM_P] = (k // 64) <= (i // 64)
    c[K_GM_S] = same
    c[K_CM0] = np.ones((128, 1)) * (i < 64)
    c[K_CM1] = np.ones((128, 1)) * (i >= 64)
    c[K_RM][:, 0] = (idx < 64)
    c[K_RM][:, 1] = (idx >= 64)
    return np.ascontiguousarray(c.transpose(1, 0, 2).reshape(128, NCST * 128))


_CACHE = {}


def kernel(**inp):
    f = lambda a: np.ascontiguousarray(np.asarray(a, dtype=np.float32))
    xpr = f(inp["x_prompt"]); xsm = f(inp["x_sample"])
    cache = f(inp["cache_conv"])[0]; state = f(inp["state_ssm"])[0]
    if "nc" not in _CACHE:
        _CACHE["nc"], _, _ = build_program()
    nc = _CACHE["nc"]
    shared = {
        "w_in": f(inp["w_in"])[0], "wa": f(inp["w_branch_a"])[0], "wb": f(inp["w_branch_b"])[0],
        "wo": f(inp["w_out"])[0], "wup": f(inp["w_up"])[0], "wdn": f(inp["w_down"])[0],
        "pre_mix_w": f(inp["pre_mix_w"]), "ln_w": f(inp["gmlp_ln_w"]), "ln_b": f(inp["gmlp_ln_b"]),
        "post_mix_w": f(inp["post_mix_w"]), "pre_ffn_w": f(inp["pre_ffn_w"]), "post_ffn_w": f(inp["post_ffn_w"]),
        "nw_l": np.ascontiguousarray(f(inp["ssm_norm_w"]).reshape(16, 128).T), "a_log": f(inp["a_log"]), "dt_bias": f(inp["dt_bias"]), "d_skip": f(inp["d_skip"]),
        "cst": _consts(),
    }
    ws = f(inp["gmlp_ws"])[0]
    bs = f(inp["gmlp_bs"])[0]
    shared["wsT_p"] = np.ascontiguousarray(ws.transpose(2, 0, 1).reshape(128, 1024))
    ws_s = np.tile(ws[:, :64, :64], (1, 2, 2))
    shared["wsT_s"] = np.ascontiguousarray(ws_s.transpose(2, 0, 1).reshape(128, 1024))
    shared["bs_p"] = np.ascontiguousarray(bs.reshape(1, 1024))
    shared["bs_s"] = np.ascontiguousarray(np.tile(bs[:, :64], (1, 2)).reshape(1, 1024))
    cw = f(inp["conv_w"])[0]
    shared["convw_l"] = np.ascontiguousarray(cw.reshape(4, 24, 128).transpose(2, 1, 0).reshape(128, 96))
    shared["convb_l"] = np.ascontiguousarray(f(inp["conv_b"])[0].reshape(24, 128).T)
    in_maps = []
    for c in range(NCORE):
        b, q = c // 4, c % 4
        m = dict(shared)
        m["xp"] = np.ascontiguousarray(xpr[b, q * PT:(q + 1) * PT])
        xh = np.zeros((4, D), np.float32)
        if q > 0:
            xh[0:3] = xpr[b, q * PT - 3:q * PT]
        m["xh"] = xh
        m["xsm"] = np.ascontiguousarray(xsm[2 * c:2 * c + 2].reshape(128, D))
        cc = cache[2 * c:2 * c + 2]
        m["cachel"] = np.ascontiguousarray(cc.reshape(2, 3, 24, 128).transpose(3, 0, 2, 1).reshape(128, 144))
        st = state[2 * c:2 * c + 2]
        m["stateT"] = np.ascontiguousarray(st.reshape(2, 2048, 128).transpose(0, 2, 1))
        xprev = np.zeros((3 * PT, D), np.float32)
        if q > 0:
            xprev[(3 - q) * PT:] = xpr[b, 0:q * PT]
        m["xprev"] = xprev
        pf = np.zeros((1, 3 * NSUP), np.float32)
        pf[0, (3 - q) * NSUP:] = 1.0
        m["pflag"] = pf
        in_maps.append(m)
    res = run_bass_kernel_spmd(nc, in_maps, core_ids=list(range(NCORE)))
    R = res.results
    yp = np.stack([np.concatenate([R[b * 4 + q]["yp"] for q in range(4)], axis=0) for b in range(2)])
    ys = np.concatenate([R[c]["ys"].reshape(2, 64, D) for c in range(NCORE)], axis=0)
    def conv_back(a, n):
        return a.reshape(128, n, 24, 3).transpose(1, 3, 2, 0).reshape(n, 3, 3072)
    def ssm_back(a):
        return a.T.reshape(32, 64, 128)
    convp = np.stack([conv_back(R[b * 4 + 3]["convp"], 1)[0] for b in range(2)])[None]
    ssmp = np.stack([ssm_back(R[b * 4 + 3]["ssmp"]) for b in range(2)])[None]
    convs = np.concatenate([conv_back(R[c]["convs"], 2) for c in range(NCORE)], axis=0)[None]
    ssms = np.stack([ssm_back(R[c]["ssms"][s]) for c in range(NCORE) for s in range(2)])[None]
    vs = np.concatenate([R[c]["vs"].reshape(2, 64, D) for c in range(NCORE)], axis=0)[None]
    out = (yp, ys, convp, ssmp, convs, ssms, vs)
    return tuple(np.ascontiguousarray(o, dtype=np.float32) for o in out)
```

```python
import numpy as np
from contextlib import ExitStack
import concourse.bass as bass
import concourse.mybir as mybir
from concourse.bass_utils import run_bass_kernel_spmd

F32 = mybir.dt.float32
BF16 = mybir.dt.bfloat16
AF = mybir.ActivationFunctionType
ALU = mybir.AluOpType

NCORE = 8
D = 1024
PT = 2048
NTP = 2
NSUP = PT // (128 * NTP)
EPS = 1e-6
NSLOT = 4
C_U, C_V, C_Z, C_XBC, C_DT, C_GA, C_GB = 0, 1024, 2048, 4096, 7168, 7200, 8224
(K_ID, K_MC_P, K_MC_S, K_LM_P, K_LM_S, K_ONES, K_SS0, K_SS1, K_GM_P, K_GM_S, K_CM0, K_CM1,
 K_RM) = range(13)
NCST = 13


def _merge(a, b):
    out = []; ia = ib = 0
    while ia < len(a) or ib < len(b):
        if ib >= len(b) or (ia < len(a) and ia * len(b) <= ib * len(a)):
            out.append(a[ia]); ia += 1
        else:
            out.append(b[ib]); ib += 1
    return out


def _mk(name, *a, **kw):
    return lambda e: getattr(e, name)(*a, **kw)


class Sched:
    def __init__(self):
        self.ops = []

    def op(self, eng, fn, r=(), w=(), dma=None):
        o = dict(eng=eng, fn=fn, r=tuple(r), w=tuple(w), dma=dma)
        self.ops.append(o)
        return o

    def placeholder(self):
        o = dict(eng=None, fn=None, r=(), w=(), dma=None)
        self.ops.append(o)
        return o

    def emit(self, nc, es, final_eng="sp"):
        ops = [o for o in self.ops if o["eng"] is not None]
        lastw, readers = {}, {}
        for i, o in enumerate(ops):
            o["id"] = i
            deps = {}
            for k in o["r"]:
                p = lastw.get(k)
                if p is not None:
                    deps[p] = "raw"
            for k in o["w"]:
                p = lastw.get(k)
                if p is not None:
                    deps.setdefault(p, "waw")
                for q in readers.get(k, ()):
                    deps.setdefault(q, "war")
            keep = {}
            for p, t in deps.items():
                po = ops[p]
                if po["dma"] is None and o["dma"] is None and po["eng"] == o["eng"]:
                    if o["eng"] == "pe" or t != "raw":
                        continue
                keep[p] = t
            o["deps"] = keep
            for k in o["r"]:
                readers.setdefault(k, set()).add(i)
            for k in o["w"]:
                lastw[k] = i
                readers[k] = set()
        need = set()
        for o in ops:
            need.update(o["deps"].keys())
        cnt = {}
        for o in ops:
            if o["dma"] is not None:
                s = "d_" + o["dma"]
                cnt[s] = cnt.get(s, 0) + 16
                o["sig"] = (s, cnt[s], 16)
            elif o["id"] in need:
                s = "e_" + o["eng"]
                cnt[s] = cnt.get(s, 0) + 1
                o["sig"] = (s, cnt[s], 1)
            else:
                o["sig"] = None
        sems = {name: es.enter_context(nc.semaphore(name)) for name in sorted(cnt)}
        block = es.enter_context(nc.Block())

        def runner(engname):
            def f(e):
                waited = {}
                for o in ops:
                    if o["eng"] != engname:
                        continue
                    req = {}
                    for p in o["deps"]:
                        s, v, _ = ops[p]["sig"]
                        if v > req.get(s, 0):
                            req[s] = v
                    for s, v in req.items():
                        if waited.get(s, 0) < v:
                            e.wait_ge(sems[s], v)
                            waited[s] = v
                    ins = o["fn"](e)
                    if o["sig"] is not None:
                        ins.then_inc(sems[o["sig"][0]], o["sig"][2])
                if engname == final_eng:
                    for s, v in cnt.items():
                        if s.startswith("d_") and waited.get(s, 0) < v:
                            e.wait_ge(sems[s], v)
            return f

        block.tensor(runner("pe"))
        block.scalar(runner("act"))
        block.vector(runner("dve"))
        block.gpsimd(runner("pool"))
        block.sync(runner("sp"))
        return len(ops)


def build_program(nsup=NSUP, debug=False, npre=3 * NSUP):
    nc = bass.Bass("TRN2", target_bir_lowering=False)
    S = Sched()
    es = ExitStack()
    dr = {}

    def din(name, shape):
        dr[name] = nc.dram_tensor(name, list(shape), F32, kind="ExternalInput").ap()

    def dout(name, shape):
        dr[name] = nc.dram_tensor(name, list(shape), F32, kind="ExternalOutput").ap()

    din("xp", [PT, D]); din("xh", [4, D]); din("xsm", [128, D])
    din("cachel", [128, 2 * 24 * 3]); din("stateT", [2, 128, 2048])
    din("w_in", [D, 9248]); din("wa", [D, D]); din("wb", [2048, D]); din("wo", [D, D])
    din("wup", [D, 4096]); din("wdn", [4096, D])
    for nm in ("pre_mix_w", "ln_w", "ln_b", "post_mix_w", "pre_ffn_w", "post_ffn_w", "bs_p", "bs_s"):
        din(nm, [1, D])
    din("nw_l", [128, 16])
    for nm in ("a_log", "dt_bias", "d_skip"):
        din(nm, [1, 32])
    din("convw_l", [128, 96]); din("convb_l", [128, 24])
    din("xprev", [3 * PT, D]); din("pflag", [1, 3 * NSUP])
    din("cst", [128, NCST * 128]); din("wsT_p", [128, 1024]); din("wsT_s", [128, 1024])
    dout("yp", [PT, D]); dout("ys", [128, D]); dout("convp", [128, 72]); dout("ssmp", [128, 2048])
    dout("convs", [128, 144]); dout("ssms", [2, 128, 2048]); dout("vs", [128, D])

    def sb(name, shape, dt=F32):
        return es.enter_context(nc.sbuf_tensor("s_" + name, list(shape), dt))

    TTM = 128 * NTP
    dbg_names = []

    def dbg(name, ap, keys, shape):
        if not debug:
            return
        d = nc.dram_tensor("dbg_" + name, [int(v) for v in ap.shape], F32, kind="ExternalOutput").ap()
        dbg_names.append("dbg_" + name)
        S.op("pool", _mk("dma_start", out=d, in_=ap), r=keys, dma="dbg")
    cst = sb("cst", [128, NCST, 128]); cstb = sb("cstb", [128, NCST, 128], BF16)
    bc = {nm: sb("bc_" + nm, [128, D]) for nm in ("pre_mix_w", "ln_w", "ln_b", "post_mix_w", "pre_ffn_w", "post_ffn_w")}
    bsb1 = sb("bsb", [128, 8, 128]); bsb = {"p": bsb1, "s": bsb1}
    nwc = sb("nwc", [128, 16])
    alog = sb("alog", [128, 32]); abc = sb("abc", [128, 32]); dtb = sb("dtb", [128, 32]); dsk = sb("dsk", [128, 32])
    convw = sb("convw", [128, 24, 4]); convb = sb("convb", [128, 24])
    neghalf = sb("neghalf", [128, 2])
    wsT = {"p": sb("wsT_p", [128, 8, 128], BF16), "s": sb("wsT_s", [128, 8, 128], BF16)}
    wdt = sb("wdt", [128, 8, 32], BF16)
    cachet = sb("cachet", [128, 2, 24, 3]); cso = sb("cso", [128, 2, 24, 3])
    histb = sb("histb", [128, 24, 3])
    pflagb = sb("pflagb", [128, 3 * NSUP])
    Sf0 = sb("Sf0", [128, 2048])
    Sb = [sb("Sb0", [128, 2048], BF16), sb("Sb1", [128, 2048], BF16)]
    slots = [sb("wslot%d" % i, [128, 4096], BF16) for i in range(NSLOT)]
    xres = sb("xres", [128, NTP, D])
    hT = sb("hT", [128, 8, TTM + 4], BF16)
    uT = sb("uT", [128, 8, TTM], BF16)
    vnb = sb("vnb", [128, NTP, D], BF16)
    oaT = sb("oaT", [128, 8, TTM], BF16)
    PA = sb("PA", [128, 8, TTM], BF16)
    xs_ = sb("xs", [128, NTP, 2048])
    Sf = [Sf0[:, :], xs_[:, 1, :]]
    Btok = sb("Btok", [128, NTP, 512], BF16)
    BT = sb("BT", [128, 4, TTM], BF16); CT = sb("CT", [128, 4, TTM], BF16)
    obT = sb("obT", [128, 16, TTM], BF16)
    mT = uT
    hidT_full = xs_[:].rearrange("p a b -> p (a b)").bitcast(BF16).rearrange("p (k c) -> p k c", k=32)
    hidT_small = xs_[:, 0, :].bitcast(BF16).rearrange("p (k c) -> p k c", k=32)
    hnbs = [sb("hnb", [128, D], BF16), sb("hnb2", [128, D], BF16)]
    vg = sb("vg", [128, D]); vn32 = sb("vn32", [128, D])
    wsraw = vg[:].rearrange("p (g c) -> p g c", g=8)
    junk = vn32
    stat = sb("stat", [128, 2, 6]); mv = sb("mv", [128, 2])
    sc = sb("sc", [128, 16])
    stage4 = sb("stage", [128, 4, TTM + 8]); cacc4 = sb("cacc", [128, 4, TTM])
    xcT = sb("xcT", [128, 4, TTM])
    dts = sb("dts", [128, NTP, 8, 32])
    dhi = sb("dhi", [128, NTP, 32], BF16); dlo = sb("dlo", [128, NTP, 32], BF16)
    decS = sb("decS", [128, NTP, 64])
    xdt = sb("xdt", [128, 512], BF16); xw = sb("xw", [128, 512], BF16)
    Bm = sb("Bm", [128, 2, 512], BF16); CTm = sb("CTm", [128, 2, 4, 128], BF16)
    cbm = sb("cbm", [128, 128])
    Rhi = sb("Rhi", [128, 8, 128], BF16); Rlo = sb("Rlo", [128, 8, 128], BF16)
    dec = sb("dec", [128, 8, 128]); Mb = sb("Mb", [128, 8, 128], BF16)
    t1 = sb("t1", [128, 512]); zs = sb("zs", [128, 512]); gnb = sb("gnb", [128, 512], BF16)
    stmp = sb("stmp", [128, 512])
    sg = sb("sg", [128, TTM]); mtmp = sb("mtmp", [128, TTM])
    rls = [sb("rl%d" % i, [128, TTM]) for i in range(3)]; fT = sb("fT", [128, TTM])
    yout = vg
    psb = [es.enter_context(nc.psum_tensor("ps%d" % i, [128, 512], F32)) for i in range(8)]
    pstate = {"n": 0}

    def bank():
        b = pstate["n"] % 8
        pstate["n"] += 1
        return b

    def PK(b):
        return ("ps", b)

    ws = {"n": 0, "ph": [S.placeholder() for _ in range(NSLOT)]}

    wscr = nc.dram_tensor("wscr", [48, 128, 4096], BF16)
    scr = {}

    def wget(srcd, kcn, cols):
        i = ws["n"]; ws["n"] += 1
        slot = i % NSLOT
        view = slots[slot][:, 0:kcn * cols].rearrange("p (k c) -> p k c", k=kcn)
        ph = ws["ph"].pop(0)
        bid = srcd["bid"]
        if bid not in scr:
            scr[bid] = len(scr)
            sv = wscr.ap()[scr[bid], :, 0:kcn * cols]
            ph.update(eng="pool", fn=_mk("dma_start", out=view, in_=srcd["ap"]), r=(), w=(("wslot", slot),), dma="w%d" % slot)
            S.op("sp", _mk("dma_start", out=sv, in_=slots[slot][:, 0:kcn * cols]), r=[("wslot", slot)], w=[("scr", scr[bid])], dma="sv%d" % slot)
        else:
            sv = wscr.ap()[scr[bid], :, 0:kcn * cols]
            ph.update(eng="sp", fn=_mk("dma_start", out=slots[slot][:, 0:kcn * cols], in_=sv), r=(("scr", scr[bid]),), w=(("wslot", slot),), dma="w%d" % slot)
        return view, ("wslot", slot)

    def wdone():
        ws["ph"].append(S.placeholder())

    def wsrc(name, r0, nr, c0, ncol):
        return dict(bid=(name, c0), ap=dr[name][r0:r0 + nr, c0:c0 + ncol].rearrange("(k p) c -> p k c", p=128))

    S.op("sp", _mk("dma_start", out=cst[:].rearrange("p k c -> p (k c)"), in_=dr["cst"][:, :]), w=["cst"], dma="c11")
    S.op("pool", _mk("dma_start", out=cstb[:].rearrange("p k c -> p (k c)"), in_=dr["cst"][:, :]), w=["cstb"], dma="c12")
    for nm in bc:
        S.op("sp", _mk("dma_start", out=bc[nm][:], in_=dr[nm].partition_broadcast(128)), w=["bc_" + nm], dma="c13_" + nm)
    S.op("sp", _mk("dma_start", out=bsb["s"][:].rearrange("p g c -> p (g c)"), in_=dr["bs_s"].partition_broadcast(128)), w=["bsb"], dma="c14")
    S.op("sp", _mk("dma_start", out=nwc[:], in_=dr["nw_l"][:, :]), w=["nwc"], dma="c15")
    S.op("sp", _mk("dma_start", out=alog[:], in_=dr["a_log"].partition_broadcast(128)), w=["alog"], dma="c16")
    S.op("sp", _mk("dma_start", out=dtb[:], in_=dr["dt_bias"].partition_broadcast(128)), w=["dtb"], dma="c17")
    S.op("sp", _mk("dma_start", out=dsk[:], in_=dr["d_skip"].partition_broadcast(128)), w=["dsk"], dma="c18")
    S.op("sp", _mk("dma_start", out=convw[:].rearrange("p a b -> p (a b)"), in_=dr["convw_l"][:, :]), w=["convw"], dma="c19")
    S.op("sp", _mk("dma_start", out=convb[:], in_=dr["convb_l"][:, :]), w=["convb"], dma="c20")
    S.op("sp", _mk("dma_start", out=cachet[:].rearrange("p a b c -> p (a b c)"), in_=dr["cachel"][:, :]), w=["cachet"], dma="c21")
    S.op("pool", _mk("dma_start", out=wdt[:], in_=wsrc("w_in", 0, D, C_DT, 32)["ap"]), w=["wdt"], dma="c22")
    S.op("dve", _mk("memset", neghalf[:], -0.5), w=["neghalf"])
    S.op("act", _mk("activation", out=abc[:], in_=alog[:], func=AF.Exp), r=["alog"], w=["abc"])
    S.op("dve", _mk("tensor_scalar", out=abc[:], in0=abc[:], scalar1=-1.0, scalar2=None, op0=ALU.mult), r=["abc"], w=["abc"])
    for kind, gm in (("p", K_GM_P), ("s", K_GM_S)):
        S.op("sp", _mk("dma_start", out=vg[:], in_=dr["wsT_" + kind][:, :]), w=["vg"], dma="c23")
        S.op("dve", _mk("tensor_tensor",
            out=wsT[kind][:], in0=wsraw, in1=cst[:, gm, :].unsqueeze(1).broadcast_to([128, 8, 128]), op=ALU.mult),
            r=["vg", "cst"], w=["wsT_" + kind])
    for s_ in range(2):
        S.op("sp", _mk("dma_start", out=Sf[s_][:], in_=dr["stateT"][s_, :, :]), w=[("Sf", s_)], dma="c24_%d" % s_)
        S.op("act", _mk("activation", out=Sb[s_][:], in_=Sf[s_][:], func=AF.Copy), r=[("Sf", s_)], w=[("Sb", s_)])

    def rstd_from(ss_ap, n, key):
        S.op("dve", _mk("tensor_scalar", out=ss_ap, in0=ss_ap, scalar1=1.0 / n, scalar2=EPS, op0=ALU.mult, op1=ALU.add), r=[key], w=[key])
        S.op("pool", _mk("tensor_tensor", out=ss_ap, in0=ss_ap, in1=neghalf[:, 0:1], op=ALU.pow), r=[key, "neghalf"], w=[key])

    def mm_group(out_ap, pairs, r, w):
        def fn(e):
            ins = None
            n = len(pairs)
            for i, (l, rr) in enumerate(pairs):
                ins = e.matmul(out_ap, lhsT=l, rhs=rr, start=(i == 0), stop=(i == n - 1))
            return ins
        S.op("pe", fn, r=r, w=w)

    def tr_group(items, r, w):
        def fn(e):
            ins = None
            for (o, i_, idn) in items:
                ins = e.transpose(o, i_, idn)
            return ins
        S.op("pe", fn, r=r, w=w)

    def norm_to_hT(src_ap, srckey, wname, t, ncols_valid=128, par=0):
        hnb = hnbs[par]; hk = ("hnb", par); sck = ("scn", par)
        scc = sc[:, 12 + par:13 + par]
        S.op("act", _mk("activation", out=hnb[:], in_=src_ap, func=AF.Square, accum_out=scc), r=[srckey], w=[hk, sck])
        rstd_from(scc, D, sck)
        S.op("dve", _mk("scalar_tensor_tensor", out=hnb[:], in0=src_ap, scalar=scc, in1=bc[wname][:], op0=ALU.mult, op1=ALU.mult),
             r=[srckey, sck, "bc_" + wname], w=[hk])
        b = bank()
        pb = psb[b][:].bitcast(BF16)
        nv = ncols_valid
        tr_group([(pb[:, kc * 128:kc * 128 + nv], hnb[0:nv, kc * 128:(kc + 1) * 128], cstb[0:nv, K_ID, 0:nv]) for kc in range(8)],
                 r=[hk, "cstb"], w=[PK(b)])
        c0 = t * 128
        S.op("act", _mk("activation", out=hT[:, :, c0:c0 + nv], in_=pb.rearrange("p (k c) -> p k c", k=8)[:, :, 0:nv], func=AF.Copy),
             r=[PK(b)], w=["hT"])

    def supertile(kind, NT, x_src, y_dst, first_prompt, pre=False, flagidx=None):
        TT = 128 * NT
        halo = 4 if first_prompt else 0
        hidT = hidT_full if NT == NTP else hidT_small

        def getw(srcd, kcn, cols):
            if pre:
                return prew[srcd["bid"]]
            return wget(srcd, kcn, cols)

        def donew():
            if not pre:
                wdone()
        NX = TT + halo
        if kind == "p":
            segs = [dict(sel=K_ONES, cm=None, rm=None, st=0)]
            mc, lm = K_MC_P, K_LM_P
            nsegc, L = 1, TT
        else:
            segs = [dict(sel=K_SS0, cm=K_CM0, rm=0, st=0), dict(sel=K_SS1, cm=K_CM1, rm=1, st=1)]
            mc, lm = K_MC_S, K_LM_S
            nsegc, L = 2, 64
        recs = []
        for t in range(NT):
            n0_ = len(S.ops)
            S.op("sp", _mk("dma_start", out=xres[:, t, :], in_=x_src[t * 128:(t + 1) * 128, :]), w=[("xres", t)], dma="x%d" % t)
            norm_to_hT(xres[:, t, :], ("xres", t), "pre_mix_w", t, par=t % 2)
            recs.append(S.ops[n0_:]); del S.ops[n0_:]
        S.ops.extend(_merge(recs[0], recs[1]) if NT == 2 else recs[0])
        if first_prompt:
            S.op("sp", _mk("dma_start", out=vg[0:4, :], in_=dr["xh"][:, :]), w=["vg"], dma="xh")
            norm_to_hT(vg[:, :], "vg", "pre_mix_w", NT, ncols_valid=4)
        for blk in (range(2) if not pre else ()):
            wv, wk = wget(wsrc("w_in", 0, D, C_U + blk * 512, 512), 8, 512)
            for sub in range(4):
                cb = blk * 4 + sub
                b = bank()
                mm_group(psb[b][:, 0:TT], [(wv[:, kc, sub * 128:(sub + 1) * 128], hT[:, kc, 0:TT]) for kc in range(8)],
                         r=[wk, "hT"], w=[PK(b)])
                S.op("act", _mk("activation", out=uT[:, cb, 0:TT], in_=psb[b][:, 0:TT], func=AF.Gelu_apprx_tanh),
                     r=[PK(b)], w=["uT"])
            wdone()
        vblk = [wget(wsrc("w_in", 0, D, C_V + blk * 512, 512), 8, 512) for blk in (range(2) if not pre else ())]
        for t in (range(NT) if not pre else ()):
            for blk in range(2):
                wv, wk = vblk[blk]
                b = bank()
                mm_group(psb[b][:, :], [(hT[:, kc, t * 128:(t + 1) * 128], wv[:, kc, :]) for kc in range(8)], r=[wk, "hT"], w=[PK(b)])
                S.op("act", _mk("activation", out=vg[:, blk * 512:(blk + 1) * 512], in_=psb[b][:, :], func=AF.Gelu_apprx_tanh),
                     r=[PK(b)], w=["vg"])
            for blk in range(2):
                S.op("dve", _mk("bn_stats", out=stat[:, blk, :], in_=vg[:, blk * 512:(blk + 1) * 512]), r=["vg"], w=["stat"])
            S.op("dve", _mk("bn_aggr", out=mv[:], in_=stat[:].rearrange("p a b -> p (a b)")), r=["stat"], w=["mv"])
            rstd_from(mv[:, 1:2], 1.0, "mv")
            S.op("dve", _mk("tensor_scalar", out=vn32[:], in0=vg[:], scalar1=mv[:, 0:1], scalar2=mv[:, 1:2], op0=ALU.subtract, op1=ALU.mult),
                 r=["vg", "mv"], w=["vn32"])
            S.op("dve", _mk("tensor_tensor", out=vn32[:], in0=vn32[:], in1=bc["ln_w"][:], op=ALU.mult), r=["vn32", "bc_ln_w"], w=["vn32"])
            S.op("dve", _mk("tensor_tensor", out=vn32[:], in0=vn32[:], in1=bc["ln_b"][:], op=ALU.add), r=["vn32", "bc_ln_b"], w=["vn32"])
            S.op("act", _mk("activation", out=vnb[:, t, :], in_=vn32[:], func=AF.Copy), r=["vn32"], w=[("vnb", t)])
            if kind == "s":
                S.op("sp", _mk("dma_start", out=dr["vs"][:, :], in_=vn32[:]), r=["vn32"], dma="o")
            for half in range(2):
                b = bank()
                def fn(e, b=b, half=half, t=t):
                    ins = None
                    for gi in range(4):
                        g = half * 4 + gi
                        ins = e.matmul(psb[b][:, gi * 128:(gi + 1) * 128], lhsT=vnb[:, t, g * 128:(g + 1) * 128], rhs=wsT[kind][:, g, :], start=True, stop=True)
                    return ins
                S.op("pe", fn, r=[("vnb", t), "wsT_" + kind], w=[PK(b)])
                S.op("dve", _mk("tensor_tensor",
                    out=stmp[:].rearrange("p (g c) -> p g c", g=4), in0=psb[b][:, :].rearrange("p (g c) -> p g c", g=4),
                    in1=bsb[kind][:, half * 4:half * 4 + 4, :], op=ALU.add), r=[PK(b), "bsb"], w=["stmp"])
                S.op("dve", _mk("tensor_tensor",
                    out=oaT[:, half * 4:half * 4 + 4, t * 128:(t + 1) * 128], in0=stmp[:].rearrange("p (g c) -> p g c", g=4),
                    in1=uT[:, half * 4:half * 4 + 4, t * 128:(t + 1) * 128], op=ALU.mult), r=["stmp", "uT"], w=["oaT"])
        if not pre:
            wdone(); wdone()
        for blk in (range(2) if not pre else ()):
            wa_v, wa_k = wget(wsrc("wa", 0, D, blk * 512, 512), 8, 512)
            ga_v, ga_k = wget(wsrc("w_in", 0, D, C_GA + blk * 512, 512), 8, 512)
            for sub in range(4):
                fb = blk * 4 + sub
                b1 = bank(); b2 = bank()
                mm_group(psb[b1][:, 0:TT], [(ga_v[:, kc, sub * 128:(sub + 1) * 128], hT[:, kc, 0:TT]) for kc in range(8)], r=[ga_k, "hT"], w=[PK(b1)])
                mm_group(psb[b2][:, 0:TT], [(wa_v[:, kc, sub * 128:(sub + 1) * 128], oaT[:, kc, 0:TT]) for kc in range(8)], r=[wa_k, "oaT"], w=[PK(b2)])
                S.op("act", _mk("activation", out=sg[:, 0:TT], in_=psb[b1][:, 0:TT], func=AF.Sigmoid), r=[PK(b1)], w=["sg"])
                S.op("dve", _mk("tensor_tensor", out=PA[:, fb, 0:TT], in0=sg[:, 0:TT], in1=psb[b2][:, 0:TT], op=ALU.mult),
                     r=["sg", PK(b2)], w=["PA"])
            wdone(); wdone()
        recs = []
        for t in range(NT):
            n0_ = len(S.ops)
            b = bank()
            mm_group(psb[b][:, 0:32], [(hT[:, kc, t * 128:(t + 1) * 128], wdt[:, kc, :]) for kc in range(8)], r=["hT", "wdt"], w=[PK(b)])
            dk = ("dts", t)
            A = lambda i, t=t: dts[:, t, i, :]
            S.op("dve", _mk("tensor_tensor", out=dts[:, t, 6, :], in0=psb[b][:, 0:32], in1=dtb[:], op=ALU.add), r=[PK(b), "dtb"], w=[dk])
            S.op("act", _mk("activation", out=dts[:, t, 7, :], in_=dts[:, t, 6, :], func=AF.Abs), r=[dk], w=[dk])
            S.op("act", _mk("activation", out=dts[:, t, 7, :], in_=dts[:, t, 7, :], func=AF.Exp, scale=-1.0), r=[dk], w=[dk])
            S.op("act", _mk("activation", out=dts[:, t, 7, :], in_=dts[:, t, 7, :], func=AF.Ln, bias=1.0), r=[dk], w=[dk])
            S.op("dve", _mk("scalar_tensor_tensor", out=dts[:, t, 0, :], in0=dts[:, t, 6, :], scalar=0.0, in1=dts[:, t, 7, :], op0=ALU.max, op1=ALU.add), r=[dk], w=[dk])
            S.op("dve", _mk("tensor_tensor", out=dts[:, t, 1, :], in0=dts[:, t, 0, :], in1=abc[:], op=ALU.mult), r=[dk, "abc"], w=[dk])
            S.op("dve", _mk("tensor_copy", out=dhi[:, t, :], in_=dts[:, t, 1, :]), r=[dk], w=[("dhi", t)])
            S.op("dve", _mk("tensor_tensor", out=dlo[:, t, :], in0=dts[:, t, 1, :], in1=dhi[:, t, :], op=ALU.subtract), r=[dk, ("dhi", t)], w=[("dlo", t)])
            b2 = bank()
            def fn(e, b2=b2, t=t):
                ins = e.matmul(psb[b2][:, 0:32], lhsT=cst[:, mc, :], rhs=dts[:, t, 1, :], start=True, stop=True)
                for si, sgm in enumerate(segs):
                    ins = e.matmul(psb[b2][:, 32 + 32 * si:64 + 32 * si], lhsT=cst[:, sgm["sel"], :], rhs=dts[:, t, 1, :], start=True, stop=True)
                return ins
            S.op("pe", fn, r=[dk, "cst"], w=[PK(b2)])
            S.op("act", _mk("activation", out=dts[:, t, 2, :], in_=psb[b2][:, 0:32], func=AF.Copy), r=[PK(b2)], w=[dk])
            S.op("act", _mk("activation", out=dts[:, t, 3, :], in_=psb[b2][:, 0:32], func=AF.Exp), r=[PK(b2)], w=[dk])
            S.op("act", _mk("activation", out=decS[:, t, 0:32 * len(segs)], in_=psb[b2][:, 32:32 + 32 * len(segs)], func=AF.Exp), r=[PK(b2)], w=[("decS", t)])
            if kind == "p":
                S.op("dve", _mk("tensor_tensor", out=dts[:, t, 4, :], in0=psb[b2][:, 32:64], in1=dts[:, t, 2, :], op=ALU.subtract), r=[PK(b2), dk], w=[dk])
            else:
                for si in range(2):
                    S.op("dve", _mk("tensor_tensor",
                        out=dts[64 * si:64 * si + 64, t, 4, :], in0=psb[b2][64 * si:64 * si + 64, 32 + 32 * si:64 + 32 * si],
                        in1=dts[64 * si:64 * si + 64, t, 2, :], op=ALU.subtract), r=[PK(b2), dk], w=[dk])
            S.op("act", _mk("activation", out=dts[:, t, 4, :], in_=dts[:, t, 4, :], func=AF.Exp), r=[dk], w=[dk])
            S.op("dve", _mk("tensor_tensor", out=dts[:, t, 5, :], in0=dts[:, t, 4, :], in1=dts[:, t, 0, :], op=ALU.mult), r=[dk], w=[dk])
            if pre:
                S.op("dve", _mk("tensor_scalar", out=dts[:, t, 5, :], in0=dts[:, t, 5, :], scalar1=pflagb[:, flagidx:flagidx + 1], scalar2=None, op0=ALU.mult),
                     r=[dk, "pflagb"], w=[dk])
            recs.append(S.ops[n0_:]); del S.ops[n0_:]
        S.ops.extend(_merge(recs[0], recs[1]) if NT == 2 else recs[0])

        if kind == "s":
            dbg("dts", dts[:, 0, :, :].rearrange("p a b -> p (a b)"), [("dts", 0)], [128, 256])
            dbg("decS", decS[:, 0, :], [("decS", 0)], [128, 64])
            dbg("hT", hT[:, :, 0:128], ["hT"], [128, 1024])
        alt_stage = [(t1, "t1"), (zs, "zs"), (stmp, "stmp"), (vg, "vg")]
        alt_cacc = [(sg, "sg"), (mtmp, "mtmp"), (rls[0], ("rl", 0)), (rls[1], ("rl", 1))]

        def conv_block(cts, wv, wk, outs, alt=False):
            bks = [bank() for _ in cts]
            for i, ct in enumerate(cts):
                mm_group(psb[bks[i]][:, 0:NX], [(wv[:, kc, i * 128:(i + 1) * 128], hT[:, kc, 0:NX]) for kc in range(8)], r=[wk, "hT"], w=[PK(bks[i])])
            st3s = []
            sks = []
            for i, ct in enumerate(cts):
                b = bks[i]
                stage = alt_stage[i][0][:, 0:TTM + 8] if alt else stage4[:, i, :]
                sk = alt_stage[i][1] if alt else ("stage", i)
                sks.append(sk)
                st3 = stage[:, 0:nsegc * (L + 3)].rearrange("p (s c) -> p s c", s=nsegc)
                st3s.append(st3)
                if first_prompt:
                    S.op("act", _mk("activation", out=stage[:, 0:3], in_=psb[b][:, TT:TT + 3], func=AF.Copy), r=[PK(b)], w=[sk])
                elif kind == "p":
                    S.op("pool", _mk("tensor_copy", out=stage[:, 0:3], in_=histb[:, ct, :]), r=[("histb", ct)], w=[sk])
                else:
                    S.op("pool", _mk("tensor_copy", out=st3[:, :, 0:3], in_=cachet[:, :, ct, :]), r=["cachet"], w=[sk])
                S.op("act", _mk("activation", out=st3[:, :, 3:3 + L], in_=psb[b][:, 0:TT].rearrange("p (s c) -> p s c", s=nsegc), func=AF.Copy),
                     r=[PK(b)], w=[sk])
                if kind == "p":
                    S.op("pool", _mk("tensor_copy", out=histb[:, ct, :], in_=stage[:, L:L + 3]), r=[sk], w=[("histb", ct)])
                else:
                    S.op("pool", _mk("tensor_copy", out=cso[:, :, ct, :], in_=st3[:, :, L:L + 3]), r=[sk], w=["cso"])
            caps = [(alt_cacc[i][0][:, 0:TT] if alt else cacc4[:, i, 0:TT]) for i in range(len(cts))]
            cks = [(alt_cacc[i][1] if alt else ("cacc", i)) for i in range(len(cts))]
            ca3s = [c_.rearrange("p (s c) -> p s c", s=nsegc) for c_ in caps]
            for i, ct in enumerate(cts):
                S.op("dve", _mk("tensor_scalar", out=ca3s[i], in0=st3s[i][:, :, 0:L], scalar1=convw[:, ct, 0:1], scalar2=convb[:, ct:ct + 1], op0=ALU.mult, op1=ALU.add),
                     r=[sks[i], "convw", "convb"], w=[cks[i]])
            for k in range(1, 4):
                for i, ct in enumerate(cts):
                    S.op("dve", _mk("scalar_tensor_tensor", out=ca3s[i], in0=st3s[i][:, :, k:k + L], scalar=convw[:, ct, k:k + 1], in1=ca3s[i], op0=ALU.mult, op1=ALU.add),
                         r=[sks[i], cks[i], "convw"], w=[cks[i]])
            for i, ct in enumerate(cts):
                S.op("act", _mk("activation", out=outs[i][0], in_=caps[i], func=AF.Silu), r=[cks[i]], w=[outs[i][1]])

        if pre:
            xcT_alt = dec[:].rearrange("p a b -> p (a b)").rearrange("p (s c) -> p s c", s=4)
            blocks = [("B", None)] + [("x", g) for g in range(4)]

            def conv_of(k):
                kind_, g = blocks[k]
                alt = (k % 2 == 1)
                if kind_ == "B":
                    wv, wk = getw(wsrc("w_in", 0, D, C_XBC + 2048, 512), 8, 512)
                    conv_block([16 + gg for gg in range(4)], wv, wk, [(BT[:, gg, 0:TT], "BT") for gg in range(4)], alt=alt)
                else:
                    wv, wk = getw(wsrc("w_in", 0, D, C_XBC + g * 512, 512), 8, 512)
                    xo, xk = (xcT_alt, "dec") if alt else (xcT, "xcT")
                    conv_block([g * 4 + sub for sub in range(4)], wv, wk, [(xo[:, sub, 0:TT], xk) for sub in range(4)], alt=alt)

            def post_of(k):
                kind_, g = blocks[k]
                alt = (k % 2 == 1)
                if kind_ == "B":
                    for t in range(NT):
                        b = bank()
                        pb = psb[b][:].bitcast(BF16)
                        tr_group([(pb[:, gg * 128:(gg + 1) * 128], BT[:, gg, t * 128:(t + 1) * 128], cstb[:, K_ID, :]) for gg in range(4)], r=["BT", "cstb"], w=[PK(b)])
                        S.op("act", _mk("activation", out=Btok[:, t, :], in_=pb[:, 0:512], func=AF.Copy), r=[PK(b)], w=[("Btok", t)])
                    return
                xo, xk = (xcT_alt, "dec") if alt else (xcT, "xcT")
                xwb, xwk = (xdt, "xdt") if alt else (xw, "xw")
                gc = slice(g * 512, (g + 1) * 512)
                for t in range(NT):
                    dk = ("dts", t)
                    b = bank()
                    tr_group([(psb[b][:, sub * 128:(sub + 1) * 128], xo[:, sub, t * 128:(t + 1) * 128], cst[:, K_ID, :]) for sub in range(4)],
                             r=[xk, "cst"], w=[PK(b)])
                    S.op("dve", _mk("tensor_tensor", out=xwb[:].rearrange("p (h c) -> p h c", h=8), in0=psb[b][:, :].rearrange("p (h c) -> p h c", h=8),
                         in1=dts[:, t, 5, g * 8:(g + 1) * 8].unsqueeze(2).broadcast_to([128, 8, 64]), op=ALU.mult), r=[PK(b), dk], w=[xwk])
                    bst = bank()
                    mm_group(psb[bst][:, :], [(Btok[:, t, g * 128:(g + 1) * 128], xwb[:, :])], r=[("Btok", t), xwk], w=[PK(bst)])
                    S.op("dve", _mk("tensor_tensor",
                        out=Sf[0][:, gc].rearrange("p (h c) -> p h c", h=8), in0=Sf[0][:, gc].rearrange("p (h c) -> p h c", h=8),
                        in1=decS[:, t, g * 8:g * 8 + 8].unsqueeze(2).broadcast_to([128, 8, 64]), op=ALU.mult),
                        r=[("Sfg", g), ("decS", t)], w=[("Sfg", g)])
                    S.op("dve", _mk("tensor_tensor", out=Sf[0][:, gc], in0=Sf[0][:, gc], in1=psb[bst][:, :], op=ALU.add),
                         r=[("Sfg", g), PK(bst)], w=[("Sfg", g)])

            conv_of(0)
            for k in range(len(blocks)):
                if k + 1 < len(blocks):
                    conv_of(k + 1)
                post_of(k)
            return
        wv, wk = getw(wsrc("w_in", 0, D, C_XBC + 2048, 512), 8, 512)
        conv_block([16 + g for g in range(4)], wv, wk, [(BT[:, g, 0:TT], "BT") for g in range(4)])
        donew()
        for t in range(NT):
            b = bank()
            pb = psb[b][:].bitcast(BF16)
            tr_group([(pb[:, g * 128:(g + 1) * 128], BT[:, g, t * 128:(t + 1) * 128], cstb[:, K_ID, :]) for g in range(4)], r=["BT", "cstb"], w=[PK(b)])
            S.op("act", _mk("activation", out=Btok[:, t, :], in_=pb[:, 0:512], func=AF.Copy), r=[PK(b)], w=[("Btok", t)])
        if not pre:
            wv, wk = getw(wsrc("w_in", 0, D, C_XBC + 2560, 512), 8, 512)
            conv_block([20 + g for g in range(4)], wv, wk, [(CT[:, g, 0:TT], "CT") for g in range(4)])
            donew()
        for g in range(4):
            wv, wk = getw(wsrc("w_in", 0, D, C_XBC + g * 512, 512), 8, 512)
            conv_block([g * 4 + sub for sub in range(4)], wv, wk, [(xcT[:, sub, 0:TT], "xcT") for sub in range(4)])
            donew()
            for t in range(NT):
                b = bank()
                tr_group([(psb[b][:, sub * 128:(sub + 1) * 128], xcT[:, sub, t * 128:(t + 1) * 128], cst[:, K_ID, :]) for sub in range(4)],
                         r=["xcT", "cst"], w=[PK(b)])
                S.op("act", _mk("activation", out=xs_[:, t, g * 512:(g + 1) * 512], in_=psb[b][:, :], func=AF.Copy), r=[PK(b)], w=[("xs", t), "hidT"])
        def merge_emit(a, b):
            out_ = []; ia = ib = 0
            while ia < len(a) or ib < len(b):
                if ib >= len(b) or (ia < len(a) and ia * len(b) <= ib * len(a)):
                    out_.append(a[ia]); ia += 1
                else:
                    out_.append(b[ib]); ib += 1
            S.ops.extend(out_)

        pend = None
        for g in range(4):
            zv, zk = wget(wsrc("w_in", 0, D, C_Z + g * 512, 512), 8, 512)
            for t in range(NT):
                n0_ = len(S.ops)
                dk = ("dts", t)
                tc_ = slice(t * 128, (t + 1) * 128)
                gc = slice(g * 512, (g + 1) * 512)
                xs3 = xs_[:, t, g * 512:(g + 1) * 512].rearrange("p (h c) -> p h c", h=8)
                if not pre:
                  S.op("dve", _mk("tensor_tensor", out=xdt[:].rearrange("p (h c) -> p h c", h=8), in0=xs3,
                     in1=dts[:, t, 0, g * 8:(g + 1) * 8].unsqueeze(2).broadcast_to([128, 8, 64]), op=ALU.mult), r=[("xs", t), dk], w=["xdt"])
                S.op("dve", _mk("tensor_tensor", out=xw[:].rearrange("p (h c) -> p h c", h=8), in0=xs3,
                     in1=dts[:, t, 5, g * 8:(g + 1) * 8].unsqueeze(2).broadcast_to([128, 8, 64]), op=ALU.mult), r=[("xs", t), dk], w=["xw"])
                if first_prompt and not pre and g == 0 and t == 0:
                    dbg("pxs0", xs_[:, 0, 0:512], [("xs", 0)], None)
                    dbg("pBT", BT[:, 0, 0:128], ["BT"], None); dbg("pCT", CT[:, 0, 0:128], ["CT"], None)
                    dbg("pSb", Sb[0][:, 0:512], [("Sb", 0)], None)
                    dbg("phT", hT[:, :, 252:260], ["hT"], None)
                if kind == "s" and g == 0:
                    dbg("xs0", xs_[:, 0, 0:512], [("xs", 0)], [128, 512])
                    dbg("xdt", xdt[:], ["xdt"], [128, 512])
                    dbg("xw", xw[:], ["xw"], [128, 512])
                    dbg("BT", BT[:, 0, 0:128], ["BT"], [128, 128])
                    dbg("CT", CT[:, 0, 0:128], ["CT"], [128, 128])
                    dbg("Btok", Btok[:, 0, 0:128], [("Btok", 0)], [128, 128])
                if not pre:
                    bcb = bank()
                    mm_group(psb[bcb][:, 0:128], [(BT[:, g, tc_], CT[:, g, tc_])], r=["BT", "CT"], w=[PK(bcb)])
                    S.op("dve", _mk("tensor_tensor", out=cbm[:], in0=psb[bcb][:, 0:128], in1=cst[:, mc, :], op=ALU.mult), r=[PK(bcb), "cst"], w=["cbm"])
                    S.op("pool", _mk("tensor_tensor", out=Rhi[:], in0=cstb[:, mc, :].unsqueeze(1).broadcast_to([128, 8, 128]),
                         in1=dhi[:, t, g * 8:(g + 1) * 8].unsqueeze(2).broadcast_to([128, 8, 128]), op=ALU.mult), r=["cstb", ("dhi", t)], w=["Rhi"])
                    S.op("pool", _mk("tensor_tensor", out=Rlo[:], in0=cstb[:, mc, :].unsqueeze(1).broadcast_to([128, 8, 128]),
                         in1=dlo[:, t, g * 8:(g + 1) * 8].unsqueeze(2).broadcast_to([128, 8, 128]), op=ALU.mult), r=["cstb", ("dlo", t)], w=["Rlo"])
                    bs0 = bank(); bs1 = bank()
                    for hh, bb in ((0, bs0), (1, bs1)):
                        mm_group(psb[bb][:, :], [(cstb[:, lm, :], Rhi[:, hh * 4:hh * 4 + 4, :].rearrange("p h c -> p (h c)")),
                                                (cstb[:, lm, :], Rlo[:, hh * 4:hh * 4 + 4, :].rearrange("p h c -> p (h c)"))],
                                 r=["cstb", "Rhi", "Rlo"], w=[PK(bb)])
                        S.op("act", _mk("activation", out=dec[:, hh * 4:hh * 4 + 4, :].rearrange("p h c -> p (h c)"), in_=psb[bb][:, :], func=AF.Exp),
                             r=[PK(bb)], w=["dec"])
                    S.op("dve", _mk("tensor_tensor", out=Mb[:], in0=dec[:], in1=cbm[:].unsqueeze(1).broadcast_to([128, 8, 128]), op=ALU.mult),
                         r=["dec", "cbm"], w=["Mb"])
                    if kind == "s" and g == 0:
                        dbg("cbm", cbm[:], ["cbm"], [128, 128])
                        dbg("dec", dec[:].rearrange("p a b -> p (a b)"), ["dec"], [128, 1024])
                        dbg("Mb", Mb[:].rearrange("p a b -> p (a b)"), ["Mb"], [128, 1024])
                    byd = bank()
                    def fn(e, byd=byd):
                        ins = None
                        for h in range(8):
                            ins = e.matmul(psb[byd][:, h * 64:(h + 1) * 64], lhsT=Mb[:, h, :], rhs=xdt[:, h * 64:(h + 1) * 64], start=True, stop=True)
                        return ins
                    S.op("pe", fn, r=["Mb", "xdt"], w=[PK(byd)])
                    byo = bank()
                    gc = slice(g * 512, (g + 1) * 512)
                    if kind == "p":
                        mm_group(psb[byo][:, :], [(CT[:, g, tc_], Sb[0][:, gc])], r=["CT", ("Sb", 0)], w=[PK(byo)])
                    else:
                        for si in range(2):
                            S.op("dve", _mk("tensor_tensor", out=CTm[:, si, g, :], in0=CT[:, g, tc_], in1=cstb[:, K_CM0 + si, :], op=ALU.mult),
                                 r=["CT", "cstb"], w=["CTm"])
                        mm_group(psb[byo][:, :], [(CTm[:, 0, g, :], Sb[0][:, gc]), (CTm[:, 1, g, :], Sb[1][:, gc])], r=["CTm", ("Sb", 0), ("Sb", 1)], w=[PK(byo)])
                for si, sgm in enumerate(segs):
                    bst = bank()
                    sidx = sgm["st"]
                    if kind == "p":
                        lhs = Btok[:, t, g * 128:(g + 1) * 128]
                        rk = [("Btok", t)]
                    else:
                        S.op("dve", _mk("tensor_scalar", out=Bm[:, si, g * 128:(g + 1) * 128], in0=Btok[:, t, g * 128:(g + 1) * 128],
                             scalar1=cst[:, K_RM, si:si + 1], scalar2=None, op0=ALU.mult), r=[("Btok", t), "cst"], w=["Bm"])
                        lhs = Bm[:, si, g * 128:(g + 1) * 128]
                        rk = ["Bm"]
                    mm_group(psb[bst][:, :], [(lhs, xw[:, :])], r=rk + ["xw"], w=[PK(bst)])
                    S.op("dve", _mk("tensor_tensor",
                        out=Sf[sidx][:, gc].rearrange("p (h c) -> p h c", h=8), in0=Sf[sidx][:, gc].rearrange("p (h c) -> p h c", h=8),
                        in1=decS[:, t, 32 * si + g * 8:32 * si + g * 8 + 8].unsqueeze(2).broadcast_to([128, 8, 64]), op=ALU.mult),
                        r=[("Sf", sidx), ("decS", t)], w=[("Sf", sidx)])
                    S.op("dve", _mk("tensor_tensor", out=Sf[sidx][:, gc], in0=Sf[sidx][:, gc], in1=psb[bst][:, :], op=ALU.add),
                         r=[("Sf", sidx), PK(bst)], w=[("Sf", sidx)])
                    if not pre:
                        S.op("act", _mk("activation", out=Sb[sidx][:, gc], in_=Sf[sidx][:, gc], func=AF.Copy), r=[("Sf", sidx)], w=[("Sb", sidx)])
                opsA = S.ops[n0_:]
                del S.ops[n0_:]
                if pend is not None:
                    merge_emit(opsA, pend["ops"])
                    if pend["last"]:
                        wdone()
                else:
                    S.ops.extend(opsA)
                n1_ = len(S.ops)
                if not pre:
                    S.op("dve", _mk("tensor_tensor", out=t1[:].rearrange("p (h c) -> p h c", h=8), in0=psb[byo][:, :].rearrange("p (h c) -> p h c", h=8),
                         in1=dts[:, t, 3, g * 8:(g + 1) * 8].unsqueeze(2).broadcast_to([128, 8, 64]), op=ALU.mult), r=[PK(byo), dk], w=["t1"])
                    S.op("dve", _mk("tensor_tensor", out=t1[:], in0=t1[:], in1=psb[byd][:, :], op=ALU.add), r=["t1", PK(byd)], w=["t1"])
                    S.op("dve", _mk("tensor_tensor", out=xs3, in0=xs3, in1=dsk[:, g * 8:(g + 1) * 8].unsqueeze(2).broadcast_to([128, 8, 64]), op=ALU.mult),
                         r=[("xs", t), "dsk"], w=[("xs", t)])
                    S.op("dve", _mk("tensor_tensor", out=t1[:], in0=t1[:], in1=xs_[:, t, gc], op=ALU.add), r=["t1", ("xs", t)], w=["t1"])
                    if first_prompt and g == 0 and t == 0:
                        dbg("py", t1[:], ["t1"], None)
                    if kind == "s" and g == 0:
                        dbg("y", t1[:], ["t1"], [128, 512])
                        dbg("Sf0", Sf[0][:, 0:512], [("Sf", 0)], [128, 512])
                    bz = bank()
                    mm_group(psb[bz][:, :], [(hT[:, kc, tc_], zv[:, kc, :]) for kc in range(8)], r=["hT", zk], w=[PK(bz)])
                    S.op("act", _mk("activation", out=zs[:], in_=psb[bz][:, :], func=AF.Silu), r=[PK(bz)], w=["zs"])
                    S.op("dve", _mk("tensor_tensor", out=t1[:], in0=t1[:], in1=zs[:], op=ALU.mult), r=["t1", "zs"], w=["t1"])
                    S.op("act", _mk("activation", out=zs[:], in_=t1[:], func=AF.Square, accum_out=sc[:, 1:2]), r=["t1"], w=["zs", "sc1"])
                    rstd_from(sc[:, 1:2], 512.0, "sc1")
                    S.op("dve", _mk("tensor_scalar", out=gnb[:], in0=t1[:], scalar1=sc[:, 1:2], scalar2=None, op0=ALU.mult),
                         r=["t1", "sc1"], w=["gnb"])
                    bt = bank()
                    pb = psb[bt][:].bitcast(BF16)
                    tr_group([(pb[:, sub * 128:(sub + 1) * 128], gnb[:, sub * 128:(sub + 1) * 128], cstb[:, K_ID, :]) for sub in range(4)], r=["gnb", "cstb"], w=[PK(bt)])
                    S.op("dve", _mk("tensor_tensor", out=obT[:, g * 4:g * 4 + 4, t * 128:(t + 1) * 128], in0=pb[:, 0:512].rearrange("p (s c) -> p s c", s=4),
                         in1=nwc[:, g * 4:g * 4 + 4].unsqueeze(2).broadcast_to([128, 4, 128]), op=ALU.mult), r=[PK(bt), "nwc"], w=["obT"])
                opsB = S.ops[n1_:]
                del S.ops[n1_:]
                pend = dict(ops=opsB, last=(t == NT - 1))
        S.ops.extend(pend["ops"])
        wdone()
        if kind == "s":
            dbg("obT", obT[:, :, 0:128], ["obT"], [128, 2048])
            dbg("PA", PA[:, :, 0:128], ["PA"], [128, 1024])
        if not pre:
            gbs = []
            for q in range(4):
                wv, wk = wget(wsrc("wb", 0, 2048, q * 256, 256), 16, 256)
                if q % 2 == 0:
                    gv, gk = wget(wsrc("w_in", 0, D, C_GB + (q // 2) * 512, 512), 8, 512)
                for sub in range(2):
                    fb = q * 2 + sub
                    gsub = (q % 2) * 2 + sub
                    b1 = bank(); b2 = bank()
                    mm_group(psb[b1][:, 0:TT], [(gv[:, kc, gsub * 128:(gsub + 1) * 128], hT[:, kc, 0:TT]) for kc in range(8)], r=[gk, "hT"], w=[PK(b1)])
                    mm_group(psb[b2][:, 0:TT], [(wv[:, kc, sub * 128:(sub + 1) * 128], obT[:, kc, 0:TT]) for kc in range(16)], r=[wk, "obT"], w=[PK(b2)])
                    S.op("act", _mk("activation", out=sg[:, 0:TT], in_=psb[b1][:, 0:TT], func=AF.Sigmoid), r=[PK(b1)], w=["sg"])
                    S.op("dve", _mk("tensor_tensor", out=mtmp[:, 0:TT], in0=sg[:, 0:TT], in1=psb[b2][:, 0:TT], op=ALU.mult), r=["sg", PK(b2)], w=["mtmp"])
                    S.op("dve", _mk("tensor_tensor", out=mT[:, fb, 0:TT], in0=mtmp[:, 0:TT], in1=PA[:, fb, 0:TT], op=ALU.add), r=["mtmp", "PA"], w=["uT"])
                if q % 2 == 0:
                    wdone()
                    pending_gb = True
                else:
                    wdone(); wdone()
            if kind == "s":
                dbg("mT", mT[:, :, 0:128], ["uT"], [128, 1024])
            wo_blk = [wget(wsrc("wo", 0, D, blk * 512, 512), 8, 512) for blk in range(2)]
            recs = []
            for t in range(NT):
                n0_ = len(S.ops)
                p_ = t % 2
                jk, jkey = ((vn32, "vn32"), (vg, "vg"))[p_]
                c0_ = (2, 6)[p_]
                sk2 = ("sc2", p_)
                bo = [bank(), bank()]
                for blk in range(2):
                    wv, wk = wo_blk[blk]
                    mm_group(psb[bo[blk]][:, :], [(mT[:, kc, t * 128:(t + 1) * 128], wv[:, kc, :]) for kc in range(8)], r=["uT", wk], w=[PK(bo[blk])])
                    S.op("act", _mk("activation", out=jk[:, blk * 512:(blk + 1) * 512], in_=psb[bo[blk]][:, :], func=AF.Square, accum_out=sc[:, c0_ + blk:c0_ + 1 + blk]),
                         r=[PK(bo[blk])], w=[jkey, sk2])
                S.op("dve", _mk("tensor_tensor", out=sc[:, c0_:c0_ + 1], in0=sc[:, c0_:c0_ + 1], in1=sc[:, c0_ + 1:c0_ + 2], op=ALU.add), r=[sk2], w=[sk2])
                rstd_from(sc[:, c0_:c0_ + 1], D, sk2)
                for blk in range(2):
                    cs = slice(blk * 512, (blk + 1) * 512)
                    S.op("dve", _mk("scalar_tensor_tensor", out=jk[:, cs], in0=psb[bo[blk]][:, :], scalar=sc[:, c0_:c0_ + 1], in1=bc["post_mix_w"][:, cs], op0=ALU.mult, op1=ALU.mult),
                         r=[PK(bo[blk]), sk2, "bc_post_mix_w"], w=[jkey])
                S.op("dve", _mk("tensor_tensor", out=xres[:, t, :], in0=xres[:, t, :], in1=jk[:], op=ALU.add), r=[("xres", t), jkey], w=[("xres", t)])
                norm_to_hT(xres[:, t, :], ("xres", t), "pre_ffn_w", t, par=p_)
                recs.append(S.ops[n0_:]); del S.ops[n0_:]
            S.ops.extend(_merge(recs[0], recs[1]) if NT == 2 else recs[0])
            wdone(); wdone()
            if kind == "s":
                dbg("x1", xres[:, 0, :], [("xres", 0)], [128, 1024])
            for blk in range(8):
                wv, wk = wget(wsrc("wup", 0, D, blk * 512, 512), 8, 512)
                for sub in range(4):
                    hb = blk * 4 + sub
                    b = bank()
                    mm_group(psb[b][:, 0:TT], [(wv[:, kc, sub * 128:(sub + 1) * 128], hT[:, kc, 0:TT]) for kc in range(8)], r=[wk, "hT"], w=[PK(b)])
                    rl = rls[hb % 3]
                    S.op("act", _mk("activation", out=rl[:, 0:TT], in_=psb[b][:, 0:TT], func=AF.Relu), r=[PK(b)], w=[("rl", hb % 3)])
                    S.op("pool", _mk("tensor_tensor", out=hidT[:, hb, 0:TT], in0=rl[:, 0:TT], in1=rl[:, 0:TT], op=ALU.mult), r=[("rl", hb % 3)], w=["hidT"] + [("xs", tt_) for tt_ in range(NT)])
                wdone()
            bdn = [[bank(), bank()] for _ in range(NT)]
            for fb in range(8):
                wv, wk = wget(wsrc("wdn", 0, 4096, fb * 128, 128), 32, 128)
                b = bank()
                while any(b in pr for pr in bdn):
                    b = bank()
                mm_group(psb[b][:, 0:TT], [(wv[:, kc, :], hidT[:, kc, 0:TT]) for kc in range(32)], r=[wk, "hidT"], w=[PK(b)])
                S.op("act", _mk("activation", out=fT[:, 0:TT], in_=psb[b][:, 0:TT], func=AF.Copy), r=[PK(b)], w=["fT"])
                for t in range(NT):
                    bb = bdn[t][fb // 4]
                    c0 = (fb % 4) * 128
                    tr_group([(psb[bb][:, c0:c0 + 128], fT[:, t * 128:(t + 1) * 128], cst[:, K_ID, :])], r=["fT", "cst"], w=[PK(bb)])
                wdone()
            recs = []
            for t in range(NT):
                n0_ = len(S.ops)
                p_ = t % 2
                yo_, ykey = ((vg, "vg"), (vn32, "vn32"))[p_]
                c0_ = (4, 8)[p_]
                sk4 = ("sc4", p_)
                for blk in range(2):
                    bb = bdn[t][blk]
                    S.op("act", _mk("activation", out=yo_[:, blk * 512:(blk + 1) * 512], in_=psb[bb][:, :], func=AF.Square, accum_out=sc[:, c0_ + blk:c0_ + 1 + blk]),
                         r=[PK(bb)], w=[ykey, sk4])
                S.op("dve", _mk("tensor_tensor", out=sc[:, c0_:c0_ + 1], in0=sc[:, c0_:c0_ + 1], in1=sc[:, c0_ + 1:c0_ + 2], op=ALU.add), r=[sk4], w=[sk4])
                rstd_from(sc[:, c0_:c0_ + 1], D, sk4)
                for blk in range(2):
                    bb = bdn[t][blk]
                    cs = slice(blk * 512, (blk + 1) * 512)
                    S.op("dve", _mk("scalar_tensor_tensor", out=yo_[:, cs], in0=psb[bb][:, :], scalar=sc[:, c0_:c0_ + 1], in1=bc["post_ffn_w"][:, cs], op0=ALU.mult, op1=ALU.mult),
                         r=[PK(bb), sk4, "bc_post_ffn_w"], w=[ykey])
                S.op("dve", _mk("tensor_tensor", out=yo_[:], in0=yo_[:], in1=xres[:, t, :], op=ALU.add), r=[ykey, ("xres", t)], w=[ykey])
                S.op("sp", _mk("dma_start", out=y_dst[t * 128:(t + 1) * 128, :], in_=yo_[:]), r=[ykey], dma="o%d" % p_)
                recs.append(S.ops[n0_:]); del S.ops[n0_:]
            S.ops.extend(_merge(recs[0], recs[1]) if NT == 2 else recs[0])

    supertile("s", 1, dr["xsm"], dr["ys"], False)
    for s_ in range(2):
        S.op("sp", _mk("dma_start", out=dr["ssms"][s_, :, :], in_=Sf[s_][:]), r=[("Sf", s_), ("xs", s_)], dma="o")
    S.op("sp", _mk("dma_start", out=dr["convs"][:, :], in_=cso[:].rearrange("p a b c -> p (a b c)")), r=["cso"], dma="o")
    S.op("sp", _mk("dma_start", out=bsb1[:].rearrange("p g c -> p (g c)"), in_=dr["bs_p"].partition_broadcast(128)), w=["bsb"], dma="c25")
    S.op("dve", _mk("memset", Sf[0][:], 0.0), w=[("Sf", 0)] + [("Sfg", g) for g in range(4)])
    S.op("dve", _mk("memset", histb[:], 0.0), w=[("histb", ct) for ct in range(24)])
    S.op("sp", _mk("dma_start", out=pflagb[:], in_=dr["pflag"].partition_broadcast(128)), w=["pflagb"], dma="c26")
    ws["ph"] = []
    prew = {}
    res_bufs = [(slots[k][:, :], ("wslot", k)) for k in range(4)] + [(obT[:].rearrange("p a b -> p (a b)"), "obT")]
    res_blocks = [wsrc("w_in", 0, D, C_XBC + g * 512, 512) for g in range(4)] + [wsrc("w_in", 0, D, C_XBC + 2048, 512)]
    for k, srcd in enumerate(res_blocks):
        buf, key = res_bufs[k]
        view = buf.rearrange("p (k c) -> p k c", k=8)
        S.op("pool", _mk("dma_start", out=view, in_=srcd["ap"]), w=[key], dma="pw%d" % k)
        prew[srcd["bid"]] = (view, key)
        if srcd["bid"] not in scr:
            scr[srcd["bid"]] = len(scr)
            S.op("sp", _mk("dma_start", out=wscr.ap()[scr[srcd["bid"]], :, :], in_=buf), r=[key], w=[("scr", scr[srcd["bid"]])], dma="psv%d" % k)
    for su in range(npre):
        r0 = su * 128 * NTP
        supertile("p", NTP, dr["xprev"][r0:r0 + 128 * NTP, :], None, False, pre=True, flagidx=su)
    ws["ph"] = [S.placeholder() for _ in range(NSLOT)]
    S.op("act", _mk("activation", out=Sb[0][:], in_=Sf[0][:], func=AF.Copy), r=[("Sf", 0)] + [("Sfg", g) for g in range(4)], w=[("Sb", 0), ("Sf", 0)])
    dbg("hinit", Sf[0][:], [("Sf", 0)], None)
    for su in range(nsup):
        r0 = su * 128 * NTP
        supertile("p", NTP, dr["xp"][r0:r0 + 128 * NTP, :], dr["yp"][r0:r0 + 128 * NTP, :], su == 0)
    S.op("sp", _mk("dma_start", out=dr["ssmp"][:, :], in_=Sf[0][:]), r=[("Sf", 0)], dma="o")
    S.op("sp", _mk("dma_start", out=dr["convp"][:, :], in_=histb[:].rearrange("p a b -> p (a b)")), r=[("histb", ct) for ct in range(24)], dma="o")
    nops = S.emit(nc, es)
    es.close()
    return nc, nops, dbg_names


def _consts():
    c = np.zeros((NCST, 128, 128), np.float32)
    idx = np.arange(128)
    k = idx[:, None]; i = idx[None, :]
    same = (k // 64) == (i // 64)
    c[K_ID] = np.eye(128)
    c[K_MC_P] = (k <= i)
    c[K_MC_S] = (k <= i) & same
    c[K_LM_P] = (k > i)
    c[K_LM_S] = (k > i) & same
    c[K_ONES] = 1.0
    c[K_SS0] = (k < 64) * np.ones((1, 128))
    c[K_SS1] = (k >= 64) * np.ones((1, 128))
    c[K_GM_P] = (k // 64) <= (i // 64)
    c[K_GM_S] = same
    c[K_CM0] = np.ones((128, 1)) * (i < 64)
    c[K_CM1] = np.ones((128, 1)) * (i >= 64)
    c[K_RM][:, 0] = (idx < 64)
    c[K_RM][:, 1] = (idx >= 64)
    return np.ascontiguousarray(c.transpose(1, 0, 2).reshape(128, NCST * 128))


_CACHE = {}


def kernel(**inp):
    f = lambda a: np.ascontiguousarray(np.asarray(a, dtype=np.float32))
    xpr = f(inp["x_prompt"]); xsm = f(inp["x_sample"])
    cache = f(inp["cache_conv"])[0]; state = f(inp["state_ssm"])[0]
    if "nc" not in _CACHE:
        _CACHE["nc"], _, _ = build_program()
    nc = _CACHE["nc"]
    shared = {
        "w_in": f(inp["w_in"])[0], "wa": f(inp["w_branch_a"])[0], "wb": f(inp["w_branch_b"])[0],
        "wo": f(inp["w_out"])[0], "wup": f(inp["w_up"])[0], "wdn": f(inp["w_down"])[0],
        "pre_mix_w": f(inp["pre_mix_w"]), "ln_w": f(inp["gmlp_ln_w"]), "ln_b": f(inp["gmlp_ln_b"]),
        "post_mix_w": f(inp["post_mix_w"]), "pre_ffn_w": f(inp["pre_ffn_w"]), "post_ffn_w": f(inp["post_ffn_w"]),
        "nw_l": np.ascontiguousarray(f(inp["ssm_norm_w"]).reshape(16, 128).T), "a_log": f(inp["a_log"]), "dt_bias": f(inp["dt_bias"]), "d_skip": f(inp["d_skip"]),
        "cst": _consts(),
    }
    ws = f(inp["gmlp_ws"])[0]
    bs = f(inp["gmlp_bs"])[0]
    shared["wsT_p"] = np.ascontiguousarray(ws.transpose(2, 0, 1).reshape(128, 1024))
    ws_s = np.tile(ws[:, :64, :64], (1, 2, 2))
    shared["wsT_s"] = np.ascontiguousarray(ws_s.transpose(2, 0, 1).reshape(128, 1024))
    shared["bs_p"] = np.ascontiguousarray(bs.reshape(1, 1024))
    shared["bs_s"] = np.ascontiguousarray(np.tile(bs[:, :64], (1, 2)).reshape(1, 1024))
    cw = f(inp["conv_w"])[0]
    shared["convw_l"] = np.ascontiguousarray(cw.reshape(4, 24, 128).transpose(2, 1, 0).reshape(128, 96))
    shared["convb_l"] = np.ascontiguousarray(f(inp["conv_b"])[0].reshape(24, 128).T)
    in_maps = []
    for c in range(NCORE):
        b, q = c // 4, c % 4
        m = dict(shared)
        m["xp"] = np.ascontiguousarray(xpr[b, q * PT:(q + 1) * PT])
        xh = np.zeros((4, D), np.float32)
        if q > 0:
            xh[0:3] = xpr[b, q * PT - 3:q * PT]
        m["xh"] = xh
        m["xsm"] = np.ascontiguousarray(xsm[2 * c:2 * c + 2].reshape(128, D))
        cc = cache[2 * c:2 * c + 2]
        m["cachel"] = np.ascontiguousarray(cc.reshape(2, 3, 24, 128).transpose(3, 0, 2, 1).reshape(128, 144))
        st = state[2 * c:2 * c + 2]
        m["stateT"] = np.ascontiguousarray(st.reshape(2, 2048, 128).transpose(0, 2, 1))
        xprev = np.zeros((3 * PT, D), np.float32)
        if q > 0:
            xprev[(3 - q) * PT:] = xpr[b, 0:q * PT]
        m["xprev"] = xprev
        pf = np.zeros((1, 3 * NSUP), np.float32)
        pf[0, (3 - q) * NSUP:] = 1.0
        m["pflag"] = pf
        in_maps.append(m)
    res = run_bass_kernel_spmd(nc, in_maps, core_ids=list(range(NCORE)))
    R = res.results
    yp = np.stack([np.concatenate([R[b * 4 + q]["yp"] for q in range(4)], axis=0) for b in range(2)])
    ys = np.concatenate([R[c]["ys"].reshape(2, 64, D) for c in range(NCORE)], axis=0)
    def conv_back(a, n):
        return a.reshape(128, n, 24, 3).transpose(1, 3, 2, 0).reshape(n, 3, 3072)
    def ssm_back(a):
        return a.T.reshape(32, 64, 128)
    convp = np.stack([conv_back(R[b * 4 + 3]["convp"], 1)[0] for b in range(2)])[None]
    ssmp = np.stack([ssm_back(R[b * 4 + 3]["ssmp"]) for b in range(2)])[None]
    convs = np.concatenate([conv_back(R[c]["convs"], 2) for c in range(NCORE)], axis=0)[None]
    ssms = np.stack([ssm_back(R[c]["ssms"][s]) for c in range(NCORE) for s in range(2)])[None]
    vs = np.concatenate([R[c]["vs"].reshape(2, 64, D) for c in range(NCORE)], axis=0)[None]
    out = (yp, ys, convp, ssmp, convs, ssms, vs)
    return tuple(np.ascontiguousarray(o, dtype=np.float32) for o in out)
```

```python
import numpy as np
from contextlib import ExitStack
import concourse.bass as bass
import concourse.mybir as mybir
from concourse.bass_utils import run_bass_kernel_spmd

F32 = mybir.dt.float32
BF16 = mybir.dt.bfloat16
AF = mybir.ActivationFunctionType
ALU = mybir.AluOpType

NCORE = 8
D = 1024
PT = 2048
NTP = 2
NSUP = PT // (128 * NTP)
EPS = 1e-6
NSLOT = 4
C_U, C_V, C_Z, C_XBC, C_DT, C_GA, C_GB = 0, 1024, 2048, 4096, 7168, 7200, 8224
(K_ID, K_MC_P, K_MC_S, K_LM_P, K_LM_S, K_ONES, K_SS0, K_SS1, K_GM_P, K_GM_S, K_CM0, K_CM1,
 K_RM) = range(13)
NCST = 13


def _merge(a, b):
    out = []; ia = ib = 0
    while ia < len(a) or ib < len(b):
        if ib >= len(b) or (ia < len(a) and ia * len(b) <= ib * len(a)):
            out.append(a[ia]); ia += 1
        else:
            out.append(b[ib]); ib += 1
    return out


def _mk(name, *a, **kw):
    return lambda e: getattr(e, name)(*a, **kw)


class Sched:
    def __init__(self):
        self.ops = []

    def op(self, eng, fn, r=(), w=(), dma=None):
        o = dict(eng=eng, fn=fn, r=tuple(r), w=tuple(w), dma=dma)
        self.ops.append(o)
        return o

    def placeholder(self):
        o = dict(eng=None, fn=None, r=(), w=(), dma=None)
        self.ops.append(o)
        return o

    def emit(self, nc, es, final_eng="sp"):
        ops = [o for o in self.ops if o["eng"] is not None]
        lastw, readers = {}, {}
        for i, o in enumerate(ops):
            o["id"] = i
            deps = {}
            for k in o["r"]:
                p = lastw.get(k)
                if p is not None:
                    deps[p] = "raw"
            for k in o["w"]:
                p = lastw.get(k)
                if p is not None:
                    deps.setdefault(p, "waw")
                for q in readers.get(k, ()):
                    deps.setdefault(q, "war")
            keep = {}
            for p, t in deps.items():
                po = ops[p]
                if po["dma"] is None and o["dma"] is None and po["eng"] == o["eng"]:
                    if o["eng"] == "pe" or t != "raw":
                        continue
                keep[p] = t
            o["deps"] = keep
            for k in o["r"]:
                readers.setdefault(k, set()).add(i)
            for k in o["w"]:
                lastw[k] = i
                readers[k] = set()
        need = set()
        for o in ops:
            need.update(o["deps"].keys())
        cnt = {}
        for o in ops:
            if o["dma"] is not None:
                s = "d_" + o["dma"]
                cnt[s] = cnt.get(s, 0) + 16
                o["sig"] = (s, cnt[s], 16)
            elif o["id"] in need:
                s = "e_" + o["eng"]
                cnt[s] = cnt.get(s, 0) + 1
                o["sig"] = (s, cnt[s], 1)
            else:
                o["sig"] = None
        sems = {name: es.enter_context(nc.semaphore(name)) for name in sorted(cnt)}
        block = es.enter_context(nc.Block())

        def runner(engname):
            def f(e):
                waited = {}
                for o in ops:
                    if o["eng"] != engname:
                        continue
                    req = {}
                    for p in o["deps"]:
                        s, v, _ = ops[p]["sig"]
                        if v > req.get(s, 0):
                            req[s] = v
                    for s, v in req.items():
                        if waited.get(s, 0) < v:
                            e.wait_ge(sems[s], v)
                            waited[s] = v
                    ins = o["fn"](e)
                    if o["sig"] is not None:
                        ins.then_inc(sems[o["sig"][0]], o["sig"][2])
                if engname == final_eng:
                    for s, v in cnt.items():
                        if s.startswith("d_") and waited.get(s, 0) < v:
                            e.wait_ge(sems[s], v)
            return f

        block.tensor(runner("pe"))
        block.scalar(runner("act"))
        block.vector(runner("dve"))
        block.gpsimd(runner("pool"))
        block.sync(runner("sp"))
        return len(ops)


def build_program(nsup=NSUP, debug=False, npre=3 * NSUP):
    nc = bass.Bass("TRN2", target_bir_lowering=False)
    S = Sched()
    es = ExitStack()
    dr = {}

    def din(name, shape):
        dr[name] = nc.dram_tensor(name, list(shape), F32, kind="ExternalInput").ap()

    def dout(name, shape):
        dr[name] = nc.dram_tensor(name, list(shape), F32, kind="ExternalOutput").ap()

    din("xp", [PT, D]); din("xh", [4, D]); din("xsm", [128, D])
    din("cachel", [128, 2 * 24 * 3]); din("stateT", [2, 128, 2048])
    din("w_in", [D, 9248]); din("wa", [D, D]); din("wb", [2048, D]); din("wo", [D, D])
    din("wup", [D, 4096]); din("wdn", [4096, D])
    for nm in ("pre_mix_w", "ln_w", "ln_b", "post_mix_w", "pre_ffn_w", "post_ffn_w", "bs_p", "bs_s"):
        din(nm, [1, D])
    din("nw_l", [128, 16])
    for nm in ("a_log", "dt_bias", "d_skip"):
        din(nm, [1, 32])
    din("convw_l", [128, 96]); din("convb_l", [128, 24])
    din("xprev", [3 * PT, D]); din("pflag", [1, 3 * NSUP])
    din("cst", [128, NCST * 128]); din("wsT_p", [128, 1024]); din("wsT_s", [128, 1024])
    dout("yp", [PT, D]); dout("ys", [128, D]); dout("convp", [128, 72]); dout("ssmp", [128, 2048])
    dout("convs", [128, 144]); dout("ssms", [2, 128, 2048]); dout("vs", [128, D])

    def sb(name, shape, dt=F32):
        return es.enter_context(nc.sbuf_tensor("s_" + name, list(shape), dt))

    TTM = 128 * NTP
    dbg_names = []

    def dbg(name, ap, keys, shape):
        if not debug:
            return
        d = nc.dram_tensor("dbg_" + name, [int(v) for v in ap.shape], F32, kind="ExternalOutput").ap()
        dbg_names.append("dbg_" + name)
        S.op("pool", _mk("dma_start", out=d, in_=ap), r=keys, dma="dbg")
    cst = sb("cst", [128, NCST, 128]); cstb = sb("cstb", [128, NCST, 128], BF16)
    bc = {nm: sb("bc_" + nm, [128, D]) for nm in ("pre_mix_w", "ln_w", "ln_b", "post_mix_w", "pre_ffn_w", "post_ffn_w")}
    bsb1 = sb("bsb", [128, 8, 128]); bsb = {"p": bsb1, "s": bsb1}
    nwc = sb("nwc", [128, 16])
    alog = sb("alog", [128, 32]); abc = sb("abc", [128, 32]); dtb = sb("dtb", [128, 32]); dsk = sb("dsk", [128, 32])
    convw = sb("convw", [128, 24, 4]); convb = sb("convb", [128, 24])
    neghalf = sb("neghalf", [128, 2])
    wsT = {"p": sb("wsT_p", [128, 8, 128], BF16), "s": sb("wsT_s", [128, 8, 128], BF16)}
    wdt = sb("wdt", [128, 8, 32], BF16)
    cachet = sb("cachet", [128, 2, 24, 3]); cso = sb("cso", [128, 2, 24, 3])
    histb = sb("histb", [128, 24, 3])
    pflagb = sb("pflagb", [128, 3 * NSUP])
    Sf0 = sb("Sf0", [128, 2048])
    Sb = [sb("Sb0", [128, 2048], BF16), sb("Sb1", [128, 2048], BF16)]
    slots = [sb("wslot%d" % i, [128, 4096], BF16) for i in range(NSLOT)]
    xres = sb("xres", [128, NTP, D])
    hT = sb("hT", [128, 8, TTM + 4], BF16)
    uT = sb("uT", [128, 8, TTM], BF16)
    vnb = sb("vnb", [128, NTP, D], BF16)
    oaT = sb("oaT", [128, 8, TTM], BF16)
    PA = sb("PA", [128, 8, TTM], BF16)
    xs_ = sb("xs", [128, NTP, 2048])
    Sf = [Sf0[:, :], xs_[:, 1, :]]
    Btok = sb("Btok", [128, NTP, 512], BF16)
    BT = sb("BT", [128, 4, TTM], BF16); CT = sb("CT", [128, 4, TTM], BF16)
    obT = sb("obT", [128, 16, TTM], BF16)
    mT = uT
    hidT_full = xs_[:].rearrange("p a b -> p (a b)").bitcast(BF16).rearrange("p (k c) -> p k c", k=32)
    hidT_small = xs_[:, 0, :].bitcast(BF16).rearrange("p (k c) -> p k c", k=32)
    hnbs = [sb("hnb", [128, D], BF16), sb("hnb2", [128, D], BF16)]
    vg = sb("vg", [128, D]); vn32 = sb("vn32", [128, D])
    wsraw = vg[:].rearrange("p (g c) -> p g c", g=8)
    junk = vn32
    stat = sb("stat", [128, 2, 6]); mv = sb("mv", [128, 2]); stat2 = sb("stat2", [128, 2, 6]); mv2 = sb("mv2", [128, 2])
    sc = sb("sc", [128, 16])
    stage4 = sb("stage", [128, 4, TTM + 8]); cacc4 = sb("cacc", [128, 4, TTM])
    xcT = sb("xcT", [128, 4, TTM])
    dts = sb("dts", [128, NTP, 8, 32])
    dhi = sb("dhi", [128, NTP, 32], BF16); dlo = sb("dlo", [128, NTP, 32], BF16)
    decS = sb("decS", [128, NTP, 64])
    xdt = sb("xdt", [128, 512], BF16); xw = sb("xw", [128, 512], BF16)
    Bm = sb("Bm", [128, 2, 512], BF16); CTm = sb("CTm", [128, 2, 4, 128], BF16)
    cbm = sb("cbm", [128, 128])
    Rhi = sb("Rhi", [128, 8, 128], BF16); Rlo = sb("Rlo", [128, 8, 128], BF16)
    dec = sb("dec", [128, 8, 128]); Mb = sb("Mb", [128, 8, 128], BF16)
    t1 = sb("t1", [128, 512]); zs = sb("zs", [128, 512]); gnb = sb("gnb", [128, 512], BF16)
    stmp = sb("stmp", [128, 512])
    sg = sb("sg", [128, TTM]); mtmp = sb("mtmp", [128, TTM])
    rls = [sb("rl%d" % i, [128, TTM]) for i in range(3)]; fT = sb("fT", [128, TTM])
    yout = vg
    psb = [es.enter_context(nc.psum_tensor("ps%d" % i, [128, 512], F32)) for i in range(8)]
    pstate = {"n": 0}

    def bank():
        b = pstate["n"] % 8
        pstate["n"] += 1
        return b

    def PK(b):
        return ("ps", b)

    ws = {"n": 0, "ph": [S.placeholder() for _ in range(NSLOT)]}

    wscr = nc.dram_tensor("wscr", [48, 128, 4096], BF16)
    scr = {}

    def wget(srcd, kcn, cols):
        i = ws["n"]; ws["n"] += 1
        slot = i % NSLOT
        view = slots[slot][:, 0:kcn * cols].rearrange("p (k c) -> p k c", k=kcn)
        ph = ws["ph"].pop(0)
        bid = srcd["bid"]
        if bid not in scr:
            scr[bid] = len(scr)
            sv = wscr.ap()[scr[bid], :, 0:kcn * cols]
            ph.update(eng="pool", fn=_mk("dma_start", out=view, in_=srcd["ap"]), r=(), w=(("wslot", slot),), dma="w%d" % slot)
            S.op("sp", _mk("dma_start", out=sv, in_=slots[slot][:, 0:kcn * cols]), r=[("wslot", slot)], w=[("scr", scr[bid])], dma="sv%d" % slot)
        else:
            sv = wscr.ap()[scr[bid], :, 0:kcn * cols]
            ph.update(eng="sp", fn=_mk("dma_start", out=slots[slot][:, 0:kcn * cols], in_=sv), r=(("scr", scr[bid]),), w=(("wslot", slot),), dma="w%d" % slot)
        return view, ("wslot", slot)

    def wdone():
        ws["ph"].append(S.placeholder())

    def wsrc(name, r0, nr, c0, ncol):
        return dict(bid=(name, c0), ap=dr[name][r0:r0 + nr, c0:c0 + ncol].rearrange("(k p) c -> p k c", p=128))

    S.op("sp", _mk("dma_start", out=cst[:].rearrange("p k c -> p (k c)"), in_=dr["cst"][:, :]), w=["cst"], dma="c11")
    S.op("pool", _mk("dma_start", out=cstb[:].rearrange("p k c -> p (k c)"), in_=dr["cst"][:, :]), w=["cstb"], dma="c12")
    for nm in bc:
        S.op("sp", _mk("dma_start", out=bc[nm][:], in_=dr[nm].partition_broadcast(128)), w=["bc_" + nm], dma="c13_" + nm)
    S.op("sp", _mk("dma_start", out=bsb["s"][:].rearrange("p g c -> p (g c)"), in_=dr["bs_s"].partition_broadcast(128)), w=["bsb"], dma="c14")
    S.op("sp", _mk("dma_start", out=nwc[:], in_=dr["nw_l"][:, :]), w=["nwc"], dma="c15")
    S.op("sp", _mk("dma_start", out=alog[:], in_=dr["a_log"].partition_broadcast(128)), w=["alog"], dma="c16")
    S.op("sp", _mk("dma_start", out=dtb[:], in_=dr["dt_bias"].partition_broadcast(128)), w=["dtb"], dma="c17")
    S.op("sp", _mk("dma_start", out=dsk[:], in_=dr["d_skip"].partition_broadcast(128)), w=["dsk"], dma="c18")
    S.op("sp", _mk("dma_start", out=convw[:].rearrange("p a b -> p (a b)"), in_=dr["convw_l"][:, :]), w=["convw"], dma="c19")
    S.op("sp", _mk("dma_start", out=convb[:], in_=dr["convb_l"][:, :]), w=["convb"], dma="c20")
    S.op("sp", _mk("dma_start", out=cachet[:].rearrange("p a b c -> p (a b c)"), in_=dr["cachel"][:, :]), w=["cachet"], dma="c21")
    S.op("pool", _mk("dma_start", out=wdt[:], in_=wsrc("w_in", 0, D, C_DT, 32)["ap"]), w=["wdt"], dma="c22")
    S.op("dve", _mk("memset", neghalf[:], -0.5), w=["neghalf"])
    S.op("act", _mk("activation", out=abc[:], in_=alog[:], func=AF.Exp), r=["alog"], w=["abc"])
    S.op("dve", _mk("tensor_scalar", out=abc[:], in0=abc[:], scalar1=-1.0, scalar2=None, op0=ALU.mult), r=["abc"], w=["abc"])
    for kind, gm in (("p", K_GM_P), ("s", K_GM_S)):
        S.op("sp", _mk("dma_start", out=vg[:], in_=dr["wsT_" + kind][:, :]), w=["vg"], dma="c23")
        S.op("dve", _mk("tensor_tensor",
            out=wsT[kind][:], in0=wsraw, in1=cst[:, gm, :].unsqueeze(1).broadcast_to([128, 8, 128]), op=ALU.mult),
            r=["vg", "cst"], w=["wsT_" + kind])
    for s_ in range(2):
        S.op("sp", _mk("dma_start", out=Sf[s_][:], in_=dr["stateT"][s_, :, :]), w=[("Sf", s_)], dma="c24_%d" % s_)
        S.op("act", _mk("activation", out=Sb[s_][:], in_=Sf[s_][:], func=AF.Copy), r=[("Sf", s_)], w=[("Sb", s_)])

    def rstd_from(ss_ap, n, key):
        S.op("dve", _mk("tensor_scalar", out=ss_ap, in0=ss_ap, scalar1=1.0 / n, scalar2=EPS, op0=ALU.mult, op1=ALU.add), r=[key], w=[key])
        S.op("pool", _mk("tensor_tensor", out=ss_ap, in0=ss_ap, in1=neghalf[:, 0:1], op=ALU.pow), r=[key, "neghalf"], w=[key])

    def mm_group(out_ap, pairs, r, w):
        def fn(e):
            ins = None
            n = len(pairs)
            for i, (l, rr) in enumerate(pairs):
                ins = e.matmul(out_ap, lhsT=l, rhs=rr, start=(i == 0), stop=(i == n - 1))
            return ins
        S.op("pe", fn, r=r, w=w)

    def tr_group(items, r, w):
        def fn(e):
            ins = None
            for (o, i_, idn) in items:
                ins = e.transpose(o, i_, idn)
            return ins
        S.op("pe", fn, r=r, w=w)

    def norm_to_hT(src_ap, srckey, wname, t, ncols_valid=128, par=0):
        hnb = hnbs[par]; hk = ("hnb", par); sck = ("scn", par)
        scc = sc[:, 12 + par:13 + par]
        S.op("act", _mk("activation", out=hnb[:], in_=src_ap, func=AF.Square, accum_out=scc), r=[srckey], w=[hk, sck])
        rstd_from(scc, D, sck)
        S.op("dve", _mk("scalar_tensor_tensor", out=hnb[:], in0=src_ap, scalar=scc, in1=bc[wname][:], op0=ALU.mult, op1=ALU.mult),
             r=[srckey, sck, "bc_" + wname], w=[hk])
        b = bank()
        pb = psb[b][:].bitcast(BF16)
        nv = ncols_valid
        tr_group([(pb[:, kc * 128:kc * 128 + nv], hnb[0:nv, kc * 128:(kc + 1) * 128], cstb[0:nv, K_ID, 0:nv]) for kc in range(8)],
                 r=[hk, "cstb"], w=[PK(b)])
        c0 = t * 128
        S.op("act", _mk("activation", out=hT[:, :, c0:c0 + nv], in_=pb.rearrange("p (k c) -> p k c", k=8)[:, :, 0:nv], func=AF.Copy),
             r=[PK(b)], w=["hT"])

    def supertile(kind, NT, x_src, y_dst, first_prompt, pre=False, flagidx=None):
        TT = 128 * NT
        halo = 4 if first_prompt else 0
        hidT = hidT_full if NT == NTP else hidT_small

        def getw(srcd, kcn, cols):
            if pre:
                return prew[srcd["bid"]]
            return wget(srcd, kcn, cols)

        def donew():
            if not pre:
                wdone()
        NX = TT + halo
        if kind == "p":
            segs = [dict(sel=K_ONES, cm=None, rm=None, st=0)]
            mc, lm = K_MC_P, K_LM_P
            nsegc, L = 1, TT
        else:
            segs = [dict(sel=K_SS0, cm=K_CM0, rm=0, st=0), dict(sel=K_SS1, cm=K_CM1, rm=1, st=1)]
            mc, lm = K_MC_S, K_LM_S
            nsegc, L = 2, 64
        recs = []
        for t in range(NT):
            n0_ = len(S.ops)
            S.op("sp", _mk("dma_start", out=xres[:, t, :], in_=x_src[t * 128:(t + 1) * 128, :]), w=[("xres", t)], dma="x%d" % t)
            norm_to_hT(xres[:, t, :], ("xres", t), "pre_mix_w", t, par=t % 2)
            recs.append(S.ops[n0_:]); del S.ops[n0_:]
        S.ops.extend(_merge(recs[0], recs[1]) if NT == 2 else recs[0])
        if first_prompt:
            S.op("sp", _mk("dma_start", out=vg[0:4, :], in_=dr["xh"][:, :]), w=["vg"], dma="xh")
            norm_to_hT(vg[:, :], "vg", "pre_mix_w", NT, ncols_valid=4)
        for blk in (range(2) if not pre else ()):
            wv, wk = wget(wsrc("w_in", 0, D, C_U + blk * 512, 512), 8, 512)
            for sub in range(4):
                cb = blk * 4 + sub
                b = bank()
                mm_group(psb[b][:, 0:TT], [(wv[:, kc, sub * 128:(sub + 1) * 128], hT[:, kc, 0:TT]) for kc in range(8)],
                         r=[wk, "hT"], w=[PK(b)])
                S.op("act", _mk("activation", out=uT[:, cb, 0:TT], in_=psb[b][:, 0:TT], func=AF.Gelu_apprx_tanh),
                     r=[PK(b)], w=["uT"])
            wdone()
        vblk = [wget(wsrc("w_in", 0, D, C_V + blk * 512, 512), 8, 512) for blk in (range(2) if not pre else ())]
        recs = []
        for t in (range(NT) if not pre else ()):
            n0_ = len(S.ops)
            p_ = t % 2
            G, gk = ((vg, "vg"), (vn32, "vn32"))[p_]
            st_, stk = ((stat, "stat"), (stat2, "stat2"))[p_]
            mv_, mvk = ((mv, "mv"), (mv2, "mv2"))[p_]
            sp_, spk = ((stmp, "stmp"), (t1, "t1"))[p_]
            for blk in range(2):
                wv, wk = vblk[blk]
                b = bank()
                mm_group(psb[b][:, :], [(hT[:, kc, t * 128:(t + 1) * 128], wv[:, kc, :]) for kc in range(8)], r=[wk, "hT"], w=[PK(b)])
                S.op("act", _mk("activation", out=G[:, blk * 512:(blk + 1) * 512], in_=psb[b][:, :], func=AF.Gelu_apprx_tanh),
                     r=[PK(b)], w=[gk])
            for blk in range(2):
                S.op("dve", _mk("bn_stats", out=st_[:, blk, :], in_=G[:, blk * 512:(blk + 1) * 512]), r=[gk], w=[stk])
            S.op("dve", _mk("bn_aggr", out=mv_[:], in_=st_[:].rearrange("p a b -> p (a b)")), r=[stk], w=[mvk])
            rstd_from(mv_[:, 1:2], 1.0, mvk)
            S.op("dve", _mk("tensor_scalar", out=G[:], in0=G[:], scalar1=mv_[:, 0:1], scalar2=mv_[:, 1:2], op0=ALU.subtract, op1=ALU.mult),
                 r=[gk, mvk], w=[gk])
            S.op("dve", _mk("tensor_tensor", out=G[:], in0=G[:], in1=bc["ln_w"][:], op=ALU.mult), r=[gk, "bc_ln_w"], w=[gk])
            S.op("dve", _mk("tensor_tensor", out=G[:], in0=G[:], in1=bc["ln_b"][:], op=ALU.add), r=[gk, "bc_ln_b"], w=[gk])
            S.op("act", _mk("activation", out=vnb[:, t, :], in_=G[:], func=AF.Copy), r=[gk], w=[("vnb", t)])
            if kind == "s":
                S.op("sp", _mk("dma_start", out=dr["vs"][:, :], in_=G[:]), r=[gk], dma="o")
            for half in range(2):
                b = bank()
                def fn(e, b=b, half=half, t=t):
                    ins = None
                    for gi in range(4):
                        g = half * 4 + gi
                        ins = e.matmul(psb[b][:, gi * 128:(gi + 1) * 128], lhsT=vnb[:, t, g * 128:(g + 1) * 128], rhs=wsT[kind][:, g, :], start=True, stop=True)
                    return ins
                S.op("pe", fn, r=[("vnb", t), "wsT_" + kind], w=[PK(b)])
                S.op("dve", _mk("tensor_tensor",
                    out=sp_[:].rearrange("p (g c) -> p g c", g=4), in0=psb[b][:, :].rearrange("p (g c) -> p g c", g=4),
                    in1=bsb[kind][:, half * 4:half * 4 + 4, :], op=ALU.add), r=[PK(b), "bsb"], w=[spk])
                S.op("dve", _mk("tensor_tensor",
                    out=oaT[:, half * 4:half * 4 + 4, t * 128:(t + 1) * 128], in0=sp_[:].rearrange("p (g c) -> p g c", g=4),
                    in1=uT[:, half * 4:half * 4 + 4, t * 128:(t + 1) * 128], op=ALU.mult), r=[spk, "uT"], w=[("oaT", t)])
            recs.append(S.ops[n0_:]); del S.ops[n0_:]
        if not pre:
            S.ops.extend(_merge(recs[0], recs[1]) if NT == 2 else recs[0])
        if not pre:
            wdone(); wdone()
        for blk in (range(2) if not pre else ()):
            wa_v, wa_k = wget(wsrc("wa", 0, D, blk * 512, 512), 8, 512)
            ga_v, ga_k = wget(wsrc("w_in", 0, D, C_GA + blk * 512, 512), 8, 512)
            for sub in range(4):
                fb = blk * 4 + sub
                b1 = bank(); b2 = bank()
                mm_group(psb[b1][:, 0:TT], [(ga_v[:, kc, sub * 128:(sub + 1) * 128], hT[:, kc, 0:TT]) for kc in range(8)], r=[ga_k, "hT"], w=[PK(b1)])
                mm_group(psb[b2][:, 0:TT], [(wa_v[:, kc, sub * 128:(sub + 1) * 128], oaT[:, kc, 0:TT]) for kc in range(8)], r=[wa_k] + [("oaT", tt_) for tt_ in range(NT)], w=[PK(b2)])
                S.op("act", _mk("activation", out=sg[:, 0:TT], in_=psb[b1][:, 0:TT], func=AF.Sigmoid), r=[PK(b1)], w=["sg"])
                S.op("dve", _mk("tensor_tensor", out=PA[:, fb, 0:TT], in0=sg[:, 0:TT], in1=psb[b2][:, 0:TT], op=ALU.mult),
                     r=["sg", PK(b2)], w=["PA"])
            wdone(); wdone()
        recs = []
        for t in range(NT):
            n0_ = len(S.ops)
            b = bank()
            mm_group(psb[b][:, 0:32], [(hT[:, kc, t * 128:(t + 1) * 128], wdt[:, kc, :]) for kc in range(8)], r=["hT", "wdt"], w=[PK(b)])
            dk = ("dts", t)
            A = lambda i, t=t: dts[:, t, i, :]
            S.op("dve", _mk("tensor_tensor", out=dts[:, t, 6, :], in0=psb[b][:, 0:32], in1=dtb[:], op=ALU.add), r=[PK(b), "dtb"], w=[dk])
            S.op("act", _mk("activation", out=dts[:, t, 7, :], in_=dts[:, t, 6, :], func=AF.Abs), r=[dk], w=[dk])
            S.op("act", _mk("activation", out=dts[:, t, 7, :], in_=dts[:, t, 7, :], func=AF.Exp, scale=-1.0), r=[dk], w=[dk])
            S.op("act", _mk("activation", out=dts[:, t, 7, :], in_=dts[:, t, 7, :], func=AF.Ln, bias=1.0), r=[dk], w=[dk])
            S.op("dve", _mk("scalar_tensor_tensor", out=dts[:, t, 0, :], in0=dts[:, t, 6, :], scalar=0.0, in1=dts[:, t, 7, :], op0=ALU.max, op1=ALU.add), r=[dk], w=[dk])
            S.op("dve", _mk("tensor_tensor", out=dts[:, t, 1, :], in0=dts[:, t, 0, :], in1=abc[:], op=ALU.mult), r=[dk, "abc"], w=[dk])
            S.op("dve", _mk("tensor_copy", out=dhi[:, t, :], in_=dts[:, t, 1, :]), r=[dk], w=[("dhi", t)])
            S.op("dve", _mk("tensor_tensor", out=dlo[:, t, :], in0=dts[:, t, 1, :], in1=dhi[:, t, :], op=ALU.subtract), r=[dk, ("dhi", t)], w=[("dlo", t)])
            b2 = bank()
            def fn(e, b2=b2, t=t):
                ins = e.matmul(psb[b2][:, 0:32], lhsT=cst[:, mc, :], rhs=dts[:, t, 1, :], start=True, stop=True)
                for si, sgm in enumerate(segs):
                    ins = e.matmul(psb[b2][:, 32 + 32 * si:64 + 32 * si], lhsT=cst[:, sgm["sel"], :], rhs=dts[:, t, 1, :], start=True, stop=True)
                return ins
            S.op("pe", fn, r=[dk, "cst"], w=[PK(b2)])
            S.op("act", _mk("activation", out=dts[:, t, 2, :], in_=psb[b2][:, 0:32], func=AF.Copy), r=[PK(b2)], w=[dk])
            S.op("act", _mk("activation", out=dts[:, t, 3, :], in_=psb[b2][:, 0:32], func=AF.Exp), r=[PK(b2)], w=[dk])
            S.op("act", _mk("activation", out=decS[:, t, 0:32 * len(segs)], in_=psb[b2][:, 32:32 + 32 * len(segs)], func=AF.Exp), r=[PK(b2)], w=[("decS", t)])
            if kind == "p":
                S.op("dve", _mk("tensor_tensor", out=dts[:, t, 4, :], in0=psb[b2][:, 32:64], in1=dts[:, t, 2, :], op=ALU.subtract), r=[PK(b2), dk], w=[dk])
            else:
                for si in range(2):
                    S.op("dve", _mk("tensor_tensor",
                        out=dts[64 * si:64 * si + 64, t, 4, :], in0=psb[b2][64 * si:64 * si + 64, 32 + 32 * si:64 + 32 * si],
                        in1=dts[64 * si:64 * si + 64, t, 2, :], op=ALU.subtract), r=[PK(b2), dk], w=[dk])
            S.op("act", _mk("activation", out=dts[:, t, 4, :], in_=dts[:, t, 4, :], func=AF.Exp), r=[dk], w=[dk])
            S.op("dve", _mk("tensor_tensor", out=dts[:, t, 5, :], in0=dts[:, t, 4, :], in1=dts[:, t, 0, :], op=ALU.mult), r=[dk], w=[dk])
            if pre:
                S.op("dve", _mk("tensor_scalar", out=dts[:, t, 5, :], in0=dts[:, t, 5, :], scalar1=pflagb[:, flagidx:flagidx + 1], scalar2=None, op0=ALU.mult),
                     r=[dk, "pflagb"], w=[dk])
            recs.append(S.ops[n0_:]); del S.ops[n0_:]
        S.ops.extend(_merge(recs[0], recs[1]) if NT == 2 else recs[0])

        if kind == "s":
            dbg("dts", dts[:, 0, :, :].rearrange("p a b -> p (a b)"), [("dts", 0)], [128, 256])
            dbg("decS", decS[:, 0, :], [("decS", 0)], [128, 64])
            dbg("hT", hT[:, :, 0:128], ["hT"], [128, 1024])
        alt_stage = [(t1, "t1"), (zs, "zs"), (stmp, "stmp"), (vg, "vg")]
        alt_cacc = [(sg, "sg"), (mtmp, "mtmp"), (rls[0], ("rl", 0)), (rls[1], ("rl", 1))]

        def conv_block(cts, wv, wk, outs, alt=False):
            bks = [bank() for _ in cts]
            for i, ct in enumerate(cts):
                mm_group(psb[bks[i]][:, 0:NX], [(wv[:, kc, i * 128:(i + 1) * 128], hT[:, kc, 0:NX]) for kc in range(8)], r=[wk, "hT"], w=[PK(bks[i])])
            st3s = []
            sks = []
            for i, ct in enumerate(cts):
                b = bks[i]
                stage = alt_stage[i][0][:, 0:TTM + 8] if alt else stage4[:, i, :]
                sk = alt_stage[i][1] if alt else ("stage", i)
                sks.append(sk)
                st3 = stage[:, 0:nsegc * (L + 3)].rearrange("p (s c) -> p s c", s=nsegc)
                st3s.append(st3)
                if first_prompt:
                    S.op("act", _mk("activation", out=stage[:, 0:3], in_=psb[b][:, TT:TT + 3], func=AF.Copy), r=[PK(b)], w=[sk])
                elif kind == "p":
                    S.op("pool", _mk("tensor_copy", out=stage[:, 0:3], in_=histb[:, ct, :]), r=[("histb", ct)], w=[sk])
                else:
                    S.op("pool", _mk("tensor_copy", out=st3[:, :, 0:3], in_=cachet[:, :, ct, :]), r=["cachet"], w=[sk])
                S.op("act", _mk("activation", out=st3[:, :, 3:3 + L], in_=psb[b][:, 0:TT].rearrange("p (s c) -> p s c", s=nsegc), func=AF.Copy),
                     r=[PK(b)], w=[sk])
                if kind == "p":
                    S.op("pool", _mk("tensor_copy", out=histb[:, ct, :], in_=stage[:, L:L + 3]), r=[sk], w=[("histb", ct)])
                else:
                    S.op("pool", _mk("tensor_copy", out=cso[:, :, ct, :], in_=st3[:, :, L:L + 3]), r=[sk], w=["cso"])
            caps = [(alt_cacc[i][0][:, 0:TT] if alt else cacc4[:, i, 0:TT]) for i in range(len(cts))]
            cks = [(alt_cacc[i][1] if alt else ("cacc", i)) for i in range(len(cts))]
            ca3s = [c_.rearrange("p (s c) -> p s c", s=nsegc) for c_ in caps]
            for i, ct in enumerate(cts):
                S.op("dve", _mk("tensor_scalar", out=ca3s[i], in0=st3s[i][:, :, 0:L], scalar1=convw[:, ct, 0:1], scalar2=convb[:, ct:ct + 1], op0=ALU.mult, op1=ALU.add),
                     r=[sks[i], "convw", "convb"], w=[cks[i]])
            for k in range(1, 4):
                for i, ct in enumerate(cts):
                    S.op("dve", _mk("scalar_tensor_tensor", out=ca3s[i], in0=st3s[i][:, :, k:k + L], scalar=convw[:, ct, k:k + 1], in1=ca3s[i], op0=ALU.mult, op1=ALU.add),
                         r=[sks[i], cks[i], "convw"], w=[cks[i]])
            for i, ct in enumerate(cts):
                S.op("act", _mk("activation", out=outs[i][0], in_=caps[i], func=AF.Silu), r=[cks[i]], w=[outs[i][1]])

        if pre:
            xcT_alt = dec[:].rearrange("p a b -> p (a b)").rearrange("p (s c) -> p s c", s=4)
            blocks = [("B", None)] + [("x", g) for g in range(4)]

            def conv_of(k):
                kind_, g = blocks[k]
                alt = (k % 2 == 1)
                if kind_ == "B":
                    wv, wk = getw(wsrc("w_in", 0, D, C_XBC + 2048, 512), 8, 512)
                    conv_block([16 + gg for gg in range(4)], wv, wk, [(BT[:, gg, 0:TT], "BT") for gg in range(4)], alt=alt)
                else:
                    wv, wk = getw(wsrc("w_in", 0, D, C_XBC + g * 512, 512), 8, 512)
                    xo, xk = (xcT_alt, "dec") if alt else (xcT, "xcT")
                    conv_block([g * 4 + sub for sub in range(4)], wv, wk, [(xo[:, sub, 0:TT], xk) for sub in range(4)], alt=alt)

            def post_of(k):
                kind_, g = blocks[k]
                alt = (k % 2 == 1)
                if kind_ == "B":
                    for t in range(NT):
                        b = bank()
                        pb = psb[b][:].bitcast(BF16)
                        tr_group([(pb[:, gg * 128:(gg + 1) * 128], BT[:, gg, t * 128:(t + 1) * 128], cstb[:, K_ID, :]) for gg in range(4)], r=["BT", "cstb"], w=[PK(b)])
                        S.op("act", _mk("activation", out=Btok[:, t, :], in_=pb[:, 0:512], func=AF.Copy), r=[PK(b)], w=[("Btok", t)])
                    return
                xo, xk = (xcT_alt, "dec") if alt else (xcT, "xcT")
                xwb, xwk = (xdt, "xdt") if alt else (xw, "xw")
                gc = slice(g * 512, (g + 1) * 512)
                for t in range(NT):
                    dk = ("dts", t)
                    b = bank()
                    tr_group([(psb[b][:, sub * 128:(sub + 1) * 128], xo[:, sub, t * 128:(t + 1) * 128], cst[:, K_ID, :]) for sub in range(4)],
                             r=[xk, "cst"], w=[PK(b)])
                    S.op("dve", _mk("tensor_tensor", out=xwb[:].rearrange("p (h c) -> p h c", h=8), in0=psb[b][:, :].rearrange("p (h c) -> p h c", h=8),
                         in1=dts[:, t, 5, g * 8:(g + 1) * 8].unsqueeze(2).broadcast_to([128, 8, 64]), op=ALU.mult), r=[PK(b), dk], w=[xwk])
                    bst = bank()
                    mm_group(psb[bst][:, :], [(Btok[:, t, g * 128:(g + 1) * 128], xwb[:, :])], r=[("Btok", t), xwk], w=[PK(bst)])
                    S.op("dve", _mk("tensor_tensor",
                        out=Sf[0][:, gc].rearrange("p (h c) -> p h c", h=8), in0=Sf[0][:, gc].rearrange("p (h c) -> p h c", h=8),
                        in1=decS[:, t, g * 8:g * 8 + 8].unsqueeze(2).broadcast_to([128, 8, 64]), op=ALU.mult),
                        r=[("Sfg", g), ("decS", t)], w=[("Sfg", g)])
                    S.op("dve", _mk("tensor_tensor", out=Sf[0][:, gc], in0=Sf[0][:, gc], in1=psb[bst][:, :], op=ALU.add),
                         r=[("Sfg", g), PK(bst)], w=[("Sfg", g)])

            conv_of(0)
            for k in range(len(blocks)):
                n0_ = len(S.ops)
                if k + 1 < len(blocks):
                    conv_of(k + 1)
                opsC = S.ops[n0_:]; del S.ops[n0_:]
                post_of(k)
                opsP = S.ops[n0_:]; del S.ops[n0_:]
                S.ops.extend(_merge(opsC, opsP))
            return
        wv, wk = getw(wsrc("w_in", 0, D, C_XBC + 2048, 512), 8, 512)
        conv_block([16 + g for g in range(4)], wv, wk, [(BT[:, g, 0:TT], "BT") for g in range(4)])
        donew()
        for t in range(NT):
            b = bank()
            pb = psb[b][:].bitcast(BF16)
            tr_group([(pb[:, g * 128:(g + 1) * 128], BT[:, g, t * 128:(t + 1) * 128], cstb[:, K_ID, :]) for g in range(4)], r=["BT", "cstb"], w=[PK(b)])
            S.op("act", _mk("activation", out=Btok[:, t, :], in_=pb[:, 0:512], func=AF.Copy), r=[PK(b)], w=[("Btok", t)])
        if not pre:
            wv, wk = getw(wsrc("w_in", 0, D, C_XBC + 2560, 512), 8, 512)
            conv_block([20 + g for g in range(4)], wv, wk, [(CT[:, g, 0:TT], "CT") for g in range(4)])
            donew()
        for g in range(4):
            wv, wk = getw(wsrc("w_in", 0, D, C_XBC + g * 512, 512), 8, 512)
            conv_block([g * 4 + sub for sub in range(4)], wv, wk, [(xcT[:, sub, 0:TT], "xcT") for sub in range(4)])
            donew()
            for t in range(NT):
                b = bank()
                tr_group([(psb[b][:, sub * 128:(sub + 1) * 128], xcT[:, sub, t * 128:(t + 1) * 128], cst[:, K_ID, :]) for sub in range(4)],
                         r=["xcT", "cst"], w=[PK(b)])
                S.op("act", _mk("activation", out=xs_[:, t, g * 512:(g + 1) * 512], in_=psb[b][:, :], func=AF.Copy), r=[PK(b)], w=[("xs", t), "hidT"])
        def merge_emit(a, b):
            out_ = []; ia = ib = 0
            while ia < len(a) or ib < len(b):
                if ib >= len(b) or (ia < len(a) and ia * len(b) <= ib * len(a)):
                    out_.append(a[ia]); ia += 1
                else:
                    out_.append(b[ib]); ib += 1
            S.ops.extend(out_)

        pend = None
        for g in range(4):
            zv, zk = wget(wsrc("w_in", 0, D, C_Z + g * 512, 512), 8, 512)
            for t in range(NT):
                n0_ = len(S.ops)
                dk = ("dts", t)
                tc_ = slice(t * 128, (t + 1) * 128)
                gc = slice(g * 512, (g + 1) * 512)
                xs3 = xs_[:, t, g * 512:(g + 1) * 512].rearrange("p (h c) -> p h c", h=8)
                if not pre:
                  S.op("dve", _mk("tensor_tensor", out=xdt[:].rearrange("p (h c) -> p h c", h=8), in0=xs3,
                     in1=dts[:, t, 0, g * 8:(g + 1) * 8].unsqueeze(2).broadcast_to([128, 8, 64]), op=ALU.mult), r=[("xs", t), dk], w=["xdt"])
                S.op("dve", _mk("tensor_tensor", out=xw[:].rearrange("p (h c) -> p h c", h=8), in0=xs3,
                     in1=dts[:, t, 5, g * 8:(g + 1) * 8].unsqueeze(2).broadcast_to([128, 8, 64]), op=ALU.mult), r=[("xs", t), dk], w=["xw"])
                if first_prompt and not pre and g == 0 and t == 0:
                    dbg("pxs0", xs_[:, 0, 0:512], [("xs", 0)], None)
                    dbg("pBT", BT[:, 0, 0:128], ["BT"], None); dbg("pCT", CT[:, 0, 0:128], ["CT"], None)
                    dbg("pSb", Sb[0][:, 0:512], [("Sb", 0)], None)
                    dbg("phT", hT[:, :, 252:260], ["hT"], None)
                if kind == "s" and g == 0:
                    dbg("xs0", xs_[:, 0, 0:512], [("xs", 0)], [128, 512])
                    dbg("xdt", xdt[:], ["xdt"], [128, 512])
                    dbg("xw", xw[:], ["xw"], [128, 512])
                    dbg("BT", BT[:, 0, 0:128], ["BT"], [128, 128])
                    dbg("CT", CT[:, 0, 0:128], ["CT"], [128, 128])
                    dbg("Btok", Btok[:, 0, 0:128], [("Btok", 0)], [128, 128])
                if not pre:
                    bcb = bank()
                    mm_group(psb[bcb][:, 0:128], [(BT[:, g, tc_], CT[:, g, tc_])], r=["BT", "CT"], w=[PK(bcb)])
                    S.op("dve", _mk("tensor_tensor", out=cbm[:], in0=psb[bcb][:, 0:128], in1=cst[:, mc, :], op=ALU.mult), r=[PK(bcb), "cst"], w=["cbm"])
                    S.op("pool", _mk("tensor_tensor", out=Rhi[:], in0=cstb[:, mc, :].unsqueeze(1).broadcast_to([128, 8, 128]),
                         in1=dhi[:, t, g * 8:(g + 1) * 8].unsqueeze(2).broadcast_to([128, 8, 128]), op=ALU.mult), r=["cstb", ("dhi", t)], w=["Rhi"])
                    S.op("pool", _mk("tensor_tensor", out=Rlo[:], in0=cstb[:, mc, :].unsqueeze(1).broadcast_to([128, 8, 128]),
                         in1=dlo[:, t, g * 8:(g + 1) * 8].unsqueeze(2).broadcast_to([128, 8, 128]), op=ALU.mult), r=["cstb", ("dlo", t)], w=["Rlo"])
                    bs0 = bank(); bs1 = bank()
                    for hh, bb in ((0, bs0), (1, bs1)):
                        mm_group(psb[bb][:, :], [(cstb[:, lm, :], Rhi[:, hh * 4:hh * 4 + 4, :].rearrange("p h c -> p (h c)")),
                                                (cstb[:, lm, :], Rlo[:, hh * 4:hh * 4 + 4, :].rearrange("p h c -> p (h c)"))],
                                 r=["cstb", "Rhi", "Rlo"], w=[PK(bb)])
                        S.op("act", _mk("activation", out=dec[:, hh * 4:hh * 4 + 4, :].rearrange("p h c -> p (h c)"), in_=psb[bb][:, :], func=AF.Exp),
                             r=[PK(bb)], w=["dec"])
                    S.op("dve", _mk("tensor_tensor", out=Mb[:], in0=dec[:], in1=cbm[:].unsqueeze(1).broadcast_to([128, 8, 128]), op=ALU.mult),
                         r=["dec", "cbm"], w=["Mb"])
                    if kind == "s" and g == 0:
                        dbg("cbm", cbm[:], ["cbm"], [128, 128])
                        dbg("dec", dec[:].rearrange("p a b -> p (a b)"), ["dec"], [128, 1024])
                        dbg("Mb", Mb[:].rearrange("p a b -> p (a b)"), ["Mb"], [128, 1024])
                    byd = bank()
                    def fn(e, byd=byd):
                        ins = None
                        for h in range(8):
                            ins = e.matmul(psb[byd][:, h * 64:(h + 1) * 64], lhsT=Mb[:, h, :], rhs=xdt[:, h * 64:(h + 1) * 64], start=True, stop=True)
                        return ins
                    S.op("pe", fn, r=["Mb", "xdt"], w=[PK(byd)])
                    byo = bank()
                    gc = slice(g * 512, (g + 1) * 512)
                    if kind == "p":
                        mm_group(psb[byo][:, :], [(CT[:, g, tc_], Sb[0][:, gc])], r=["CT", ("Sb", 0)], w=[PK(byo)])
                    else:
                        for si in range(2):
                            S.op("dve", _mk("tensor_tensor", out=CTm[:, si, g, :], in0=CT[:, g, tc_], in1=cstb[:, K_CM0 + si, :], op=ALU.mult),
                                 r=["CT", "cstb"], w=["CTm"])
                        mm_group(psb[byo][:, :], [(CTm[:, 0, g, :], Sb[0][:, gc]), (CTm[:, 1, g, :], Sb[1][:, gc])], r=["CTm", ("Sb", 0), ("Sb", 1)], w=[PK(byo)])
                for si, sgm in enumerate(segs):
                    bst = bank()
                    sidx = sgm["st"]
                    if kind == "p":
                        lhs = Btok[:, t, g * 128:(g + 1) * 128]
                        rk = [("Btok", t)]
                    else:
                        S.op("dve", _mk("tensor_scalar", out=Bm[:, si, g * 128:(g + 1) * 128], in0=Btok[:, t, g * 128:(g + 1) * 128],
                             scalar1=cst[:, K_RM, si:si + 1], scalar2=None, op0=ALU.mult), r=[("Btok", t), "cst"], w=["Bm"])
                        lhs = Bm[:, si, g * 128:(g + 1) * 128]
                        rk = ["Bm"]
                    mm_group(psb[bst][:, :], [(lhs, xw[:, :])], r=rk + ["xw"], w=[PK(bst)])
                    S.op("dve", _mk("tensor_tensor",
                        out=Sf[sidx][:, gc].rearrange("p (h c) -> p h c", h=8), in0=Sf[sidx][:, gc].rearrange("p (h c) -> p h c", h=8),
                        in1=decS[:, t, 32 * si + g * 8:32 * si + g * 8 + 8].unsqueeze(2).broadcast_to([128, 8, 64]), op=ALU.mult),
                        r=[("Sf", sidx), ("decS", t)], w=[("Sf", sidx)])
                    S.op("dve", _mk("tensor_tensor", out=Sf[sidx][:, gc], in0=Sf[sidx][:, gc], in1=psb[bst][:, :], op=ALU.add),
                         r=[("Sf", sidx), PK(bst)], w=[("Sf", sidx)])
                    if not pre:
                        S.op("act", _mk("activation", out=Sb[sidx][:, gc], in_=Sf[sidx][:, gc], func=AF.Copy), r=[("Sf", sidx)], w=[("Sb", sidx)])
                opsA = S.ops[n0_:]
                del S.ops[n0_:]
                if pend is not None:
                    merge_emit(opsA, pend["ops"])
                    if pend["last"]:
                        wdone()
                else:
                    S.ops.extend(opsA)
                n1_ = len(S.ops)
                if not pre:
                    S.op("dve", _mk("tensor_tensor", out=t1[:].rearrange("p (h c) -> p h c", h=8), in0=psb[byo][:, :].rearrange("p (h c) -> p h c", h=8),
                         in1=dts[:, t, 3, g * 8:(g + 1) * 8].unsqueeze(2).broadcast_to([128, 8, 64]), op=ALU.mult), r=[PK(byo), dk], w=["t1"])
                    S.op("dve", _mk("tensor_tensor", out=t1[:], in0=t1[:], in1=psb[byd][:, :], op=ALU.add), r=["t1", PK(byd)], w=["t1"])
                    S.op("dve", _mk("tensor_tensor", out=xs3, in0=xs3, in1=dsk[:, g * 8:(g + 1) * 8].unsqueeze(2).broadcast_to([128, 8, 64]), op=ALU.mult),
                         r=[("xs", t), "dsk"], w=[("xs", t)])
                    S.op("dve", _mk("tensor_tensor", out=t1[:], in0=t1[:], in1=xs_[:, t, gc], op=ALU.add), r=["t1", ("xs", t)], w=["t1"])
                    if first_prompt and g == 0 and t == 0:
                        dbg("py", t1[:], ["t1"], None)
                    if kind == "s" and g == 0:
                        dbg("y", t1[:], ["t1"], [128, 512])
                        dbg("Sf0", Sf[0][:, 0:512], [("Sf", 0)], [128, 512])
                    bz = bank()
                    mm_group(psb[bz][:, :], [(hT[:, kc, tc_], zv[:, kc, :]) for kc in range(8)], r=["hT", zk], w=[PK(bz)])
                    S.op("act", _mk("activation", out=zs[:], in_=psb[bz][:, :], func=AF.Silu), r=[PK(bz)], w=["zs"])
                    S.op("dve", _mk("tensor_tensor", out=t1[:], in0=t1[:], in1=zs[:], op=ALU.mult), r=["t1", "zs"], w=["t1"])
                    S.op("act", _mk("activation", out=zs[:], in_=t1[:], func=AF.Square, accum_out=sc[:, 1:2]), r=["t1"], w=["zs", "sc1"])
                    rstd_from(sc[:, 1:2], 512.0, "sc1")
                    S.op("dve", _mk("tensor_scalar", out=gnb[:], in0=t1[:], scalar1=sc[:, 1:2], scalar2=None, op0=ALU.mult),
                         r=["t1", "sc1"], w=["gnb"])
                    bt = bank()
                    pb = psb[bt][:].bitcast(BF16)
                    tr_group([(pb[:, sub * 128:(sub + 1) * 128], gnb[:, sub * 128:(sub + 1) * 128], cstb[:, K_ID, :]) for sub in range(4)], r=["gnb", "cstb"], w=[PK(bt)])
                    S.op("dve", _mk("tensor_tensor", out=obT[:, g * 4:g * 4 + 4, t * 128:(t + 1) * 128], in0=pb[:, 0:512].rearrange("p (s c) -> p s c", s=4),
                         in1=nwc[:, g * 4:g * 4 + 4].unsqueeze(2).broadcast_to([128, 4, 128]), op=ALU.mult), r=[PK(bt), "nwc"], w=["obT"])
                opsB = S.ops[n1_:]
                del S.ops[n1_:]
                pend = dict(ops=opsB, last=(t == NT - 1))
        S.ops.extend(pend["ops"])
        wdone()
        if kind == "s":
            dbg("obT", obT[:, :, 0:128], ["obT"], [128, 2048])
            dbg("PA", PA[:, :, 0:128], ["PA"], [128, 1024])
        if not pre:
            gbs = []
            for q in range(4):
                wv, wk = wget(wsrc("wb", 0, 2048, q * 256, 256), 16, 256)
                if q % 2 == 0:
                    gv, gk = wget(wsrc("w_in", 0, D, C_GB + (q // 2) * 512, 512), 8, 512)
                for sub in range(2):
                    fb = q * 2 + sub
                    gsub = (q % 2) * 2 + sub
                    b1 = bank(); b2 = bank()
                    mm_group(psb[b1][:, 0:TT], [(gv[:, kc, gsub * 128:(gsub + 1) * 128], hT[:, kc, 0:TT]) for kc in range(8)], r=[gk, "hT"], w=[PK(b1)])
                    mm_group(psb[b2][:, 0:TT], [(wv[:, kc, sub * 128:(sub + 1) * 128], obT[:, kc, 0:TT]) for kc in range(16)], r=[wk, "obT"], w=[PK(b2)])
                    S.op("act", _mk("activation", out=sg[:, 0:TT], in_=psb[b1][:, 0:TT], func=AF.Sigmoid), r=[PK(b1)], w=["sg"])
                    S.op("dve", _mk("tensor_tensor", out=mtmp[:, 0:TT], in0=sg[:, 0:TT], in1=psb[b2][:, 0:TT], op=ALU.mult), r=["sg", PK(b2)], w=["mtmp"])
                    S.op("dve", _mk("tensor_tensor", out=mT[:, fb, 0:TT], in0=mtmp[:, 0:TT], in1=PA[:, fb, 0:TT], op=ALU.add), r=["mtmp", "PA"], w=["uT"])
                if q % 2 == 0:
                    wdone()
                    pending_gb = True
                else:
                    wdone(); wdone()
            if kind == "s":
                dbg("mT", mT[:, :, 0:128], ["uT"], [128, 1024])
            wo_blk = [wget(wsrc("wo", 0, D, blk * 512, 512), 8, 512) for blk in range(2)]
            recs = []
            for t in range(NT):
                n0_ = len(S.ops)
                p_ = t % 2
                jk, jkey = ((vn32, "vn32"), (vg, "vg"))[p_]
                c0_ = (2, 6)[p_]
                sk2 = ("sc2", p_)
                bo = [bank(), bank()]
                for blk in range(2):
                    wv, wk = wo_blk[blk]
                    mm_group(psb[bo[blk]][:, :], [(mT[:, kc, t * 128:(t + 1) * 128], wv[:, kc, :]) for kc in range(8)], r=["uT", wk], w=[PK(bo[blk])])
                    S.op("act", _mk("activation", out=jk[:, blk * 512:(blk + 1) * 512], in_=psb[bo[blk]][:, :], func=AF.Square, accum_out=sc[:, c0_ + blk:c0_ + 1 + blk]),
                         r=[PK(bo[blk])], w=[jkey, sk2])
                S.op("dve", _mk("tensor_tensor", out=sc[:, c0_:c0_ + 1], in0=sc[:, c0_:c0_ + 1], in1=sc[:, c0_ + 1:c0_ + 2], op=ALU.add), r=[sk2], w=[sk2])
                rstd_from(sc[:, c0_:c0_ + 1], D, sk2)
                for blk in range(2):
                    cs = slice(blk * 512, (blk + 1) * 512)
                    S.op("dve", _mk("scalar_tensor_tensor", out=jk[:, cs], in0=psb[bo[blk]][:, :], scalar=sc[:, c0_:c0_ + 1], in1=bc["post_mix_w"][:, cs], op0=ALU.mult, op1=ALU.mult),
                         r=[PK(bo[blk]), sk2, "bc_post_mix_w"], w=[jkey])
                S.op("dve", _mk("tensor_tensor", out=xres[:, t, :], in0=xres[:, t, :], in1=jk[:], op=ALU.add), r=[("xres", t), jkey], w=[("xres", t)])
                norm_to_hT(xres[:, t, :], ("xres", t), "pre_ffn_w", t, par=p_)
                recs.append(S.ops[n0_:]); del S.ops[n0_:]
            S.ops.extend(_merge(recs[0], recs[1]) if NT == 2 else recs[0])
            wdone(); wdone()
            if kind == "s":
                dbg("x1", xres[:, 0, :], [("xres", 0)], [128, 1024])
            for blk in range(8):
                wv, wk = wget(wsrc("wup", 0, D, blk * 512, 512), 8, 512)
                for sub in range(4):
                    hb = blk * 4 + sub
                    b = bank()
                    mm_group(psb[b][:, 0:TT], [(wv[:, kc, sub * 128:(sub + 1) * 128], hT[:, kc, 0:TT]) for kc in range(8)], r=[wk, "hT"], w=[PK(b)])
                    rl = rls[hb % 3]
                    S.op("act", _mk("activation", out=rl[:, 0:TT], in_=psb[b][:, 0:TT], func=AF.Relu), r=[PK(b)], w=[("rl", hb % 3)])
                    S.op("pool", _mk("tensor_tensor", out=hidT[:, hb, 0:TT], in0=rl[:, 0:TT], in1=rl[:, 0:TT], op=ALU.mult), r=[("rl", hb % 3)], w=["hidT"] + [("xs", tt_) for tt_ in range(NT)])
                wdone()
            bdn = [[bank(), bank()] for _ in range(NT)]
            for fb in range(8):
                wv, wk = wget(wsrc("wdn", 0, 4096, fb * 128, 128), 32, 128)
                b = bank()
                while any(b in pr for pr in bdn):
                    b = bank()
                mm_group(psb[b][:, 0:TT], [(wv[:, kc, :], hidT[:, kc, 0:TT]) for kc in range(32)], r=[wk, "hidT"], w=[PK(b)])
                S.op("act", _mk("activation", out=fT[:, 0:TT], in_=psb[b][:, 0:TT], func=AF.Copy), r=[PK(b)], w=["fT"])
                for t in range(NT):
                    bb = bdn[t][fb // 4]
                    c0 = (fb % 4) * 128
                    tr_group([(psb[bb][:, c0:c0 + 128], fT[:, t * 128:(t + 1) * 128], cst[:, K_ID, :])], r=["fT", "cst"], w=[PK(bb)])
                wdone()
            recs = []
            for t in range(NT):
                n0_ = len(S.ops)
                p_ = t % 2
                yo_, ykey = ((vg, "vg"), (vn32, "vn32"))[p_]
                c0_ = (4, 8)[p_]
                sk4 = ("sc4", p_)
                for blk in range(2):
                    bb = bdn[t][blk]
                    S.op("act", _mk("activation", out=yo_[:, blk * 512:(blk + 1) * 512], in_=psb[bb][:, :], func=AF.Square, accum_out=sc[:, c0_ + blk:c0_ + 1 + blk]),
                         r=[PK(bb)], w=[ykey, sk4])
                S.op("dve", _mk("tensor_tensor", out=sc[:, c0_:c0_ + 1], in0=sc[:, c0_:c0_ + 1], in1=sc[:, c0_ + 1:c0_ + 2], op=ALU.add), r=[sk4], w=[sk4])
                rstd_from(sc[:, c0_:c0_ + 1], D, sk4)
                for blk in range(2):
                    bb = bdn[t][blk]
                    cs = slice(blk * 512, (blk + 1) * 512)
                    S.op("dve", _mk("scalar_tensor_tensor", out=yo_[:, cs], in0=psb[bb][:, :], scalar=sc[:, c0_:c0_ + 1], in1=bc["post_ffn_w"][:, cs], op0=ALU.mult, op1=ALU.mult),
                         r=[PK(bb), sk4, "bc_post_ffn_w"], w=[ykey])
                S.op("dve", _mk("tensor_tensor", out=yo_[:], in0=yo_[:], in1=xres[:, t, :], op=ALU.add), r=[ykey, ("xres", t)], w=[ykey])
                S.op("sp", _mk("dma_start", out=y_dst[t * 128:(t + 1) * 128, :], in_=yo_[:]), r=[ykey], dma="o%d" % p_)
                recs.append(S.ops[n0_:]); del S.ops[n0_:]
            S.ops.extend(_merge(recs[0], recs[1]) if NT == 2 else recs[0])

    supertile("s", 1, dr["xsm"], dr["ys"], False)
    for s_ in range(2):
        S.op("sp", _mk("dma_start", out=dr["ssms"][s_, :, :], in_=Sf[s_][:]), r=[("Sf", s_), ("xs", s_)], dma="o")
    S.op("sp", _mk("dma_start", out=dr["convs"][:, :], in_=cso[:].rearrange("p a b c -> p (a b c)")), r=["cso"], dma="o")
    S.op("sp", _mk("dma_start", out=bsb1[:].rearrange("p g c -> p (g c)"), in_=dr["bs_p"].partition_broadcast(128)), w=["bsb"], dma="c25")
    S.op("dve", _mk("memset", Sf[0][:], 0.0), w=[("Sf", 0)] + [("Sfg", g) for g in range(4)])
    S.op("dve", _mk("memset", histb[:], 0.0), w=[("histb", ct) for ct in range(24)])
    S.op("sp", _mk("dma_start", out=pflagb[:], in_=dr["pflag"].partition_broadcast(128)), w=["pflagb"], dma="c26")
    ws["ph"] = []
    prew = {}
    res_bufs = [(slots[k][:, :], ("wslot", k)) for k in range(4)] + [(obT[:].rearrange("p a b -> p (a b)"), "obT")]
    res_blocks = [wsrc("w_in", 0, D, C_XBC + g * 512, 512) for g in range(4)] + [wsrc("w_in", 0, D, C_XBC + 2048, 512)]
    for k, srcd in enumerate(res_blocks):
        buf, key = res_bufs[k]
        view = buf.rearrange("p (k c) -> p k c", k=8)
        S.op("pool", _mk("dma_start", out=view, in_=srcd["ap"]), w=[key], dma="pw%d" % k)
        prew[srcd["bid"]] = (view, key)
        if srcd["bid"] not in scr:
            scr[srcd["bid"]] = len(scr)
            S.op("sp", _mk("dma_start", out=wscr.ap()[scr[srcd["bid"]], :, :], in_=buf), r=[key], w=[("scr", scr[srcd["bid"]])], dma="psv%d" % k)
    for su in range(npre):
        r0 = su * 128 * NTP
        supertile("p", NTP, dr["xprev"][r0:r0 + 128 * NTP, :], None, False, pre=True, flagidx=su)
    ws["ph"] = [S.placeholder() for _ in range(NSLOT)]
    S.op("act", _mk("activation", out=Sb[0][:], in_=Sf[0][:], func=AF.Copy), r=[("Sf", 0)] + [("Sfg", g) for g in range(4)], w=[("Sb", 0), ("Sf", 0)])
    dbg("hinit", Sf[0][:], [("Sf", 0)], None)
    for su in range(nsup):
        r0 = su * 128 * NTP
        supertile("p", NTP, dr["xp"][r0:r0 + 128 * NTP, :], dr["yp"][r0:r0 + 128 * NTP, :], su == 0)
    S.op("sp", _mk("dma_start", out=dr["ssmp"][:, :], in_=Sf[0][:]), r=[("Sf", 0)], dma="o")
    S.op("sp", _mk("dma_start", out=dr["convp"][:, :], in_=histb[:].rearrange("p a b -> p (a b)")), r=[("histb", ct) for ct in range(24)], dma="o")
    nops = S.emit(nc, es)
    es.close()
    return nc, nops, dbg_names


def _consts():
    c = np.zeros((NCST, 128, 128), np.float32)
    idx = np.arange(128)
    k = idx[:, None]; i = idx[None, :]
    same = (k // 64) == (i // 64)
    c[K_ID] = np.eye(128)
    c[K_MC_P] = (k <= i)
    c[K_MC_S] = (k <= i) & same
    c[K_LM_P] = (k > i)
    c[K_LM_S] = (k > i) & same
    c[K_ONES] = 1.0
    c[K_SS0] = (k < 64) * np.ones((1, 128))
    c[K_SS1] = (k >= 64) * np.ones((1, 128))
    c[K_GM_P] = (k // 64) <= (i // 64)
    c[K_GM_S] = same
    c[K_CM0] = np.ones((128, 1)) * (i < 64)
    c[K_CM1] = np.ones((128, 1)) * (i >= 64)
    c[K_RM][:, 0] = (idx < 64)
    c[K_RM][:, 1] = (idx >= 64)
    return np.ascontiguousarray(c.transpose(1, 0, 2).reshape(128, NCST * 128))


_CACHE = {}


def kernel(**inp):
    f = lambda a: np.ascontiguousarray(np.asarray(a, dtype=np.float32))
    xpr = f(inp["x_prompt"]); xsm = f(inp["x_sample"])
    cache = f(inp["cache_conv"])[0]; state = f(inp["state_ssm"])[0]
    if "nc" not in _CACHE:
        _CACHE["nc"], _, _ = build_program()
    nc = _CACHE["nc"]
    shared = {
        "w_in": f(inp["w_in"])[0], "wa": f(inp["w_branch_a"])[0], "wb": f(inp["w_branch_b"])[0],
        "wo": f(inp["w_out"])[0], "wup": f(inp["w_up"])[0], "wdn": f(inp["w_down"])[0],
        "pre_mix_w": f(inp["pre_mix_w"]), "ln_w": f(inp["gmlp_ln_w"]), "ln_b": f(inp["gmlp_ln_b"]),
        "post_mix_w": f(inp["post_mix_w"]), "pre_ffn_w": f(inp["pre_ffn_w"]), "post_ffn_w": f(inp["post_ffn_w"]),
        "nw_l": np.ascontiguousarray(f(inp["ssm_norm_w"]).reshape(16, 128).T), "a_log": f(inp["a_log"]), "dt_bias": f(inp["dt_bias"]), "d_skip": f(inp["d_skip"]),
        "cst": _consts(),
    }
    ws = f(inp["gmlp_ws"])[0]
    bs = f(inp["gmlp_bs"])[0]
    shared["wsT_p"] = np.ascontiguousarray(ws.transpose(2, 0, 1).reshape(128, 1024))
    ws_s = np.tile(ws[:, :64, :64], (1, 2, 2))
    shared["wsT_s"] = np.ascontiguousarray(ws_s.transpose(2, 0, 1).reshape(128, 1024))
    shared["bs_p"] = np.ascontiguousarray(bs.reshape(1, 1024))
    shared["bs_s"] = np.ascontiguousarray(np.tile(bs[:, :64], (1, 2)).reshape(1, 1024))
    cw = f(inp["conv_w"])[0]
    shared["convw_l"] = np.ascontiguousarray(cw.reshape(4, 24, 128).transpose(2, 1, 0).reshape(128, 96))
    shared["convb_l"] = np.ascontiguousarray(f(inp["conv_b"])[0].reshape(24, 128).T)
    in_maps = []
    for c in range(NCORE):
        b, q = c // 4, c % 4
        m = dict(shared)
        m["xp"] = np.ascontiguousarray(xpr[b, q * PT:(q + 1) * PT])
        xh = np.zeros((4, D), np.float32)
        if q > 0:
            xh[0:3] = xpr[b, q * PT - 3:q * PT]
        m["xh"] = xh
        m["xsm"] = np.ascontiguousarray(xsm[2 * c:2 * c + 2].reshape(128, D))
        cc = cache[2 * c:2 * c + 2]
        m["cachel"] = np.ascontiguousarray(cc.reshape(2, 3, 24, 128).transpose(3, 0, 2, 1).reshape(128, 144))
        st = state[2 * c:2 * c + 2]
        m["stateT"] = np.ascontiguousarray(st.reshape(2, 2048, 128).transpose(0, 2, 1))
        xprev = np.zeros((3 * PT, D), np.float32)
        if q > 0:
            xprev[(3 - q) * PT:] = xpr[b, 0:q * PT]
        m["xprev"] = xprev
        pf = np.zeros((1, 3 * NSUP), np.float32)
        pf[0, (3 - q) * NSUP:] = 1.0
        m["pflag"] = pf
        in_maps.append(m)
    res = run_bass_kernel_spmd(nc, in_maps, core_ids=list(range(NCORE)))
    R = res.results
    yp = np.stack([np.concatenate([R[b * 4 + q]["yp"] for q in range(4)], axis=0) for b in range(2)])
    ys = np.concatenate([R[c]["ys"].reshape(2, 64, D) for c in range(NCORE)], axis=0)
    def conv_back(a, n):
        return a.reshape(128, n, 24, 3).transpose(1, 3, 2, 0).reshape(n, 3, 3072)
    def ssm_back(a):
        return a.T.reshape(32, 64, 128)
    convp = np.stack([conv_back(R[b * 4 + 3]["convp"], 1)[0] for b in range(2)])[None]
    ssmp = np.stack([ssm_back(R[b * 4 + 3]["ssmp"]) for b in range(2)])[None]
    convs = np.concatenate([conv_back(R[c]["convs"], 2) for c in range(NCORE)], axis=0)[None]
    ssms = np.stack([ssm_back(R[c]["ssms"][s]) for c in range(NCORE) for s in range(2)])[None]
    vs = np.concatenate([R[c]["vs"].reshape(2, 64, D) for c in range(NCORE)], axis=0)[None]
    out = (yp, ys, convp, ssmp, convs, ssms, vs)
    return tuple(np.ascontiguousarray(o, dtype=np.float32) for o in out)
```

```python
import numpy as np
from contextlib import ExitStack
import concourse.bass as bass
import concourse.mybir as mybir
from concourse.bass_utils import run_bass_kernel_spmd

F32 = mybir.dt.float32
BF16 = mybir.dt.bfloat16
AF = mybir.ActivationFunctionType
ALU = mybir.AluOpType

NCORE = 8
D = 1024
PT = 2048
NTP = 2
NSUP = PT // (128 * NTP)
EPS = 1e-6
NSLOT = 4
C_U, C_V, C_Z, C_XBC, C_DT, C_GA, C_GB = 0, 1024, 2048, 4096, 7168, 7200, 8224
(K_ID, K_MC_P, K_MC_S, K_LM_P, K_LM_S, K_ONES, K_SS0, K_SS1, K_GM_P, K_GM_S, K_CM0, K_CM1,
 K_RM) = range(13)
NCST = 13


def _merge(a, b):
    out = []; ia = ib = 0
    while ia < len(a) or ib < len(b):
        if ib >= len(b) or (ia < len(a) and ia * len(b) <= ib * len(a)):
            out.append(a[ia]); ia += 1
        else:
            out.append(b[ib]); ib += 1
    return out


def _mk(name, *a, **kw):
    return lambda e: getattr(e, name)(*a, **kw)


class Sched:
    def __init__(self):
        self.ops = []

    def op(self, eng, fn, r=(), w=(), dma=None):
        o = dict(eng=eng, fn=fn, r=tuple(r), w=tuple(w), dma=dma)
        self.ops.append(o)
        return o

    def placeholder(self):
        o = dict(eng=None, fn=None, r=(), w=(), dma=None)
        self.ops.append(o)
        return o

    def emit(self, nc, es, final_eng="sp"):
        ops = [o for o in self.ops if o["eng"] is not None]
        lastw, readers = {}, {}
        for i, o in enumerate(ops):
            o["id"] = i
            deps = {}
            for k in o["r"]:
                p = lastw.get(k)
                if p is not None:
                    deps[p] = "raw"
            for k in o["w"]:
                p = lastw.get(k)
                if p is not None:
                    deps.setdefault(p, "waw")
                for q in readers.get(k, ()):
                    deps.setdefault(q, "war")
            keep = {}
            for p, t in deps.items():
                po = ops[p]
                if po["dma"] is None and o["dma"] is None and po["eng"] == o["eng"]:
                    if o["eng"] == "pe" or t != "raw":
                        continue
                keep[p] = t
            o["deps"] = keep
            for k in o["r"]:
                readers.setdefault(k, set()).add(i)
            for k in o["w"]:
                lastw[k] = i
                readers[k] = set()
        need = set()
        for o in ops:
            need.update(o["deps"].keys())
        cnt = {}
        for o in ops:
            if o["dma"] is not None:
                s = "d_" + o["dma"]
                cnt[s] = cnt.get(s, 0) + 16
                o["sig"] = (s, cnt[s], 16)
            elif o["id"] in need:
                s = "e_" + o["eng"]
                cnt[s] = cnt.get(s, 0) + 1
                o["sig"] = (s, cnt[s], 1)
            else:
                o["sig"] = None
        sems = {name: es.enter_context(nc.semaphore(name)) for name in sorted(cnt)}
        block = es.enter_context(nc.Block())

        def runner(engname):
            def f(e):
                waited = {}
                for o in ops:
                    if o["eng"] != engname:
                        continue
                    req = {}
                    for p in o["deps"]:
                        s, v, _ = ops[p]["sig"]
                        if v > req.get(s, 0):
                            req[s] = v
                    for s, v in req.items():
                        if waited.get(s, 0) < v:
                            e.wait_ge(sems[s], v)
                            waited[s] = v
                    ins = o["fn"](e)
                    if o["sig"] is not None:
                        ins.then_inc(sems[o["sig"][0]], o["sig"][2])
                if engname == final_eng:
                    for s, v in cnt.items():
                        if s.startswith("d_") and waited.get(s, 0) < v:
                            e.wait_ge(sems[s], v)
            return f

        block.tensor(runner("pe"))
        block.scalar(runner("act"))
        block.vector(runner("dve"))
        block.gpsimd(runner("pool"))
        block.sync(runner("sp"))
        return len(ops)


def build_program(nsup=NSUP, debug=False, npre=3 * NSUP):
    nc = bass.Bass("TRN2", target_bir_lowering=False)
    S = Sched()
    es = ExitStack()
    dr = {}

    def din(name, shape):
        dr[name] = nc.dram_tensor(name, list(shape), F32, kind="ExternalInput").ap()

    def dout(name, shape):
        dr[name] = nc.dram_tensor(name, list(shape), F32, kind="ExternalOutput").ap()

    din("xp", [PT, D]); din("xh", [4, D]); din("xsm", [128, D])
    din("cachel", [128, 2 * 24 * 3]); din("stateT", [2, 128, 2048])
    din("w_in", [D, 9248]); din("wa", [D, D]); din("wb", [2048, D]); din("wo", [D, D])
    din("wup", [D, 4096]); din("wdn", [4096, D])
    for nm in ("pre_mix_w", "ln_w", "ln_b", "post_mix_w", "pre_ffn_w", "post_ffn_w", "bs_p", "bs_s"):
        din(nm, [1, D])
    din("nw_l", [128, 16])
    for nm in ("a_log", "dt_bias", "d_skip"):
        din(nm, [1, 32])
    din("convw_l", [128, 96]); din("convb_l", [128, 24])
    din("xprev", [3 * PT, D]); din("pflag", [1, 3 * NSUP])
    din("cst", [128, NCST * 128]); din("wsT_p", [128, 1024]); din("wsT_s", [128, 1024])
    dout("yp", [PT, D]); dout("ys", [128, D]); dout("convp", [128, 72]); dout("ssmp", [128, 2048])
    dout("convs", [128, 144]); dout("ssms", [2, 128, 2048]); dout("vs", [128, D])

    def sb(name, shape, dt=F32):
        return es.enter_context(nc.sbuf_tensor("s_" + name, list(shape), dt))

    TTM = 128 * NTP
    dbg_names = []

    def dbg(name, ap, keys, shape):
        if not debug:
            return
        d = nc.dram_tensor("dbg_" + name, [int(v) for v in ap.shape], F32, kind="ExternalOutput").ap()
        dbg_names.append("dbg_" + name)
        S.op("pool", _mk("dma_start", out=d, in_=ap), r=keys, dma="dbg")
    cst = sb("cst", [128, NCST, 128]); cstb = sb("cstb", [128, NCST, 128], BF16)
    bc = {nm: sb("bc_" + nm, [128, D]) for nm in ("pre_mix_w", "ln_w", "ln_b", "post_mix_w", "pre_ffn_w", "post_ffn_w")}
    bsb1 = sb("bsb", [128, 8, 128]); bsb = {"p": bsb1, "s": bsb1}
    nwc = sb("nwc", [128, 16])
    alog = sb("alog", [128, 32]); abc = sb("abc", [128, 32]); dtb = sb("dtb", [128, 32]); dsk = sb("dsk", [128, 32])
    convw = sb("convw", [128, 24, 4]); convb = sb("convb", [128, 24])
    neghalf = sb("neghalf", [128, 2])
    wsT = {"p": sb("wsT_p", [128, 8, 128], BF16), "s": sb("wsT_s", [128, 8, 128], BF16)}
    wdt = sb("wdt", [128, 8, 32], BF16)
    cachet = sb("cachet", [128, 2, 24, 3]); cso = sb("cso", [128, 2, 24, 3])
    histb = sb("histb", [128, 24, 3])
    pflagb = sb("pflagb", [128, 3 * NSUP])
    Sf0 = sb("Sf0", [128, 2048])
    Sb = [sb("Sb0", [128, 2048], BF16), sb("Sb1", [128, 2048], BF16)]
    slots = [sb("wslot%d" % i, [128, 4096], BF16) for i in range(NSLOT)]
    xres = sb("xres", [128, NTP, D])
    hT = sb("hT", [128, 8, TTM + 4], BF16)
    uT = sb("uT", [128, 8, TTM], BF16)
    vnb = sb("vnb", [128, NTP, D], BF16)
    oaT = sb("oaT", [128, 8, TTM], BF16)
    PA = sb("PA", [128, 8, TTM], BF16)
    xs_ = sb("xs", [128, NTP, 2048])
    Sf = [Sf0[:, :], xs_[:, 1, :]]
    Btok = sb("Btok", [128, NTP, 512], BF16)
    BT = sb("BT", [128, 4, TTM], BF16); CT = sb("CT", [128, 4, TTM], BF16)
    obT = sb("obT", [128, 16, TTM], BF16)
    mT = uT
    hidT_full = xs_[:].rearrange("p a b -> p (a b)").bitcast(BF16).rearrange("p (k c) -> p k c", k=32)
    hidT_small = xs_[:, 0, :].bitcast(BF16).rearrange("p (k c) -> p k c", k=32)
    hnbs = [sb("hnb", [128, D], BF16), sb("hnb2", [128, D], BF16)]
    vg = sb("vg", [128, D]); vn32 = sb("vn32", [128, D])
    wsraw = vg[:].rearrange("p (g c) -> p g c", g=8)
    junk = vn32
    stat = sb("stat", [128, 2, 6]); mv = sb("mv", [128, 2]); stat2 = sb("stat2", [128, 2, 6]); mv2 = sb("mv2", [128, 2])
    sc = sb("sc", [128, 16])
    stage4 = sb("stage", [128, 4, TTM + 8]); cacc4 = sb("cacc", [128, 4, TTM])
    xcT = sb("xcT", [128, 4, TTM])
    dts = sb("dts", [128, NTP, 8, 32])
    dhi = sb("dhi", [128, NTP, 32], BF16); dlo = sb("dlo", [128, NTP, 32], BF16)
    decS = sb("decS", [128, NTP, 64])
    xdt = sb("xdt", [128, 512], BF16); xw = sb("xw", [128, 512], BF16)
    Bm = sb("Bm", [128, 2, 512], BF16); CTm = sb("CTm", [128, 2, 4, 128], BF16)
    cbm = sb("cbm", [128, 128])
    Rhi = sb("Rhi", [128, 8, 128], BF16); Rlo = sb("Rlo", [128, 8, 128], BF16)
    dec = sb("dec", [128, 8, 128]); Mb = sb("Mb", [128, 8, 128], BF16)
    t1 = sb("t1", [128, 512]); zs = sb("zs", [128, 512]); gnb = sb("gnb", [128, 512], BF16)
    stmp = sb("stmp", [128, 512])
    sg = sb("sg", [128, TTM]); mtmp = sb("mtmp", [128, TTM])
    rls = [sb("rl%d" % i, [128, TTM]) for i in range(3)]; fT = sb("fT", [128, TTM])
    yout = vg
    psb = [es.enter_context(nc.psum_tensor("ps%d" % i, [128, 512], F32)) for i in range(8)]
    pstate = {"n": 0}

    def bank():
        b = pstate["n"] % 8
        pstate["n"] += 1
        return b

    def PK(b):
        return ("ps", b)

    ws = {"n": 0, "ph": [S.placeholder() for _ in range(NSLOT)]}

    wscr = nc.dram_tensor("wscr", [48, 128, 4096], BF16)
    scr = {}

    def wget(srcd, kcn, cols):
        i = ws["n"]; ws["n"] += 1
        slot = i % NSLOT
        view = slots[slot][:, 0:kcn * cols].rearrange("p (k c) -> p k c", k=kcn)
        ph = ws["ph"].pop(0)
        bid = srcd["bid"]
        if bid not in scr:
            scr[bid] = len(scr)
            sv = wscr.ap()[scr[bid], :, 0:kcn * cols]
            ph.update(eng="pool", fn=_mk("dma_start", out=view, in_=srcd["ap"]), r=(), w=(("wslot", slot),), dma="w%d" % slot)
            S.op("sp", _mk("dma_start", out=sv, in_=slots[slot][:, 0:kcn * cols]), r=[("wslot", slot)], w=[("scr", scr[bid])], dma="sv%d" % slot)
        else:
            sv = wscr.ap()[scr[bid], :, 0:kcn * cols]
            ph.update(eng="sp", fn=_mk("dma_start", out=slots[slot][:, 0:kcn * cols], in_=sv), r=(("scr", scr[bid]),), w=(("wslot", slot),), dma="w%d" % slot)
        return view, ("wslot", slot)

    def wdone():
        ws["ph"].append(S.placeholder())

    def wsrc(name, r0, nr, c0, ncol):
        return dict(bid=(name, c0), ap=dr[name][r0:r0 + nr, c0:c0 + ncol].rearrange("(k p) c -> p k c", p=128))

    S.op("sp", _mk("dma_start", out=cst[:].rearrange("p k c -> p (k c)"), in_=dr["cst"][:, :]), w=["cst"], dma="c11")
    S.op("pool", _mk("dma_start", out=cstb[:].rearrange("p k c -> p (k c)"), in_=dr["cst"][:, :]), w=["cstb"], dma="c12")
    for nm in bc:
        S.op("sp", _mk("dma_start", out=bc[nm][:], in_=dr[nm].partition_broadcast(128)), w=["bc_" + nm], dma="c13_" + nm)
    S.op("sp", _mk("dma_start", out=bsb["s"][:].rearrange("p g c -> p (g c)"), in_=dr["bs_s"].partition_broadcast(128)), w=["bsb"], dma="c14")
    S.op("sp", _mk("dma_start", out=nwc[:], in_=dr["nw_l"][:, :]), w=["nwc"], dma="c15")
    S.op("sp", _mk("dma_start", out=alog[:], in_=dr["a_log"].partition_broadcast(128)), w=["alog"], dma="c16")
    S.op("sp", _mk("dma_start", out=dtb[:], in_=dr["dt_bias"].partition_broadcast(128)), w=["dtb"], dma="c17")
    S.op("sp", _mk("dma_start", out=dsk[:], in_=dr["d_skip"].partition_broadcast(128)), w=["dsk"], dma="c18")
    S.op("sp", _mk("dma_start", out=convw[:].rearrange("p a b -> p (a b)"), in_=dr["convw_l"][:, :]), w=["convw"], dma="c19")
    S.op("sp", _mk("dma_start", out=convb[:], in_=dr["convb_l"][:, :]), w=["convb"], dma="c20")
    S.op("sp", _mk("dma_start", out=cachet[:].rearrange("p a b c -> p (a b c)"), in_=dr["cachel"][:, :]), w=["cachet"], dma="c21")
    S.op("pool", _mk("dma_start", out=wdt[:], in_=wsrc("w_in", 0, D, C_DT, 32)["ap"]), w=["wdt"], dma="c22")
    S.op("dve", _mk("memset", neghalf[:], -0.5), w=["neghalf"])
    S.op("act", _mk("activation", out=abc[:], in_=alog[:], func=AF.Exp), r=["alog"], w=["abc"])
    S.op("dve", _mk("tensor_scalar", out=abc[:], in0=abc[:], scalar1=-1.0, scalar2=None, op0=ALU.mult), r=["abc"], w=["abc"])
    for kind, gm in (("p", K_GM_P), ("s", K_GM_S)):
        S.op("sp", _mk("dma_start", out=vg[:], in_=dr["wsT_" + kind][:, :]), w=["vg"], dma="c23")
        S.op("dve", _mk("tensor_tensor",
            out=wsT[kind][:], in0=wsraw, in1=cst[:, gm, :].unsqueeze(1).broadcast_to([128, 8, 128]), op=ALU.mult),
            r=["vg", "cst"], w=["wsT_" + kind])
    for s_ in range(2):
        S.op("sp", _mk("dma_start", out=Sf[s_][:], in_=dr["stateT"][s_, :, :]), w=[("Sf", s_)], dma="c24_%d" % s_)
        S.op("act", _mk("activation", out=Sb[s_][:], in_=Sf[s_][:], func=AF.Copy), r=[("Sf", s_)], w=[("Sb", s_)])

    def rstd_from(ss_ap, n, key):
        S.op("dve", _mk("tensor_scalar", out=ss_ap, in0=ss_ap, scalar1=1.0 / n, scalar2=EPS, op0=ALU.mult, op1=ALU.add), r=[key], w=[key])
        S.op("pool", _mk("tensor_tensor", out=ss_ap, in0=ss_ap, in1=neghalf[:, 0:1], op=ALU.pow), r=[key, "neghalf"], w=[key])

    def mm_group(out_ap, pairs, r, w):
        def fn(e):
            ins = None
            n = len(pairs)
            for i, (l, rr) in enumerate(pairs):
                ins = e.matmul(out_ap, lhsT=l, rhs=rr, start=(i == 0), stop=(i == n - 1))
            return ins
        S.op("pe", fn, r=r, w=w)

    def tr_group(items, r, w):
        def fn(e):
            ins = None
            for (o, i_, idn) in items:
                ins = e.transpose(o, i_, idn)
            return ins
        S.op("pe", fn, r=r, w=w)

    def norm_to_hT(src_ap, srckey, wname, t, ncols_valid=128, par=0):
        hnb = hnbs[par]; hk = ("hnb", par); sck = ("scn", par)
        scc = sc[:, 12 + par:13 + par]
        S.op("act", _mk("activation", out=hnb[:], in_=src_ap, func=AF.Square, accum_out=scc), r=[srckey], w=[hk, sck])
        rstd_from(scc, D, sck)
        S.op("dve", _mk("scalar_tensor_tensor", out=hnb[:], in0=src_ap, scalar=scc, in1=bc[wname][:], op0=ALU.mult, op1=ALU.mult),
             r=[srckey, sck, "bc_" + wname], w=[hk])
        b = bank()
        pb = psb[b][:].bitcast(BF16)
        nv = ncols_valid
        tr_group([(pb[:, kc * 128:kc * 128 + nv], hnb[0:nv, kc * 128:(kc + 1) * 128], cstb[0:nv, K_ID, 0:nv]) for kc in range(8)],
                 r=[hk, "cstb"], w=[PK(b)])
        c0 = t * 128
        S.op("act", _mk("activation", out=hT[:, :, c0:c0 + nv], in_=pb.rearrange("p (k c) -> p k c", k=8)[:, :, 0:nv], func=AF.Copy),
             r=[PK(b)], w=["hT"])

    def supertile(kind, NT, x_src, y_dst, first_prompt, pre=False, flagidx=None):
        TT = 128 * NT
        halo = 4 if first_prompt else 0
        hidT = hidT_full if NT == NTP else hidT_small

        def getw(srcd, kcn, cols):
            if pre:
                return prew[srcd["bid"]]
            return wget(srcd, kcn, cols)

        def donew():
            if not pre:
                wdone()
        NX = TT + halo
        if kind == "p":
            segs = [dict(sel=K_ONES, cm=None, rm=None, st=0)]
            mc, lm = K_MC_P, K_LM_P
            nsegc, L = 1, TT
        else:
            segs = [dict(sel=K_SS0, cm=K_CM0, rm=0, st=0), dict(sel=K_SS1, cm=K_CM1, rm=1, st=1)]
            mc, lm = K_MC_S, K_LM_S
            nsegc, L = 2, 64
        recs = []
        for t in range(NT):
            n0_ = len(S.ops)
            S.op("sp", _mk("dma_start", out=xres[:, t, :], in_=x_src[t * 128:(t + 1) * 128, :]), w=[("xres", t)], dma="x%d" % t)
            norm_to_hT(xres[:, t, :], ("xres", t), "pre_mix_w", t, par=t % 2)
            recs.append(S.ops[n0_:]); del S.ops[n0_:]
        S.ops.extend(_merge(recs[0], recs[1]) if NT == 2 else recs[0])
        if first_prompt:
            S.op("sp", _mk("dma_start", out=vg[0:4, :], in_=dr["xh"][:, :]), w=["vg"], dma="xh")
            norm_to_hT(vg[:, :], "vg", "pre_mix_w", NT, ncols_valid=4)
        for blk in (range(2) if not pre else ()):
            wv, wk = wget(wsrc("w_in", 0, D, C_U + blk * 512, 512), 8, 512)
            for sub in range(4):
                cb = blk * 4 + sub
                b = bank()
                mm_group(psb[b][:, 0:TT], [(wv[:, kc, sub * 128:(sub + 1) * 128], hT[:, kc, 0:TT]) for kc in range(8)],
                         r=[wk, "hT"], w=[PK(b)])
                S.op("act", _mk("activation", out=uT[:, cb, 0:TT], in_=psb[b][:, 0:TT], func=AF.Gelu_apprx_tanh),
                     r=[PK(b)], w=["uT"])
            wdone()
        vblk = [wget(wsrc("w_in", 0, D, C_V + blk * 512, 512), 8, 512) for blk in (range(2) if not pre else ())]
        recs = []
        for t in (range(NT) if not pre else ()):
            n0_ = len(S.ops)
            p_ = t % 2
            G, gk = ((vg, "vg"), (vn32, "vn32"))[p_]
            st_, stk = ((stat, "stat"), (stat2, "stat2"))[p_]
            mv_, mvk = ((mv, "mv"), (mv2, "mv2"))[p_]
            sp_, spk = ((stmp, "stmp"), (t1, "t1"))[p_]
            for blk in range(2):
                wv, wk = vblk[blk]
                b = bank()
                mm_group(psb[b][:, :], [(hT[:, kc, t * 128:(t + 1) * 128], wv[:, kc, :]) for kc in range(8)], r=[wk, "hT"], w=[PK(b)])
                S.op("act", _mk("activation", out=G[:, blk * 512:(blk + 1) * 512], in_=psb[b][:, :], func=AF.Gelu_apprx_tanh),
                     r=[PK(b)], w=[gk])
            for blk in range(2):
                S.op("dve", _mk("bn_stats", out=st_[:, blk, :], in_=G[:, blk * 512:(blk + 1) * 512]), r=[gk], w=[stk])
            S.op("dve", _mk("bn_aggr", out=mv_[:], in_=st_[:].rearrange("p a b -> p (a b)")), r=[stk], w=[mvk])
            rstd_from(mv_[:, 1:2], 1.0, mvk)
            S.op("dve", _mk("tensor_scalar", out=G[:], in0=G[:], scalar1=mv_[:, 0:1], scalar2=mv_[:, 1:2], op0=ALU.subtract, op1=ALU.mult),
                 r=[gk, mvk], w=[gk])
            S.op("dve", _mk("tensor_tensor", out=G[:], in0=G[:], in1=bc["ln_w"][:], op=ALU.mult), r=[gk, "bc_ln_w"], w=[gk])
            S.op("dve", _mk("tensor_tensor", out=G[:], in0=G[:], in1=bc["ln_b"][:], op=ALU.add), r=[gk, "bc_ln_b"], w=[gk])
            S.op("act", _mk("activation", out=vnb[:, t, :], in_=G[:], func=AF.Copy), r=[gk], w=[("vnb", t)])
            if kind == "s":
                S.op("sp", _mk("dma_start", out=dr["vs"][:, :], in_=G[:]), r=[gk], dma="o")
            for half in range(2):
                b = bank()
                def fn(e, b=b, half=half, t=t):
                    ins = None
                    for gi in range(4):
                        g = half * 4 + gi
                        ins = e.matmul(psb[b][:, gi * 128:(gi + 1) * 128], lhsT=vnb[:, t, g * 128:(g + 1) * 128], rhs=wsT[kind][:, g, :], start=True, stop=True)
                    return ins
                S.op("pe", fn, r=[("vnb", t), "wsT_" + kind], w=[PK(b)])
                S.op("dve", _mk("tensor_tensor",
                    out=sp_[:].rearrange("p (g c) -> p g c", g=4), in0=psb[b][:, :].rearrange("p (g c) -> p g c", g=4),
                    in1=bsb[kind][:, half * 4:half * 4 + 4, :], op=ALU.add), r=[PK(b), "bsb"], w=[spk])
                S.op("dve", _mk("tensor_tensor",
                    out=oaT[:, half * 4:half * 4 + 4, t * 128:(t + 1) * 128], in0=sp_[:].rearrange("p (g c) -> p g c", g=4),
                    in1=uT[:, half * 4:half * 4 + 4, t * 128:(t + 1) * 128], op=ALU.mult), r=[spk, "uT"], w=[("oaT", t)])
            recs.append(S.ops[n0_:]); del S.ops[n0_:]
        if not pre:
            S.ops.extend(_merge(recs[0], recs[1]) if NT == 2 else recs[0])
        if not pre:
            wdone(); wdone()
        for blk in (range(2) if not pre else ()):
            wa_v, wa_k = wget(wsrc("wa", 0, D, blk * 512, 512), 8, 512)
            ga_v, ga_k = wget(wsrc("w_in", 0, D, C_GA + blk * 512, 512), 8, 512)
            for sub in range(4):
                fb = blk * 4 + sub
                b1 = bank(); b2 = bank()
                mm_group(psb[b1][:, 0:TT], [(ga_v[:, kc, sub * 128:(sub + 1) * 128], hT[:, kc, 0:TT]) for kc in range(8)], r=[ga_k, "hT"], w=[PK(b1)])
                mm_group(psb[b2][:, 0:TT], [(wa_v[:, kc, sub * 128:(sub + 1) * 128], oaT[:, kc, 0:TT]) for kc in range(8)], r=[wa_k] + [("oaT", tt_) for tt_ in range(NT)], w=[PK(b2)])
                S.op("act", _mk("activation", out=sg[:, 0:TT], in_=psb[b1][:, 0:TT], func=AF.Sigmoid), r=[PK(b1)], w=["sg"])
                S.op("dve", _mk("tensor_tensor", out=PA[:, fb, 0:TT], in0=sg[:, 0:TT], in1=psb[b2][:, 0:TT], op=ALU.mult),
                     r=["sg", PK(b2)], w=["PA"])
            wdone(); wdone()
        recs = []
        for t in range(NT):
            n0_ = len(S.ops)
            b = bank()
            mm_group(psb[b][:, 0:32], [(hT[:, kc, t * 128:(t + 1) * 128], wdt[:, kc, :]) for kc in range(8)], r=["hT", "wdt"], w=[PK(b)])
            dk = ("dts", t)
            A = lambda i, t=t: dts[:, t, i, :]
            S.op("dve", _mk("tensor_tensor", out=dts[:, t, 6, :], in0=psb[b][:, 0:32], in1=dtb[:], op=ALU.add), r=[PK(b), "dtb"], w=[dk])
            S.op("act", _mk("activation", out=dts[:, t, 7, :], in_=dts[:, t, 6, :], func=AF.Abs), r=[dk], w=[dk])
            S.op("act", _mk("activation", out=dts[:, t, 7, :], in_=dts[:, t, 7, :], func=AF.Exp, scale=-1.0), r=[dk], w=[dk])
            S.op("act", _mk("activation", out=dts[:, t, 7, :], in_=dts[:, t, 7, :], func=AF.Ln, bias=1.0), r=[dk], w=[dk])
            S.op("dve", _mk("scalar_tensor_tensor", out=dts[:, t, 0, :], in0=dts[:, t, 6, :], scalar=0.0, in1=dts[:, t, 7, :], op0=ALU.max, op1=ALU.add), r=[dk], w=[dk])
            S.op("dve", _mk("tensor_tensor", out=dts[:, t, 1, :], in0=dts[:, t, 0, :], in1=abc[:], op=ALU.mult), r=[dk, "abc"], w=[dk])
            S.op("dve", _mk("tensor_copy", out=dhi[:, t, :], in_=dts[:, t, 1, :]), r=[dk], w=[("dhi", t)])
            S.op("dve", _mk("tensor_tensor", out=dlo[:, t, :], in0=dts[:, t, 1, :], in1=dhi[:, t, :], op=ALU.subtract), r=[dk, ("dhi", t)], w=[("dlo", t)])
            b2 = bank()
            def fn(e, b2=b2, t=t):
                ins = e.matmul(psb[b2][:, 0:32], lhsT=cst[:, mc, :], rhs=dts[:, t, 1, :], start=True, stop=True)
                for si, sgm in enumerate(segs):
                    ins = e.matmul(psb[b2][:, 32 + 32 * si:64 + 32 * si], lhsT=cst[:, sgm["sel"], :], rhs=dts[:, t, 1, :], start=True, stop=True)
                return ins
            S.op("pe", fn, r=[dk, "cst"], w=[PK(b2)])
            S.op("act", _mk("activation", out=dts[:, t, 2, :], in_=psb[b2][:, 0:32], func=AF.Copy), r=[PK(b2)], w=[dk])
            S.op("act", _mk("activation", out=dts[:, t, 3, :], in_=psb[b2][:, 0:32], func=AF.Exp), r=[PK(b2)], w=[dk])
            S.op("act", _mk("activation", out=decS[:, t, 0:32 * len(segs)], in_=psb[b2][:, 32:32 + 32 * len(segs)], func=AF.Exp), r=[PK(b2)], w=[("decS", t)])
            if kind == "p":
                S.op("dve", _mk("tensor_tensor", out=dts[:, t, 4, :], in0=psb[b2][:, 32:64], in1=dts[:, t, 2, :], op=ALU.subtract), r=[PK(b2), dk], w=[dk])
            else:
                for si in range(2):
                    S.op("dve", _mk("tensor_tensor",
                        out=dts[64 * si:64 * si + 64, t, 4, :], in0=psb[b2][64 * si:64 * si + 64, 32 + 32 * si:64 + 32 * si],
                        in1=dts[64 * si:64 * si + 64, t, 2, :], op=ALU.subtract), r=[PK(b2), dk], w=[dk])
            S.op("act", _mk("activation", out=dts[:, t, 4, :], in_=dts[:, t, 4, :], func=AF.Exp), r=[dk], w=[dk])
            S.op("dve", _mk("tensor_tensor", out=dts[:, t, 5, :], in0=dts[:, t, 4, :], in1=dts[:, t, 0, :], op=ALU.mult), r=[dk], w=[dk])
            if pre:
                S.op("dve", _mk("tensor_scalar", out=dts[:, t, 5, :], in0=dts[:, t, 5, :], scalar1=pflagb[:, flagidx:flagidx + 1], scalar2=None, op0=ALU.mult),
                     r=[dk, "pflagb"], w=[dk])
            recs.append(S.ops[n0_:]); del S.ops[n0_:]
        S.ops.extend(_merge(recs[0], recs[1]) if NT == 2 else recs[0])

        if kind == "s":
            dbg("dts", dts[:, 0, :, :].rearrange("p a b -> p (a b)"), [("dts", 0)], [128, 256])
            dbg("decS", decS[:, 0, :], [("decS", 0)], [128, 64])
            dbg("hT", hT[:, :, 0:128], ["hT"], [128, 1024])
        alt_stage = [(t1, "t1"), (zs, "zs"), (stmp, "stmp"), (vg, "vg")]
        alt_cacc = [(sg, "sg"), (mtmp, "mtmp"), (rls[0], ("rl", 0)), (rls[1], ("rl", 1))]

        def conv_block(cts, wv, wk, outs, alt=False):
            bks = [bank() for _ in cts]
            for i, ct in enumerate(cts):
                mm_group(psb[bks[i]][:, 0:NX], [(wv[:, kc, i * 128:(i + 1) * 128], hT[:, kc, 0:NX]) for kc in range(8)], r=[wk, "hT"], w=[PK(bks[i])])
            st3s = []
            sks = []
            for i, ct in enumerate(cts):
                b = bks[i]
                stage = alt_stage[i][0][:, 0:TTM + 8] if alt else stage4[:, i, :]
                sk = alt_stage[i][1] if alt else ("stage", i)
                sks.append(sk)
                st3 = stage[:, 0:nsegc * (L + 3)].rearrange("p (s c) -> p s c", s=nsegc)
                st3s.append(st3)
                if first_prompt:
                    S.op("act", _mk("activation", out=stage[:, 0:3], in_=psb[b][:, TT:TT + 3], func=AF.Copy), r=[PK(b)], w=[sk])
                elif kind == "p":
                    S.op("pool", _mk("tensor_copy", out=stage[:, 0:3], in_=histb[:, ct, :]), r=[("histb", ct)], w=[sk])
                else:
                    S.op("pool", _mk("tensor_copy", out=st3[:, :, 0:3], in_=cachet[:, :, ct, :]), r=["cachet"], w=[sk])
                S.op("act", _mk("activation", out=st3[:, :, 3:3 + L], in_=psb[b][:, 0:TT].rearrange("p (s c) -> p s c", s=nsegc), func=AF.Copy),
                     r=[PK(b)], w=[sk])
                if kind == "p":
                    S.op("pool", _mk("tensor_copy", out=histb[:, ct, :], in_=stage[:, L:L + 3]), r=[sk], w=[("histb", ct)])
                else:
                    S.op("pool", _mk("tensor_copy", out=cso[:, :, ct, :], in_=st3[:, :, L:L + 3]), r=[sk], w=["cso"])
            caps = [(alt_cacc[i][0][:, 0:TT] if alt else cacc4[:, i, 0:TT]) for i in range(len(cts))]
            cks = [(alt_cacc[i][1] if alt else ("cacc", i)) for i in range(len(cts))]
            ca3s = [c_.rearrange("p (s c) -> p s c", s=nsegc) for c_ in caps]
            for i, ct in enumerate(cts):
                S.op("dve", _mk("tensor_scalar", out=ca3s[i], in0=st3s[i][:, :, 0:L], scalar1=convw[:, ct, 0:1], scalar2=convb[:, ct:ct + 1], op0=ALU.mult, op1=ALU.add),
                     r=[sks[i], "convw", "convb"], w=[cks[i]])
            for k in range(1, 4):
                for i, ct in enumerate(cts):
                    S.op("dve", _mk("scalar_tensor_tensor", out=ca3s[i], in0=st3s[i][:, :, k:k + L], scalar=convw[:, ct, k:k + 1], in1=ca3s[i], op0=ALU.mult, op1=ALU.add),
                         r=[sks[i], cks[i], "convw"], w=[cks[i]])
            for i, ct in enumerate(cts):
                S.op("act", _mk("activation", out=outs[i][0], in_=caps[i], func=AF.Silu), r=[cks[i]], w=[outs[i][1]])

        if pre:
            xcT_alt = dec[:].rearrange("p a b -> p (a b)").rearrange("p (s c) -> p s c", s=4)
            blocks = [("B", None)] + [("x", g) for g in range(4)]

            def conv_of(k):
                kind_, g = blocks[k]
                alt = (k % 2 == 1)
                if kind_ == "B":
                    wv, wk = getw(wsrc("w_in", 0, D, C_XBC + 2048, 512), 8, 512)
                    conv_block([16 + gg for gg in range(4)], wv, wk, [(BT[:, gg, 0:TT], "BT") for gg in range(4)], alt=alt)
                else:
                    wv, wk = getw(wsrc("w_in", 0, D, C_XBC + g * 512, 512), 8, 512)
                    xo, xk = (xcT_alt, "dec") if alt else (xcT, "xcT")
                    conv_block([g * 4 + sub for sub in range(4)], wv, wk, [(xo[:, sub, 0:TT], xk) for sub in range(4)], alt=alt)

            def post_of(k):
                kind_, g = blocks[k]
                alt = (k % 2 == 1)
                if kind_ == "B":
                    for t in range(NT):
                        b = bank()
                        pb = psb[b][:].bitcast(BF16)
                        tr_group([(pb[:, gg * 128:(gg + 1) * 128], BT[:, gg, t * 128:(t + 1) * 128], cstb[:, K_ID, :]) for gg in range(4)], r=["BT", "cstb"], w=[PK(b)])
                        S.op("act", _mk("activation", out=Btok[:, t, :], in_=pb[:, 0:512], func=AF.Copy), r=[PK(b)], w=[("Btok", t)])
                    return
                xo, xk = (xcT_alt, "dec") if alt else (xcT, "xcT")
                xwb, xwk = (xdt, "xdt") if alt else (xw, "xw")
                gc = slice(g * 512, (g + 1) * 512)
                for t in range(NT):
                    dk = ("dts", t)
                    b = bank()
                    tr_group([(psb[b][:, sub * 128:(sub + 1) * 128], xo[:, sub, t * 128:(t + 1) * 128], cst[:, K_ID, :]) for sub in range(4)],
                             r=[xk, "cst"], w=[PK(b)])
                    S.op("dve", _mk("tensor_tensor", out=xwb[:].rearrange("p (h c) -> p h c", h=8), in0=psb[b][:, :].rearrange("p (h c) -> p h c", h=8),
                         in1=dts[:, t, 5, g * 8:(g + 1) * 8].unsqueeze(2).broadcast_to([128, 8, 64]), op=ALU.mult), r=[PK(b), dk], w=[xwk])
                    bst = bank()
                    mm_group(psb[bst][:, :], [(Btok[:, t, g * 128:(g + 1) * 128], xwb[:, :])], r=[("Btok", t), xwk], w=[PK(bst)])
                    S.op("dve", _mk("tensor_tensor",
                        out=Sf[0][:, gc].rearrange("p (h c) -> p h c", h=8), in0=Sf[0][:, gc].rearrange("p (h c) -> p h c", h=8),
                        in1=decS[:, t, g * 8:g * 8 + 8].unsqueeze(2).broadcast_to([128, 8, 64]), op=ALU.mult),
                        r=[("Sfg", g), ("decS", t)], w=[("Sfg", g)])
                    S.op("dve", _mk("tensor_tensor", out=Sf[0][:, gc], in0=Sf[0][:, gc], in1=psb[bst][:, :], op=ALU.add),
                         r=[("Sfg", g), PK(bst)], w=[("Sfg", g)])

            conv_of(0)
            for k in range(len(blocks)):
                n0_ = len(S.ops)
                if k + 1 < len(blocks):
                    conv_of(k + 1)
                opsC = S.ops[n0_:]; del S.ops[n0_:]
                post_of(k)
                opsP = S.ops[n0_:]; del S.ops[n0_:]
                S.ops.extend(opsC + opsP)
            return
        xcT_alt = dec[:].rearrange("p a b -> p (a b)").rearrange("p (s c) -> p s c", s=4)
        mblocks = [("B", None), ("C", None)] + [("x", g) for g in range(4)]

        def mconv_of(k):
            kind_, g = mblocks[k]
            alt = (k % 2 == 1)
            if kind_ == "B":
                wv, wk = wget(wsrc("w_in", 0, D, C_XBC + 2048, 512), 8, 512)
                conv_block([16 + gg for gg in range(4)], wv, wk, [(BT[:, gg, 0:TT], "BT") for gg in range(4)], alt=alt)
            elif kind_ == "C":
                wv, wk = wget(wsrc("w_in", 0, D, C_XBC + 2560, 512), 8, 512)
                conv_block([20 + gg for gg in range(4)], wv, wk, [(CT[:, gg, 0:TT], "CT") for gg in range(4)], alt=alt)
            else:
                wv, wk = wget(wsrc("w_in", 0, D, C_XBC + g * 512, 512), 8, 512)
                xo, xk = (xcT_alt, "dec") if alt else (xcT, "xcT")
                conv_block([g * 4 + sub for sub in range(4)], wv, wk, [(xo[:, sub, 0:TT], xk) for sub in range(4)], alt=alt)
            wdone()

        def mpost_of(k):
            kind_, g = mblocks[k]
            alt = (k % 2 == 1)
            if kind_ == "B":
                for t in range(NT):
                    b = bank()
                    pb = psb[b][:].bitcast(BF16)
                    tr_group([(pb[:, gg * 128:(gg + 1) * 128], BT[:, gg, t * 128:(t + 1) * 128], cstb[:, K_ID, :]) for gg in range(4)], r=["BT", "cstb"], w=[PK(b)])
                    S.op("act", _mk("activation", out=Btok[:, t, :], in_=pb[:, 0:512], func=AF.Copy), r=[PK(b)], w=[("Btok", t)])
            elif kind_ == "x":
                xo, xk = (xcT_alt, "dec") if alt else (xcT, "xcT")
                for t in range(NT):
                    b = bank()
                    tr_group([(psb[b][:, sub * 128:(sub + 1) * 128], xo[:, sub, t * 128:(t + 1) * 128], cst[:, K_ID, :]) for sub in range(4)],
                             r=[xk, "cst"], w=[PK(b)])
                    S.op("act", _mk("activation", out=xs_[:, t, g * 512:(g + 1) * 512], in_=psb[b][:, :], func=AF.Copy), r=[PK(b)], w=[("xs", t), "hidT"])

        mconv_of(0)
        for k in range(len(mblocks)):
            if k + 1 < len(mblocks):
                mconv_of(k + 1)
            mpost_of(k)
        def merge_emit(a, b):
            out_ = []; ia = ib = 0
            while ia < len(a) or ib < len(b):
                if ib >= len(b) or (ia < len(a) and ia * len(b) <= ib * len(a)):
                    out_.append(a[ia]); ia += 1
                else:
                    out_.append(b[ib]); ib += 1
            S.ops.extend(out_)

        pend = None
        for g in range(4):
            zv, zk = wget(wsrc("w_in", 0, D, C_Z + g * 512, 512), 8, 512)
            for t in range(NT):
                n0_ = len(S.ops)
                dk = ("dts", t)
                tc_ = slice(t * 128, (t + 1) * 128)
                gc = slice(g * 512, (g + 1) * 512)
                xs3 = xs_[:, t, g * 512:(g + 1) * 512].rearrange("p (h c) -> p h c", h=8)
                if not pre:
                  S.op("dve", _mk("tensor_tensor", out=xdt[:].rearrange("p (h c) -> p h c", h=8), in0=xs3,
                     in1=dts[:, t, 0, g * 8:(g + 1) * 8].unsqueeze(2).broadcast_to([128, 8, 64]), op=ALU.mult), r=[("xs", t), dk], w=["xdt"])
                S.op("dve", _mk("tensor_tensor", out=xw[:].rearrange("p (h c) -> p h c", h=8), in0=xs3,
                     in1=dts[:, t, 5, g * 8:(g + 1) * 8].unsqueeze(2).broadcast_to([128, 8, 64]), op=ALU.mult), r=[("xs", t), dk], w=["xw"])
                if first_prompt and not pre and g == 0 and t == 0:
                    dbg("pxs0", xs_[:, 0, 0:512], [("xs", 0)], None)
                    dbg("pBT", BT[:, 0, 0:128], ["BT"], None); dbg("pCT", CT[:, 0, 0:128], ["CT"], None)
                    dbg("pSb", Sb[0][:, 0:512], [("Sb", 0)], None)
                    dbg("phT", hT[:, :, 252:260], ["hT"], None)
                if kind == "s" and g == 0:
                    dbg("xs0", xs_[:, 0, 0:512], [("xs", 0)], [128, 512])
                    dbg("xdt", xdt[:], ["xdt"], [128, 512])
                    dbg("xw", xw[:], ["xw"], [128, 512])
                    dbg("BT", BT[:, 0, 0:128], ["BT"], [128, 128])
                    dbg("CT", CT[:, 0, 0:128], ["CT"], [128, 128])
                    dbg("Btok", Btok[:, 0, 0:128], [("Btok", 0)], [128, 128])
                if not pre:
                    bcb = bank()
                    mm_group(psb[bcb][:, 0:128], [(BT[:, g, tc_], CT[:, g, tc_])], r=["BT", "CT"], w=[PK(bcb)])
                    S.op("dve", _mk("tensor_tensor", out=cbm[:], in0=psb[bcb][:, 0:128], in1=cst[:, mc, :], op=ALU.mult), r=[PK(bcb), "cst"], w=["cbm"])
                    S.op("pool", _mk("tensor_tensor", out=Rhi[:], in0=cstb[:, mc, :].unsqueeze(1).broadcast_to([128, 8, 128]),
                         in1=dhi[:, t, g * 8:(g + 1) * 8].unsqueeze(2).broadcast_to([128, 8, 128]), op=ALU.mult), r=["cstb", ("dhi", t)], w=["Rhi"])
                    S.op("pool", _mk("tensor_tensor", out=Rlo[:], in0=cstb[:, mc, :].unsqueeze(1).broadcast_to([128, 8, 128]),
                         in1=dlo[:, t, g * 8:(g + 1) * 8].unsqueeze(2).broadcast_to([128, 8, 128]), op=ALU.mult), r=["cstb", ("dlo", t)], w=["Rlo"])
                    bs0 = bank(); bs1 = bank()
                    for hh, bb in ((0, bs0), (1, bs1)):
                        mm_group(psb[bb][:, :], [(cstb[:, lm, :], Rhi[:, hh * 4:hh * 4 + 4, :].rearrange("p h c -> p (h c)")),
                                                (cstb[:, lm, :], Rlo[:, hh * 4:hh * 4 + 4, :].rearrange("p h c -> p (h c)"))],
                                 r=["cstb", "Rhi", "Rlo"], w=[PK(bb)])
                        S.op("act", _mk("activation", out=dec[:, hh * 4:hh * 4 + 4, :].rearrange("p h c -> p (h c)"), in_=psb[bb][:, :], func=AF.Exp),
                             r=[PK(bb)], w=["dec"])
                    S.op("dve", _mk("tensor_tensor", out=Mb[:], in0=dec[:], in1=cbm[:].unsqueeze(1).broadcast_to([128, 8, 128]), op=ALU.mult),
                         r=["dec", "cbm"], w=["Mb"])
                    if kind == "s" and g == 0:
                        dbg("cbm", cbm[:], ["cbm"], [128, 128])
                        dbg("dec", dec[:].rearrange("p a b -> p (a b)"), ["dec"], [128, 1024])
                        dbg("Mb", Mb[:].rearrange("p a b -> p (a b)"), ["Mb"], [128, 1024])
                    byd = bank()
                    def fn(e, byd=byd):
                        ins = None
                        for h in range(8):
                            ins = e.matmul(psb[byd][:, h * 64:(h + 1) * 64], lhsT=Mb[:, h, :], rhs=xdt[:, h * 64:(h + 1) * 64], start=True, stop=True)
                        return ins
                    S.op("pe", fn, r=["Mb", "xdt"], w=[PK(byd)])
                    byo = bank()
                    gc = slice(g * 512, (g + 1) * 512)
                    if kind == "p":
                        mm_group(psb[byo][:, :], [(CT[:, g, tc_], Sb[0][:, gc])], r=["CT", ("Sb", 0)], w=[PK(byo)])
                    else:
                        for si in range(2):
                            S.op("dve", _mk("tensor_tensor", out=CTm[:, si, g, :], in0=CT[:, g, tc_], in1=cstb[:, K_CM0 + si, :], op=ALU.mult),
                                 r=["CT", "cstb"], w=["CTm"])
                        mm_group(psb[byo][:, :], [(CTm[:, 0, g, :], Sb[0][:, gc]), (CTm[:, 1, g, :], Sb[1][:, gc])], r=["CTm", ("Sb", 0), ("Sb", 1)], w=[PK(byo)])
                for si, sgm in enumerate(segs):
                    bst = bank()
                    sidx = sgm["st"]
                    if kind == "p":
                        lhs = Btok[:, t, g * 128:(g + 1) * 128]
                        rk = [("Btok", t)]
                    else:
                        S.op("dve", _mk("tensor_scalar", out=Bm[:, si, g * 128:(g + 1) * 128], in0=Btok[:, t, g * 128:(g + 1) * 128],
                             scalar1=cst[:, K_RM, si:si + 1], scalar2=None, op0=ALU.mult), r=[("Btok", t), "cst"], w=["Bm"])
                        lhs = Bm[:, si, g * 128:(g + 1) * 128]
                        rk = ["Bm"]
                    mm_group(psb[bst][:, :], [(lhs, xw[:, :])], r=rk + ["xw"], w=[PK(bst)])
                    S.op("dve", _mk("tensor_tensor",
                        out=Sf[sidx][:, gc].rearrange("p (h c) -> p h c", h=8), in0=Sf[sidx][:, gc].rearrange("p (h c) -> p h c", h=8),
                        in1=decS[:, t, 32 * si + g * 8:32 * si + g * 8 + 8].unsqueeze(2).broadcast_to([128, 8, 64]), op=ALU.mult),
                        r=[("Sf", sidx), ("decS", t)], w=[("Sf", sidx)])
                    S.op("dve", _mk("tensor_tensor", out=Sf[sidx][:, gc], in0=Sf[sidx][:, gc], in1=psb[bst][:, :], op=ALU.add),
                         r=[("Sf", sidx), PK(bst)], w=[("Sf", sidx)])
                    if not pre:
                        S.op("act", _mk("activation", out=Sb[sidx][:, gc], in_=Sf[sidx][:, gc], func=AF.Copy), r=[("Sf", sidx)], w=[("Sb", sidx)])
                opsA = S.ops[n0_:]
                del S.ops[n0_:]
                if pend is not None:
                    merge_emit(opsA, pend["ops"])
                    if pend["last"]:
                        wdone()
                else:
                    S.ops.extend(opsA)
                n1_ = len(S.ops)
                if not pre:
                    S.op("dve", _mk("tensor_tensor", out=t1[:].rearrange("p (h c) -> p h c", h=8), in0=psb[byo][:, :].rearrange("p (h c) -> p h c", h=8),
                         in1=dts[:, t, 3, g * 8:(g + 1) * 8].unsqueeze(2).broadcast_to([128, 8, 64]), op=ALU.mult), r=[PK(byo), dk], w=["t1"])
                    S.op("dve", _mk("tensor_tensor", out=t1[:], in0=t1[:], in1=psb[byd][:, :], op=ALU.add), r=["t1", PK(byd)], w=["t1"])
                    S.op("dve", _mk("tensor_tensor", out=xs3, in0=xs3, in1=dsk[:, g * 8:(g + 1) * 8].unsqueeze(2).broadcast_to([128, 8, 64]), op=ALU.mult),
                         r=[("xs", t), "dsk"], w=[("xs", t)])
                    S.op("dve", _mk("tensor_tensor", out=t1[:], in0=t1[:], in1=xs_[:, t, gc], op=ALU.add), r=["t1", ("xs", t)], w=["t1"])
                    if first_prompt and g == 0 and t == 0:
                        dbg("py", t1[:], ["t1"], None)
                    if kind == "s" and g == 0:
                        dbg("y", t1[:], ["t1"], [128, 512])
                        dbg("Sf0", Sf[0][:, 0:512], [("Sf", 0)], [128, 512])
                    bz = bank()
                    mm_group(psb[bz][:, :], [(hT[:, kc, tc_], zv[:, kc, :]) for kc in range(8)], r=["hT", zk], w=[PK(bz)])
                    S.op("act", _mk("activation", out=zs[:], in_=psb[bz][:, :], func=AF.Silu), r=[PK(bz)], w=["zs"])
                    S.op("dve", _mk("tensor_tensor", out=t1[:], in0=t1[:], in1=zs[:], op=ALU.mult), r=["t1", "zs"], w=["t1"])
                    S.op("act", _mk("activation", out=zs[:], in_=t1[:], func=AF.Square, accum_out=sc[:, 1:2]), r=["t1"], w=["zs", "sc1"])
                    rstd_from(sc[:, 1:2], 512.0, "sc1")
                    S.op("dve", _mk("tensor_scalar", out=gnb[:], in0=t1[:], scalar1=sc[:, 1:2], scalar2=None, op0=ALU.mult),
                         r=["t1", "sc1"], w=["gnb"])
                    bt = bank()
                    pb = psb[bt][:].bitcast(BF16)
                    tr_group([(pb[:, sub * 128:(sub + 1) * 128], gnb[:, sub * 128:(sub + 1) * 128], cstb[:, K_ID, :]) for sub in range(4)], r=["gnb", "cstb"], w=[PK(bt)])
                    S.op("dve", _mk("tensor_tensor", out=obT[:, g * 4:g * 4 + 4, t * 128:(t + 1) * 128], in0=pb[:, 0:512].rearrange("p (s c) -> p s c", s=4),
                         in1=nwc[:, g * 4:g * 4 + 4].unsqueeze(2).broadcast_to([128, 4, 128]), op=ALU.mult), r=[PK(bt), "nwc"], w=["obT"])
                opsB = S.ops[n1_:]
                del S.ops[n1_:]
                pend = dict(ops=opsB, last=(t == NT - 1))
        S.ops.extend(pend["ops"])
        wdone()
        if kind == "s":
            dbg("obT", obT[:, :, 0:128], ["obT"], [128, 2048])
            dbg("PA", PA[:, :, 0:128], ["PA"], [128, 1024])
        if not pre:
            gbs = []
            for q in range(4):
                wv, wk = wget(wsrc("wb", 0, 2048, q * 256, 256), 16, 256)
                if q % 2 == 0:
                    gv, gk = wget(wsrc("w_in", 0, D, C_GB + (q // 2) * 512, 512), 8, 512)
                for sub in range(2):
                    fb = q * 2 + sub
                    gsub = (q % 2) * 2 + sub
                    b1 = bank(); b2 = bank()
                    mm_group(psb[b1][:, 0:TT], [(gv[:, kc, gsub * 128:(gsub + 1) * 128], hT[:, kc, 0:TT]) for kc in range(8)], r=[gk, "hT"], w=[PK(b1)])
                    mm_group(psb[b2][:, 0:TT], [(wv[:, kc, sub * 128:(sub + 1) * 128], obT[:, kc, 0:TT]) for kc in range(16)], r=[wk, "obT"], w=[PK(b2)])
                    S.op("act", _mk("activation", out=sg[:, 0:TT], in_=psb[b1][:, 0:TT], func=AF.Sigmoid), r=[PK(b1)], w=["sg"])
                    S.op("dve", _mk("tensor_tensor", out=mtmp[:, 0:TT], in0=sg[:, 0:TT], in1=psb[b2][:, 0:TT], op=ALU.mult), r=["sg", PK(b2)], w=["mtmp"])
                    S.op("dve", _mk("tensor_tensor", out=mT[:, fb, 0:TT], in0=mtmp[:, 0:TT], in1=PA[:, fb, 0:TT], op=ALU.add), r=["mtmp", "PA"], w=["uT"])
                if q % 2 == 0:
                    wdone()
                    pending_gb = True
                else:
                    wdone(); wdone()
            if kind == "s":
                dbg("mT", mT[:, :, 0:128], ["uT"], [128, 1024])
            wo_blk = [wget(wsrc("wo", 0, D, blk * 512, 512), 8, 512) for blk in range(2)]
            recs = []
            for t in range(NT):
                n0_ = len(S.ops)
                p_ = t % 2
                jk, jkey = ((vn32, "vn32"), (vg, "vg"))[p_]
                c0_ = (2, 6)[p_]
                sk2 = ("sc2", p_)
                bo = [bank(), bank()]
                for blk in range(2):
                    wv, wk = wo_blk[blk]
                    mm_group(psb[bo[blk]][:, :], [(mT[:, kc, t * 128:(t + 1) * 128], wv[:, kc, :]) for kc in range(8)], r=["uT", wk], w=[PK(bo[blk])])
                    S.op("act", _mk("activation", out=jk[:, blk * 512:(blk + 1) * 512], in_=psb[bo[blk]][:, :], func=AF.Square, accum_out=sc[:, c0_ + blk:c0_ + 1 + blk]),
                         r=[PK(bo[blk])], w=[jkey, sk2])
                S.op("dve", _mk("tensor_tensor", out=sc[:, c0_:c0_ + 1], in0=sc[:, c0_:c0_ + 1], in1=sc[:, c0_ + 1:c0_ + 2], op=ALU.add), r=[sk2], w=[sk2])
                rstd_from(sc[:, c0_:c0_ + 1], D, sk2)
                for blk in range(2):
                    cs = slice(blk * 512, (blk + 1) * 512)
                    S.op("dve", _mk("scalar_tensor_tensor", out=jk[:, cs], in0=psb[bo[blk]][:, :], scalar=sc[:, c0_:c0_ + 1], in1=bc["post_mix_w"][:, cs], op0=ALU.mult, op1=ALU.mult),
                         r=[PK(bo[blk]), sk2, "bc_post_mix_w"], w=[jkey])
                S.op("dve", _mk("tensor_tensor", out=xres[:, t, :], in0=xres[:, t, :], in1=jk[:], op=ALU.add), r=[("xres", t), jkey], w=[("xres", t)])
                norm_to_hT(xres[:, t, :], ("xres", t), "pre_ffn_w", t, par=p_)
                recs.append(S.ops[n0_:]); del S.ops[n0_:]
            S.ops.extend(_merge(recs[0], recs[1]) if NT == 2 else recs[0])
            wdone(); wdone()
            if kind == "s":
                dbg("x1", xres[:, 0, :], [("xres", 0)], [128, 1024])
            for blk in range(8):
                wv, wk = wget(wsrc("wup", 0, D, blk * 512, 512), 8, 512)
                for sub in range(4):
                    hb = blk * 4 + sub
                    b = bank()
                    mm_group(psb[b][:, 0:TT], [(wv[:, kc, sub * 128:(sub + 1) * 128], hT[:, kc, 0:TT]) for kc in range(8)], r=[wk, "hT"], w=[PK(b)])
                    rl = rls[hb % 3]
                    S.op("act", _mk("activation", out=rl[:, 0:TT], in_=psb[b][:, 0:TT], func=AF.Relu), r=[PK(b)], w=[("rl", hb % 3)])
                    S.op("pool", _mk("tensor_tensor", out=hidT[:, hb, 0:TT], in0=rl[:, 0:TT], in1=rl[:, 0:TT], op=ALU.mult), r=[("rl", hb % 3)], w=["hidT"] + [("xs", tt_) for tt_ in range(NT)])
                wdone()
            bdn = [[bank(), bank()] for _ in range(NT)]
            for fb in range(8):
                wv, wk = wget(wsrc("wdn", 0, 4096, fb * 128, 128), 32, 128)
                b = bank()
                while any(b in pr for pr in bdn):
                    b = bank()
                mm_group(psb[b][:, 0:TT], [(wv[:, kc, :], hidT[:, kc, 0:TT]) for kc in range(32)], r=[wk, "hidT"], w=[PK(b)])
                S.op("act", _mk("activation", out=fT[:, 0:TT], in_=psb[b][:, 0:TT], func=AF.Copy), r=[PK(b)], w=["fT"])
                for t in range(NT):
                    bb = bdn[t][fb // 4]
                    c0 = (fb % 4) * 128
                    tr_group([(psb[bb][:, c0:c0 + 128], fT[:, t * 128:(t + 1) * 128], cst[:, K_ID, :])], r=["fT", "cst"], w=[PK(bb)])
                wdone()
            recs = []
            for t in range(NT):
                n0_ = len(S.ops)
                p_ = t % 2
                yo_, ykey = ((vg, "vg"), (vn32, "vn32"))[p_]
                c0_ = (4, 8)[p_]
                sk4 = ("sc4", p_)
                for blk in range(2):
                    bb = bdn[t][blk]
                    S.op("act", _mk("activation", out=yo_[:, blk * 512:(blk + 1) * 512], in_=psb[bb][:, :], func=AF.Square, accum_out=sc[:, c0_ + blk:c0_ + 1 + blk]),
                         r=[PK(bb)], w=[ykey, sk4])
                S.op("dve", _mk("tensor_tensor", out=sc[:, c0_:c0_ + 1], in0=sc[:, c0_:c0_ + 1], in1=sc[:, c0_ + 1:c0_ + 2], op=ALU.add), r=[sk4], w=[sk4])
                rstd_from(sc[:, c0_:c0_ + 1], D, sk4)
                for blk in range(2):
                    bb = bdn[t][blk]
                    cs = slice(blk * 512, (blk + 1) * 512)
                    S.op("dve", _mk("scalar_tensor_tensor", out=yo_[:, cs], in0=psb[bb][:, :], scalar=sc[:, c0_:c0_ + 1], in1=bc["post_ffn_w"][:, cs], op0=ALU.mult, op1=ALU.mult),
                         r=[PK(bb), sk4, "bc_post_ffn_w"], w=[ykey])
                S.op("dve", _mk("tensor_tensor", out=yo_[:], in0=yo_[:], in1=xres[:, t, :], op=ALU.add), r=[ykey, ("xres", t)], w=[ykey])
                S.op("sp", _mk("dma_start", out=y_dst[t * 128:(t + 1) * 128, :], in_=yo_[:]), r=[ykey], dma="o%d" % p_)
                recs.append(S.ops[n0_:]); del S.ops[n0_:]
            S.ops.extend(_merge(recs[0], recs[1]) if NT == 2 else recs[0])

    supertile("s", 1, dr["xsm"], dr["ys"], False)
    for s_ in range(2):
        S.op("sp", _mk("dma_start", out=dr["ssms"][s_, :, :], in_=Sf[s_][:]), r=[("Sf", s_), ("xs", s_)], dma="o")
    S.op("sp", _mk("dma_start", out=dr["convs"][:, :], in_=cso[:].rearrange("p a b c -> p (a b c)")), r=["cso"], dma="o")
    S.op("sp", _mk("dma_start", out=bsb1[:].rearrange("p g c -> p (g c)"), in_=dr["bs_p"].partition_broadcast(128)), w=["bsb"], dma="c25")
    S.op("dve", _mk("memset", Sf[0][:], 0.0), w=[("Sf", 0)] + [("Sfg", g) for g in range(4)])
    S.op("dve", _mk("memset", histb[:], 0.0), w=[("histb", ct) for ct in range(24)])
    S.op("sp", _mk("dma_start", out=pflagb[:], in_=dr["pflag"].partition_broadcast(128)), w=["pflagb"], dma="c26")
    ws["ph"] = []
    prew = {}
    res_bufs = [(slots[k][:, :], ("wslot", k)) for k in range(4)] + [(obT[:].rearrange("p a b -> p (a b)"), "obT")]
    res_blocks = [wsrc("w_in", 0, D, C_XBC + g * 512, 512) for g in range(4)] + [wsrc("w_in", 0, D, C_XBC + 2048, 512)]
    for k, srcd in enumerate(res_blocks):
        buf, key = res_bufs[k]
        view = buf.rearrange("p (k c) -> p k c", k=8)
        S.op("pool", _mk("dma_start", out=view, in_=srcd["ap"]), w=[key], dma="pw%d" % k)
        prew[srcd["bid"]] = (view, key)
        if srcd["bid"] not in scr:
            scr[srcd["bid"]] = len(scr)
            S.op("sp", _mk("dma_start", out=wscr.ap()[scr[srcd["bid"]], :, :], in_=buf), r=[key], w=[("scr", scr[srcd["bid"]])], dma="psv%d" % k)
    for su in range(npre):
        r0 = su * 128 * NTP
        supertile("p", NTP, dr["xprev"][r0:r0 + 128 * NTP, :], None, False, pre=True, flagidx=su)
    ws["ph"] = [S.placeholder() for _ in range(NSLOT)]
    S.op("act", _mk("activation", out=Sb[0][:], in_=Sf[0][:], func=AF.Copy), r=[("Sf", 0)] + [("Sfg", g) for g in range(4)], w=[("Sb", 0), ("Sf", 0)])
    dbg("hinit", Sf[0][:], [("Sf", 0)], None)
    for su in range(nsup):
        r0 = su * 128 * NTP
        supertile("p", NTP, dr["xp"][r0:r0 + 128 * NTP, :], dr["yp"][r0:r0 + 128 * NTP, :], su == 0)
    S.op("sp", _mk("dma_start", out=dr["ssmp"][:, :], in_=Sf[0][:]), r=[("Sf", 0)], dma="o")
    S.op("sp", _mk("dma_start", out=dr["convp"][:, :], in_=histb[:].rearrange("p a b -> p (a b)")), r=[("histb", ct) for ct in range(24)], dma="o")
    nops = S.emit(nc, es)
    es.close()
    return nc, nops, dbg_names


def _consts():
    c = np.zeros((NCST, 128, 128), np.float32)
    idx = np.arange(128)
    k = idx[:, None]; i = idx[None, :]
    same = (k // 64) == (i // 64)
    c[K_ID] = np.eye(128)
    c[K_MC_P] = (k <= i)
    c[K_MC_S] = (k <= i) & same
    c[K_LM_P] = (k > i)
    c[K_LM_S] = (k > i) & same
    c[K_ONES] = 1.0
    c[K_SS0] = (k < 64) * np.ones((1, 128))
    c[K_SS1] = (k >= 64) * np.ones((1, 128))
    c[K_GM_P] = (k // 64) <= (i // 64)
    c[K_GM_S] = same
    c[K_CM0] = np.ones((128, 1)) * (i < 64)
    c[K_CM1] = np.ones((128, 1)) * (i >= 64)
    c[K_RM][:, 0] = (idx < 64)
    c[K_RM][:, 1] = (idx >= 64)
    return np.ascontiguousarray(c.transpose(1, 0, 2).reshape(128, NCST * 128))


_CACHE = {}


def kernel(**inp):
    f = lambda a: np.ascontiguousarray(np.asarray(a, dtype=np.float32))
    xpr = f(inp["x_prompt"]); xsm = f(inp["x_sample"])
    cache = f(inp["cache_conv"])[0]; state = f(inp["state_ssm"])[0]
    if "nc" not in _CACHE:
        _CACHE["nc"], _, _ = build_program()
    nc = _CACHE["nc"]
    shared = {
        "w_in": f(inp["w_in"])[0], "wa": f(inp["w_branch_a"])[0], "wb": f(inp["w_branch_b"])[0],
        "wo": f(inp["w_out"])[0], "wup": f(inp["w_up"])[0], "wdn": f(inp["w_down"])[0],
        "pre_mix_w": f(inp["pre_mix_w"]), "ln_w": f(inp["gmlp_ln_w"]), "ln_b": f(inp["gmlp_ln_b"]),
        "post_mix_w": f(inp["post_mix_w"]), "pre_ffn_w": f(inp["pre_ffn_w"]), "post_ffn_w": f(inp["post_ffn_w"]),
        "nw_l": np.ascontiguousarray(f(inp["ssm_norm_w"]).reshape(16, 128).T), "a_log": f(inp["a_log"]), "dt_bias": f(inp["dt_bias"]), "d_skip": f(inp["d_skip"]),
        "cst": _consts(),
    }
    ws = f(inp["gmlp_ws"])[0]
    bs = f(inp["gmlp_bs"])[0]
    shared["wsT_p"] = np.ascontiguousarray(ws.transpose(2, 0, 1).reshape(128, 1024))
    ws_s = np.tile(ws[:, :64, :64], (1, 2, 2))
    shared["wsT_s"] = np.ascontiguousarray(ws_s.transpose(2, 0, 1).reshape(128, 1024))
    shared["bs_p"] = np.ascontiguousarray(bs.reshape(1, 1024))
    shared["bs_s"] = np.ascontiguousarray(np.tile(bs[:, :64], (1, 2)).reshape(1, 1024))
    cw = f(inp["conv_w"])[0]
    shared["convw_l"] = np.ascontiguousarray(cw.reshape(4, 24, 128).transpose(2, 1, 0).reshape(128, 96))
    shared["convb_l"] = np.ascontiguousarray(f(inp["conv_b"])[0].reshape(24, 128).T)
    in_maps = []
    for c in range(NCORE):
        b, q = c // 4, c % 4
        m = dict(shared)
        m["xp"] = np.ascontiguousarray(xpr[b, q * PT:(q + 1) * PT])
        xh = np.zeros((4, D), np.float32)
        if q > 0:
            xh[0:3] = xpr[b, q * PT - 3:q * PT]
        m["xh"] = xh
        m["xsm"] = np.ascontiguousarray(xsm[2 * c:2 * c + 2].reshape(128, D))
        cc = cache[2 * c:2 * c + 2]
        m["cachel"] = np.ascontiguousarray(cc.reshape(2, 3, 24, 128).transpose(3, 0, 2, 1).reshape(128, 144))
        st = state[2 * c:2 * c + 2]
        m["stateT"] = np.ascontiguousarray(st.reshape(2, 2048, 128).transpose(0, 2, 1))
        xprev = np.zeros((3 * PT, D), np.float32)
        if q > 0:
            xprev[(3 - q) * PT:] = xpr[b, 0:q * PT]
        m["xprev"] = xprev
        pf = np.zeros((1, 3 * NSUP), np.float32)
        pf[0, (3 - q) * NSUP:] = 1.0
        m["pflag"] = pf
        in_maps.append(m)
    res = run_bass_kernel_spmd(nc, in_maps, core_ids=list(range(NCORE)))
    R = res.results
    yp = np.stack([np.concatenate([R[b * 4 + q]["yp"] for q in range(4)], axis=0) for b in range(2)])
    ys = np.concatenate([R[c]["ys"].reshape(2, 64, D) for c in range(NCORE)], axis=0)
    def conv_back(a, n):
        return a.reshape(128, n, 24, 3).transpose(1, 3, 2, 0).reshape(n, 3, 3072)
    def ssm_back(a):
        return a.T.reshape(32, 64, 128)
    convp = np.stack([conv_back(R[b * 4 + 3]["convp"], 1)[0] for b in range(2)])[None]
    ssmp = np.stack([ssm_back(R[b * 4 + 3]["ssmp"]) for b in range(2)])[None]
    convs = np.concatenate([conv_back(R[c]["convs"], 2) for c in range(NCORE)], axis=0)[None]
    ssms = np.stack([ssm_back(R[c]["ssms"][s]) for c in range(NCORE) for s in range(2)])[None]
    vs = np.concatenate([R[c]["vs"].reshape(2, 64, D) for c in range(NCORE)], axis=0)[None]
    out = (yp, ys, convp, ssmp, convs, ssms, vs)
    return tuple(np.ascontiguousarray(o, dtype=np.float32) for o in out)
```
